# Optimizing a Trainium2 kernel written in Bass

```python
import math
import jax
import jax.numpy as jnp
from jax import lax
import numpy as np

D_MODEL = 1024
BATCH = 8
SEQ = 4096
DEPTH = 2

CTX_LEN = 256
GRID_W = 64
ROPE_BASE = 10000.0
F32 = jnp.float32
NEG_INF = -1e30
LN_EPS = 1e-6

HY_W = 256
HY_ORDER = 2
HY_DIRS = 2
HY_EMB = 33
HY_FFN = 64
HY_MIN_DECAY = math.log(1e-2) / 1.5
HY_MAX_DECAY = math.log(1e-2) / 0.3

SWA_HEADS = 4
SWA_KV = 2
SWA_HD = 64
SWA_WIN = 128
SWA_BLOCK = 128

RW_HEADS = 4
RW_HD = 64
RW_W = RW_HEADS * RW_HD
RW_DECAY_R = 64
RW_AAA_R = 64
RW_GATE_R = 128
RW_GN_EPS = 64e-5

DF_HEADS = 4
DF_HD = 32
DF_VD = 2 * DF_HD
DF_W = DF_HEADS * DF_VD
DF_BLOCK = 128

N_BRANCH = 4
BR_W = 256
D_FF = 2816
DN_ALPHA = (2 * DEPTH) ** 0.25
DN_BETA = (8 * DEPTH) ** -0.25

SWA_SIZES = (SWA_HEADS * SWA_HD, SWA_KV * SWA_HD, SWA_KV * SWA_HD)
RW_SIZES = (RW_W, RW_W, RW_W, RW_DECAY_R, RW_DECAY_R, RW_AAA_R, RW_GATE_R, RW_GATE_R)
DF_SIZES = (DF_W, DF_W, DF_W)
IN_SIZES = (3 * HY_W, sum(SWA_SIZES), sum(RW_SIZES), sum(DF_SIZES), N_BRANCH * D_MODEL)
IN_COLS = sum(IN_SIZES)

kernel_name = 'hybrid_hyena_swa_rwkv7_diffattn_block'


def split_at(t, sizes):
    return jnp.split(t, np.cumsum(sizes)[:-1].tolist(), axis=-1)


def layer_norm(x, g=None, b=None):
    xf = x.astype(F32)
    mu = jnp.mean(xf, -1, keepdims=True)
    var = jnp.mean(jnp.square(xf - mu), -1, keepdims=True)
    y = (xf - mu) * lax.rsqrt(var + LN_EPS)
    if g is not None:
        y = y * g + b
    return y.astype(x.dtype)


def modulate(x, shift, scale):
    return layer_norm(x) * (1 + scale) + shift


def dwconv3(x, w, b):
    xp = jnp.pad(x, ((0, 0), (1, 1), (0, 0)))
    return xp[:, :-2] * w[0] + x * w[1] + xp[:, 2:] * w[2] + b


def axial_rope(x):
    L, d = x.shape[1], x.shape[-1]
    rows = L // GRID_W
    row = jnp.repeat(jnp.arange(rows), GRID_W)
    col = jnp.tile(jnp.arange(GRID_W), rows)
    nf = d // 4
    inv = ROPE_BASE ** (-jnp.arange(nf, dtype=F32) / nf)
    bshape = (L,) + (1,) * (x.ndim - 3) + (nf,)
    xf = x.astype(F32)
    out = []
    for half, pos in enumerate((row, col)):
        ang = (pos.astype(F32)[:, None] * inv[None, :]).reshape(bshape)
        cos, sin = jnp.cos(ang), jnp.sin(ang)
        xh = xf[..., half * 2 * nf:(half + 1) * 2 * nf]
        x1, x2 = xh[..., :nf], xh[..., nf:]
        out += [x1 * cos - x2 * sin, x1 * sin + x2 * cos]
    return jnp.concatenate(out, -1).astype(x.dtype)


def sink_softmax(logit_list, sink_b):
    lead = logit_list[0].shape[:-1]
    z = jnp.concatenate([t.astype(F32) for t in logit_list]
                        + [jnp.broadcast_to(sink_b.astype(F32), lead + (1,))], -1)
    return jax.nn.softmax(z, -1)[..., :-1]


def hyena_spectrum(L, w1, b1, w2, b2, w3, freq):
    t = jnp.linspace(0.0, 1.0, L, dtype=F32)[:, None]
    bands = (HY_EMB - 1) // 2
    w = 2.0 * math.pi * jnp.arange(L, dtype=F32) / L
    fb = jnp.linspace(1e-4, bands - 1, bands, dtype=F32)
    ang = w[:, None] * fb[None, :]
    z = jnp.concatenate([t, jnp.cos(ang), -jnp.sin(ang)], -1)
    fr = freq.astype(F32)
    hdn = jnp.sin(fr * (z @ w1.astype(F32) + b1.astype(F32)))
    hdn = jnp.sin(fr * (hdn @ w2.astype(F32) + b2.astype(F32)))
    filt = (hdn @ w3.astype(F32)).reshape(L, HY_ORDER, HY_DIRS, HY_W)
    deltas = jnp.abs(jnp.linspace(HY_MIN_DECAY, HY_MAX_DECAY, HY_W, dtype=F32))
    filt = filt * jnp.exp(-t * deltas[None, :])[:, None, None, :]
    filt = filt / jnp.sum(jnp.abs(filt), axis=(0, 2), keepdims=True)
    fwd, bwd = filt[:, :, 0], filt[:, :, 1]
    kern = jnp.concatenate([fwd, jnp.zeros_like(fwd[:1]), bwd[:0:-1]], axis=0)
    return jnp.fft.rfft(kern, axis=0)


def fft_longconv(z, kf, bias):
    L = z.shape[1]
    zf = jnp.fft.rfft(z.astype(F32), n=2 * L, axis=1)
    y = jnp.fft.irfft(zf * kf[None], n=2 * L, axis=1)[:, :L]
    return (y + z * bias).astype(z.dtype)


def hyena_mix(p, conv_w, conv_b, kf, bias):
    v, x1, x2 = jnp.split(dwconv3(p, conv_w, conv_b), 3, -1)
    zz = x1 * fft_longconv(v, kf[:, 0], bias[0])
    return x2 * fft_longconv(zz, kf[:, 1], bias[1])


def hyena_branch(p, pc, need_ctx, conv_w, conv_b, fw1, fb1, fw2, fb2, fw3, ffreq, bias):
    kf = hyena_spectrum(p.shape[1], fw1, fb1, fw2, fb2, fw3, ffreq)
    y = hyena_mix(p, conv_w, conv_b, kf, bias)
    yc = None
    if need_ctx:
        kfc = hyena_spectrum(pc.shape[1], fw1, fb1, fw2, fb2, fw3, ffreq)
        yc = hyena_mix(pc, conv_w, conv_b, kfc, bias)
    return y, yc


def swa_latent(q, k, v, kc, vc, sink):
    B, L = q.shape[:2]
    nb = L // SWA_BLOCK
    G = SWA_HEADS // SWA_KV
    scale = SWA_HD ** -0.5
    qb = q.reshape(B, nb, SWA_BLOCK, SWA_KV, G, SWA_HD)

    def band(t):
        tp = jnp.pad(t, ((0, 0), (SWA_BLOCK, SWA_BLOCK), (0, 0), (0, 0)))
        tp = tp.reshape(B, nb + 2, SWA_BLOCK, SWA_KV, SWA_HD)
        return jnp.concatenate([tp[:, :-2], tp[:, 1:-1], tp[:, 2:]], axis=2)

    kb, vb = band(k), band(v)
    s_loc = jnp.einsum('bnqkgd,bnskd->bnkgqs', qb, kb).astype(F32) * scale
    s_ctx = jnp.einsum('bnqkgd,bckd->bnkgqc', qb, kc).astype(F32) * scale
    blk = jnp.arange(nb)[:, None, None]
    qpos = blk * SWA_BLOCK + jnp.arange(SWA_BLOCK)[None, :, None]
    kpos = (blk - 1) * SWA_BLOCK + jnp.arange(3 * SWA_BLOCK)[None, None, :]
    valid = (jnp.abs(kpos - qpos) <= SWA_WIN) & (kpos >= 0) & (kpos < L)
    s_loc = jnp.where(valid[None, :, None, None], s_loc, NEG_INF)
    p = sink_softmax([s_loc, s_ctx], sink.reshape(SWA_KV, G)[:, :, None, None])
    nl = 3 * SWA_BLOCK
    o = (jnp.einsum('bnkgqs,bnskd->bnqkgd', p[..., :nl].astype(v.dtype), vb)
         + jnp.einsum('bnkgqc,bckd->bnqkgd', p[..., nl:].astype(v.dtype), vc))
    return o.reshape(B, L, SWA_HEADS * SWA_HD)


def swa_context(qc, kc, vc, sink):
    B, C = qc.shape[:2]
    G = SWA_HEADS // SWA_KV
    qg = qc.reshape(B, C, SWA_KV, G, SWA_HD)
    s = jnp.einsum('bqkgd,bckd->bkgqc', qg, kc).astype(F32) * SWA_HD ** -0.5
    p = sink_softmax([s], sink.reshape(SWA_KV, G)[:, :, None, None])
    o = jnp.einsum('bkgqc,bckd->bqkgd', p.astype(vc.dtype), vc)
    return o.reshape(B, C, SWA_HEADS * SWA_HD)


def swa_branch(p, pc, need_ctx, sink):
    B, L = p.shape[:2]
    C = pc.shape[1]
    q, k, v = split_at(p, SWA_SIZES)
    qc, kc, vc = split_at(pc, SWA_SIZES)
    q = axial_rope(q.reshape(B, L, SWA_HEADS, SWA_HD))
    k = axial_rope(k.reshape(B, L, SWA_KV, SWA_HD))
    v = v.reshape(B, L, SWA_KV, SWA_HD)
    kc = kc.reshape(B, C, SWA_KV, SWA_HD)
    vc = vc.reshape(B, C, SWA_KV, SWA_HD)
    y = swa_latent(q, k, v, kc, vc, sink)
    yc = swa_context(qc.reshape(B, C, SWA_HEADS, SWA_HD), kc, vc, sink) if need_ctx else None
    return y, yc


def rwkv_streams(p, mu, w0, w2, a0, a2, g2, k_k, k_a):
    B, L = p.shape[:2]
    pp = jnp.pad(p, ((0, 0), (1, 1), (0, 0)))
    p = p + (0.5 * (pp[:, :-2] + pp[:, 2:]) - p) * mu
    r, k, v, wd_f, wd_b, ad, gd_f, gd_b = split_at(p, RW_SIZES)

    def heads(t):
        return t.reshape(B, L, RW_HEADS, RW_HD)

    def decay(wd, w0d, w2d):
        wlog = -jax.nn.softplus(-(w0d + jnp.tanh(wd) @ w2d)) - 0.5
        return jnp.exp(-jnp.exp(wlog.astype(F32)))

    a = jax.nn.sigmoid(a0 + ad @ a2)
    kk = heads(k * k_k).astype(F32)
    kk = kk / jnp.maximum(jnp.sqrt(jnp.sum(kk * kk, -1, keepdims=True)), 1e-12)
    k = k * (1 + (a - 1) * k_a)
    g_f = jax.nn.sigmoid(gd_f) @ g2[0]
    g_b = jax.nn.sigmoid(gd_b) @ g2[1]
    return (heads(r), heads(k), heads(v), kk, heads(a),
            heads(decay(wd_f, w0[0], w2[0])), heads(decay(wd_b, w0[1], w2[1])),
            heads(g_f), heads(g_b))


def rwkv_scan(state0, r, dec, k, v, kk, a, reverse, emit):
    xs = tuple(jnp.moveaxis(t.astype(F32), 1, 0) for t in (r, dec, k, v, kk, a))

    def step(S, inp):
        r_t, w_t, k_t, v_t, kk_t, a_t = inp
        sa = jnp.einsum('bhvk,bhk->bhv', S, -kk_t)
        S = (S * w_t[:, :, None, :] + sa[..., None] * (kk_t * a_t)[:, :, None, :]
             + v_t[..., None] * k_t[:, :, None, :])
        return S, (jnp.einsum('bhvk,bhk->bhv', S, r_t) if emit else None)

    S, out = lax.scan(step, state0, xs, reverse=reverse)
    return S, (jnp.moveaxis(out, 0, 1) if emit else None)


def rwkv_out(o_f, o_b, r, k, v, g_f, g_b, r_k, lnx_g, lnx_b):
    B, L = r.shape[:2]
    gam = lnx_g.reshape(RW_HEADS, RW_HD)
    bet = lnx_b.reshape(RW_HEADS, RW_HD)

    def gn(o):
        mu = jnp.mean(o, -1, keepdims=True)
        var = jnp.mean(jnp.square(o - mu), -1, keepdims=True)
        return (o - mu) * lax.rsqrt(var + RW_GN_EPS) * gam + bet

    bonus = jnp.sum(r * k * r_k, -1, keepdims=True) * v
    y = (gn(o_f) + bonus) * g_f + (gn(o_b) + bonus) * g_b
    return y.reshape(B, L, RW_W).astype(v.dtype)


def rwkv_branch(p, pc, need_ctx, mu, w0, w2, a0, a2, g2, k_k, k_a, r_k, lnx_g, lnx_b):
    r, k, v, kk, a, d_f, d_b, g_f, g_b = rwkv_streams(p, mu, w0, w2, a0, a2, g2, k_k, k_a)
    rc, kc, vc, kkc, ac, dc_f, dc_b, gc_f, gc_b = rwkv_streams(pc, mu, w0, w2, a0, a2, g2, k_k, k_a)
    S0 = jnp.zeros((p.shape[0], RW_HEADS, RW_HD, RW_HD), F32)
    Sf, oc_f = rwkv_scan(S0, rc, dc_f, kc, vc, kkc, ac, False, need_ctx)
    Sb, oc_b = rwkv_scan(S0, rc, dc_b, kc, vc, kkc, ac, True, need_ctx)
    _, o_f = rwkv_scan(Sf, r, d_f, k, v, kk, a, False, True)
    _, o_b = rwkv_scan(Sb, r, d_b, k, v, kk, a, True, True)
    y = rwkv_out(o_f, o_b, r, k, v, g_f, g_b, r_k, lnx_g, lnx_b)
    yc = rwkv_out(oc_f, oc_b, rc, kc, vc, gc_f, gc_b, r_k, lnx_g, lnx_b) if need_ctx else None
    return y, yc


def diff_attend(q, k, v, lam, lam_init, subln_g):
    B, Lq = q.shape[:2]
    nb = Lq // DF_BLOCK
    qb = jnp.moveaxis(q.reshape(B, nb, DF_BLOCK, DF_HEADS, 2, DF_HD), 1, 0)
    scale = DF_HD ** -0.5

    def block(qblk):
        s = jnp.einsum('bqhcd,bshcd->bhcqs', qblk, k).astype(F32) * scale
        pr = jax.nn.softmax(s, -1)
        wgt = pr[:, :, 0] - lam * pr[:, :, 1]
        return jnp.einsum('bhqs,bshe->bqhe', wgt.astype(v.dtype), v)

    o = lax.map(block, qb)
    o = jnp.moveaxis(o, 0, 1).reshape(B, Lq, DF_HEADS, DF_VD).astype(F32)
    o = o * lax.rsqrt(jnp.mean(o * o, -1, keepdims=True) + 1e-5) * subln_g * (1.0 - lam_init)
    return o.reshape(B, Lq, DF_W).astype(v.dtype)


def diff_branch(p, pc, need_ctx, lq1, lk1, lq2, lk2, subln_g, lam_init):
    q, k, v = split_at(p, DF_SIZES)
    qc, kc, vc = split_at(pc, DF_SIZES)

    def qk(t):
        return t.reshape(t.shape[0], t.shape[1], DF_HEADS, 2, DF_HD)

    def vs(t):
        return t.reshape(t.shape[0], t.shape[1], DF_HEADS, DF_VD)

    lam = (jnp.exp(jnp.sum(lq1.astype(F32) * lk1.astype(F32)))
           - jnp.exp(jnp.sum(lq2.astype(F32) * lk2.astype(F32))) + lam_init)
    k_all = jnp.concatenate([axial_rope(qk(k)), qk(kc)], axis=1)
    v_all = jnp.concatenate([vs(v), vs(vc)], axis=1)
    y = diff_attend(axial_rope(qk(q)), k_all, v_all, lam, lam_init, subln_g)
    yc = diff_attend(qk(qc), qk(kc), vs(vc), lam, lam_init, subln_g) if need_ctx else None
    return y, yc


def merge_branches(ys, p_gate, w_br, w_o):
    gates = jnp.split(p_gate, N_BRANCH, -1)
    acc = None
    for j in range(N_BRANCH):
        term = jax.nn.sigmoid(gates[j]) * (ys[j] @ w_br[j])
        acc = term if acc is None else acc + term
    return acc @ w_o


def conv_ffn(u, w_up, conv_w, conv_b, w_down):
    hdn = dwconv3(u @ w_up, conv_w, conv_b)
    a, b = jnp.split(hdn, 2, -1)
    return (jax.nn.silu(a) * b) @ w_down


def setup_inputs(seed: int = 0) -> dict:
    key = jax.random.key(seed)
    ks = iter(jax.random.split(key, 64))
    D = D_MODEL

    def nrm(shape, scale):
        return jax.random.normal(next(ks), shape, F32) * scale

    def near_one(shape):
        return 1.0 + nrm(shape, 0.02)

    return {
        'x': nrm((BATCH, SEQ, D), 1.0),
        'c': nrm((BATCH, D), 1.0),
        'ctx': nrm((BATCH, CTX_LEN, D), 1.0),
        'c_ctx': nrm((D,), 1.0),
        'ada_w': nrm((DEPTH, D, 6 * D), D ** -0.5),
        'ada_b': nrm((DEPTH, 6 * D), 0.02),
        'w_in': nrm((DEPTH, D, IN_COLS), D ** -0.5),
        'hy_conv_w': nrm((DEPTH, 3, 3 * HY_W), 3 ** -0.5),
        'hy_conv_b': nrm((DEPTH, 3 * HY_W), 0.02),
        'hy_f_w1': nrm((DEPTH, HY_EMB, HY_FFN), HY_EMB ** -0.5),
        'hy_f_b1': nrm((DEPTH, HY_FFN), 0.02),
        'hy_f_w2': nrm((DEPTH, HY_FFN, HY_FFN), HY_FFN ** -0.5),
        'hy_f_b2': nrm((DEPTH, HY_FFN), 0.02),
        'hy_f_w3': nrm((DEPTH, HY_FFN, HY_ORDER * HY_DIRS * HY_W), HY_FFN ** -0.5),
        'hy_f_freq': near_one((DEPTH, HY_FFN)),
        'hy_bias': nrm((DEPTH, HY_ORDER, HY_W), 1.0),
        'swa_sink': nrm((DEPTH, SWA_HEADS), 0.5),
        'rwkv_mu': jax.random.uniform(next(ks), (DEPTH, sum(RW_SIZES)), F32),
        'rwkv_w0': jnp.linspace(-6.5, -1.5, RW_W, dtype=F32)[None, None, :] + nrm((DEPTH, 2, RW_W), 0.1),
        'rwkv_w2': nrm((DEPTH, 2, RW_DECAY_R, RW_W), 0.5 * RW_DECAY_R ** -0.5),
        'rwkv_a0': nrm((DEPTH, RW_W), 0.1),
        'rwkv_a2': nrm((DEPTH, RW_AAA_R, RW_W), 0.5 * RW_AAA_R ** -0.5),
        'rwkv_g2': nrm((DEPTH, 2, RW_GATE_R, RW_W), RW_GATE_R ** -0.5),
        'rwkv_kk': 0.85 + nrm((DEPTH, RW_W), 0.02),
        'rwkv_ka': near_one((DEPTH, RW_W)),
        'rwkv_rk': nrm((DEPTH, RW_HEADS, RW_HD), 0.1),
        'rwkv_lnx_g': near_one((DEPTH, RW_W)),
        'rwkv_lnx_b': nrm((DEPTH, RW_W), 0.02),
        'diff_lq1': nrm((DEPTH, DF_HD), 0.1),
        'diff_lk1': nrm((DEPTH, DF_HD), 0.1),
        'diff_lq2': nrm((DEPTH, DF_HD), 0.1),
        'diff_lk2': nrm((DEPTH, DF_HD), 0.1),
        'diff_subln_g': near_one((DEPTH, DF_VD)),
        'w_branch': nrm((DEPTH, N_BRANCH, BR_W, D), DN_BETA * BR_W ** -0.5),
        'w_out': nrm((DEPTH, D, D), DN_BETA * D ** -0.5),
        'ln1_g': near_one((DEPTH, D)),
        'ln1_b': nrm((DEPTH, D), 0.02),
        'ffn_w_up': nrm((DEPTH, D, 2 * D_FF), D ** -0.5),
        'ffn_conv_w': nrm((DEPTH, 3, 2 * D_FF), 3 ** -0.5),
        'ffn_conv_b': nrm((DEPTH, 2 * D_FF), 0.02),
        'ffn_w_down': nrm((DEPTH, D_FF, D), DN_BETA * D_FF ** -0.5),
        'ln2_g': near_one((DEPTH, D)),
        'ln2_b': nrm((DEPTH, D), 0.02),
    }


def reference(x, c, ctx, c_ctx, ada_w, ada_b, w_in, hy_conv_w, hy_conv_b, hy_f_w1, hy_f_b1,
              hy_f_w2, hy_f_b2, hy_f_w3, hy_f_freq, hy_bias, swa_sink, rwkv_mu, rwkv_w0, rwkv_w2,
              rwkv_a0, rwkv_a2, rwkv_g2, rwkv_kk, rwkv_ka, rwkv_rk, rwkv_lnx_g, rwkv_lnx_b,
              diff_lq1, diff_lk1, diff_lq2, diff_lk2, diff_subln_g, w_branch, w_out, ln1_g, ln1_b,
              ffn_w_up, ffn_conv_w, ffn_conv_b, ffn_w_down, ln2_g, ln2_b):
    h, hc = x, ctx
    s_lat = jax.nn.silu(c)
    s_ctx = jax.nn.silu(c_ctx)
    for i in range(DEPTH):
        need_ctx = i < DEPTH - 1
        mod = (s_lat @ ada_w[i] + ada_b[i])[:, None, :]
        mod_c = s_ctx @ ada_w[i] + ada_b[i]
        sh1, sc1, g1, sh2, sc2, g2 = jnp.split(mod, 6, -1)
        csh1, csc1, cg1, csh2, csc2, cg2 = jnp.split(mod_c, 6, -1)

        u = modulate(h, sh1, sc1)
        uc = modulate(hc, csh1, csc1)
        p_hy, p_sw, p_rw, p_df, p_gt = split_at(u @ w_in[i], IN_SIZES)
        pc_hy, pc_sw, pc_rw, pc_df, pc_gt = split_at(uc @ w_in[i], IN_SIZES)

        y_hy, yc_hy = hyena_branch(p_hy, pc_hy, need_ctx, hy_conv_w[i], hy_conv_b[i], hy_f_w1[i],
                                   hy_f_b1[i], hy_f_w2[i], hy_f_b2[i], hy_f_w3[i], hy_f_freq[i],
                                   hy_bias[i])
        y_sw, yc_sw = swa_branch(p_sw, pc_sw, need_ctx, swa_sink[i])
        y_rw, yc_rw = rwkv_branch(p_rw, pc_rw, need_ctx, rwkv_mu[i], rwkv_w0[i], rwkv_w2[i],
                                  rwkv_a0[i], rwkv_a2[i], rwkv_g2[i], rwkv_kk[i], rwkv_ka[i],
                                  rwkv_rk[i], rwkv_lnx_g[i], rwkv_lnx_b[i])
        lam_init = 0.8 - 0.6 * math.exp(-0.3 * i)
        y_df, yc_df = diff_branch(p_df, pc_df, need_ctx, diff_lq1[i], diff_lk1[i], diff_lq2[i],
                                  diff_lk2[i], diff_subln_g[i], lam_init)

        mix = merge_branches((y_hy, y_sw, y_rw, y_df), p_gt, w_branch[i], w_out[i])
        h = layer_norm(DN_ALPHA * h + g1 * mix, ln1_g[i], ln1_b[i])
        f = conv_ffn(modulate(h, sh2, sc2), ffn_w_up[i], ffn_conv_w[i], ffn_conv_b[i], ffn_w_down[i])
        h = layer_norm(DN_ALPHA * h + g2 * f, ln2_g[i], ln2_b[i])

        if need_ctx:
            mix_c = merge_branches((yc_hy, yc_sw, yc_rw, yc_df), pc_gt, w_branch[i], w_out[i])
            hc = layer_norm(DN_ALPHA * hc + cg1 * mix_c, ln1_g[i], ln1_b[i])
            fc = conv_ffn(modulate(hc, csh2, csc2), ffn_w_up[i], ffn_conv_w[i], ffn_conv_b[i],
                          ffn_w_down[i])
            hc = layer_norm(DN_ALPHA * hc + cg2 * fc, ln2_g[i], ln2_b[i])
    return h
```

```python
import numpy as np
import concourse.bass as bass
import concourse.mybir as mybir
from contextlib import ExitStack

F32 = mybir.dt.float32
BF16 = mybir.dt.bfloat16
AF = mybir.ActivationFunctionType
ALU = mybir.AluOpType
AX = mybir.AxisListType

EPOCH = 30000
NDMA = 8


class V:
    __slots__ = ("ap", "key")

    def __init__(self, ap, key):
        self.ap = ap
        self.key = key

    def __getitem__(self, idx):
        return V(self.ap[idx], self.key)

    def k(self, sub):
        base = self.key[0] if isinstance(self.key, tuple) else self.key
        return V(self.ap, (base, sub))

    def re(self, s, **kw):
        return V(self.ap.rearrange(s, **kw), self.key)

    def bc(self, shape):
        return V(self.ap.to_broadcast(list(shape)), self.key)

    def bitcast(self, dt):
        return V(self.ap.bitcast(dt), self.key)


def _ap(x):
    return x.ap if isinstance(x, V) else x


class Prog:
    ENG = ("pe", "act", "dve", "pool", "sp")

    def __init__(self, nc):
        self.nc = nc
        self.es = ExitStack()
        self.ops = {e: [] for e in self.ENG}
        self.cnt = {e: 0 for e in self.ENG}
        self.sems = {e: [] for e in self.ENG}
        self.waited = {}
        self.last_w = {}
        self.readers = {}
        self.dma_cnt = {e: 0 for e in self.ENG}
        self.dma_sems = {e: None for e in self.ENG}
        self.nbuf = 0
        self.out_tokens = []

    def sb(self, shape, dt=F32, name=None, stack=None):
        self.nbuf += 1
        name = name or f"sb{self.nbuf}"
        h = (stack or self.es).enter_context(self.nc.sbuf_tensor(f"{name}_{self.nbuf}", list(shape), dt))
        return V(h[:] if hasattr(h, "__getitem__") else h.ap(), f"{name}_{self.nbuf}")

    def ps(self, shape, dt=F32, name=None, stack=None):
        self.nbuf += 1
        name = name or f"ps{self.nbuf}"
        h = (stack or self.es).enter_context(self.nc.psum_tensor(f"{name}_{self.nbuf}", list(shape), dt))
        return V(h[:] if hasattr(h, "__getitem__") else h.ap(), f"{name}_{self.nbuf}")

    def dram(self, name, shape, dt=F32, kind="Internal"):
        h = self.nc.dram_tensor(name, list(shape), dt, kind=kind)
        return V(h.ap(), name)

    def _sem_for(self, eng, idx):
        ep = idx // EPOCH
        while len(self.sems[eng]) <= ep:
            s = self.es.enter_context(self.nc.semaphore(f"s_{eng}_{len(self.sems[eng])}"))
            self.sems[eng].append(s)
        return self.sems[eng][ep], (idx % EPOCH) + 1

    def _wait(self, eng, tok):
        if tok is None:
            return
        if tok[0] == "e":
            _, src, idx = tok
            if src == eng and eng == "pe":
                return
            if self.waited.get((eng, src), -1) >= idx:
                return
            self.waited[(eng, src)] = idx
            sem, val = self._sem_for(src, idx)
            self.ops[eng].append(("w", sem, val))
        else:
            _, sem, val, sid = tok
            if self.waited.get((eng, sid), -1) >= val:
                return
            self.waited[(eng, sid)] = val
            self.ops[eng].append(("w", sem, val))

    def _deps(self, eng, reads, writes):
        for k in reads:
            self._wait(eng, self.last_w.get(k))
        for k in writes:
            self._wait(eng, self.last_w.get(k))
            for t in self.readers.get(k, ()):
                self._wait(eng, t)

    def _commit(self, tok, reads, writes):
        for k in reads:
            self.readers.setdefault(k, []).append(tok)
        for k in writes:
            self.last_w[k] = tok
            self.readers[k] = []

    def op(self, eng, fn, outs, ins):
        reads = [x.key for x in ins if isinstance(x, V)]
        writes = [x.key for x in outs if isinstance(x, V)]
        self._deps(eng, reads, writes)
        idx = self.cnt[eng]
        self.cnt[eng] += 1
        sem, _ = self._sem_for(eng, idx)
        self.ops[eng].append(("i", fn, sem, 1))
        tok = ("e", eng, idx)
        self._commit(tok, reads, writes)
        return tok

    def dma(self, out, in_, eng="sp", **kw):
        reads = [in_.key]
        writes = [out.key]
        self._deps(eng, reads, writes)
        if self.dma_sems[eng] is None:
            self.dma_sems[eng] = [self.es.enter_context(self.nc.semaphore(f"d_{eng}_{i}")) for i in range(NDMA)]
        j = self.dma_cnt[eng]
        self.dma_cnt[eng] += 1
        s = j % NDMA
        sem = self.dma_sems[eng][s]
        sid = f"d_{eng}_{s}"
        if j >= NDMA:
            self._wait(eng, ("d", sem, 16 * (j // NDMA), sid))
        o, i = out.ap, in_.ap
        self.ops[eng].append(("i", lambda e: e.dma_start(out=o, in_=i, **kw), sem, 16))
        tok = ("d", sem, 16 * (j // NDMA + 1), sid)
        self._commit(tok, reads, writes)
        return tok

    def mm(self, out, lhsT, rhs, start=True, stop=True):
        o, l, r = out.ap, lhsT.ap, rhs.ap
        return self.op("pe", lambda e: e.matmul(o, l, r, start=start, stop=stop), [out], [lhsT, rhs])

    def transpose(self, out, in_, ident):
        o, i, d = out.ap, in_.ap, ident.ap
        return self.op("pe", lambda e: e.transpose(o, i, d), [out], [in_, ident])

    def act(self, out, in_, func, bias=0.0, scale=1.0, accum_out=None):
        o, i, b, s = out.ap, in_.ap, _ap(bias), _ap(scale)
        if accum_out is not None:
            a = accum_out.ap
            fn = lambda e: e.activation(o, i, func, bias=b, scale=s, accum_out=a)
            outs = [out, accum_out]
        else:
            fn = lambda e: e.activation(o, i, func, bias=b, scale=s)
            outs = [out]
        return self.op("act", fn, outs, [in_, bias, scale])

    def tt(self, out, a, b, op, eng="dve"):
        o, x, y = out.ap, a.ap, b.ap
        return self.op(eng, lambda e: e.tensor_tensor(o, x, y, op), [out], [a, b])

    def ts(self, out, a, s1, op0, s2=None, op1=None, eng="dve", accum_out=None):
        o, x, p, q = out.ap, a.ap, _ap(s1), _ap(s2)
        kw = {}
        outs = [out]
        if op1 is not None:
            kw["op1"] = op1
        if accum_out is not None:
            kw["accum_out"] = accum_out.ap
            outs.append(accum_out)
        return self.op(eng, lambda e: e.tensor_scalar(o, x, p, q, op0, **kw), outs, [a, s1, s2])

    def stt(self, out, a, scalar, b, op0, op1, eng="dve"):
        o, x, s, y = out.ap, a.ap, _ap(scalar), b.ap
        return self.op("dve", lambda e: e.scalar_tensor_tensor(o, x, s, y, op0, op1), [out], [a, scalar, b])

    def copy(self, out, in_, eng="dve"):
        o, i = out.ap, in_.ap
        if eng == "act":
            return self.op("act", lambda e: e.copy(o, i), [out], [in_])
        return self.op(eng, lambda e: e.tensor_copy(o, i), [out], [in_])

    def reduce(self, out, in_, op, axis=AX.X, eng="dve", abs_=None):
        o, i = out.ap, in_.ap
        kw = {}
        if abs_:
            kw["apply_absolute_value"] = True
        return self.op(eng, lambda e: e.tensor_reduce(o, i, axis, op, **kw), [out], [in_])

    def recip(self, out, in_):
        o, i = out.ap, in_.ap
        return self.op("dve", lambda e: e.reciprocal(o, i), [out], [in_])

    def memset(self, out, val, eng="dve"):
        o = out.ap
        return self.op(eng, lambda e: e.memset(o, val), [out], [])

    def iota(self, out, pattern, base=0, channel_multiplier=0):
        o = out.ap
        return self.op("pool", lambda e: e.iota(o, pattern, base=base, channel_multiplier=channel_multiplier,
                                                allow_small_or_imprecise_dtypes=True), [out], [])

    def barrier(self):
        toks = [("e", e, self.cnt[e] - 1) for e in self.ENG if self.cnt[e] > 0]
        for e in self.ENG:
            for j in range(min(self.dma_cnt[e], NDMA)):
                jj = self.dma_cnt[e] - 1 - j
                s = jj % NDMA
                toks.append(("d", self.dma_sems[e][s], 16 * (jj // NDMA + 1), f"d_{e}_{s}"))
        for e in self.ENG:
            for t in toks:
                self._wait(e, t)
        self.last_w = {}
        self.readers = {}

    def finalize(self):
        self.barrier()
        nc = self.nc
        ops = self.ops
        with nc.Block() as block:
            def mk(name):
                def body(e):
                    for it in ops[name]:
                        if it[0] == "w":
                            e.wait_ge(it[1], it[2])
                        else:
                            it[1](e).then_inc(it[2], it[3])
                return body
            block.tensor(mk("pe"))
            block.scalar(mk("act"))
            block.vector(mk("dve"))
            block.gpsimd(mk("pool"))
            block.sync(mk("sp"))
        self.es.close()
from concourse.bass_utils import run_bass_kernel_spmd
D = 1024
SEQ = 4096
CTX = 256
NTOK = SEQ + CTX
NT = NTOK // 128
INC = 7360
UTW = NTOK + 3
LN_EPS = 1e-6
HY0 = 0
SWQ0, SWK0, SWV0 = 768, 1024, 1152
RW0 = 1280
DFQ0, DFK0, DFV0 = 2496, 2752, 3008
GT0 = 3264
ALPHA = 4 ** 0.25


def ucol(tok):
    return tok + 1 if tok < CTX else tok + 2


def bc_rows(v, n):
    return V(v.ap.to_broadcast([128, n]), v.key)


def stage_mod(P, S, l):
    with ExitStack() as st:
        cs = P.sb([128, 2, 8], stack=st)
        P.dma(cs, S["cvec"].re("w (kc p) -> p w kc", p=128), allow_slow_non_contiguous=True)
        P.act(cs, cs, AF.Silu)
        brow = P.sb([1, 6144], stack=st)
        P.dma(brow, S["ada_b"][l:l + 1, :])
        res = P.sb([1, 2, 6144], stack=st)
        wts = [P.sb([128, 8, 512], stack=st) for _ in range(2)]
        pss = [P.ps([1, 512], stack=st) for _ in range(2)]
        for nb in range(12):
            wt = wts[nb % 2]
            P.dma(wt, S["ada_w"][l, :, nb * 512:(nb + 1) * 512].re("(kc p) n -> p kc n", p=128),
                  eng="sp" if nb % 2 == 0 else "pool")
            for w in range(2):
                ps = pss[w]
                for kc in range(8):
                    P.mm(ps, cs[:, w, kc:kc + 1], wt[:, kc, :], start=(kc == 0), stop=(kc == 7))
                P.tt(res[:, w, nb * 512:(nb + 1) * 512], ps, brow[:, nb * 512:(nb + 1) * 512], ALU.add)
        for w in range(2):
            for off in (1024, 4096):
                P.ts(res[:, w, off:off + 1024], res[:, w, off:off + 1024], 1.0, ALU.add)
        P.dma(S["MODD"], res.re("o w n -> o (w n)"))
    P.barrier()


def ln_stats(P, xt, junk, stt_):
    P.memset(stt_, 0.0)
    P.act(junk, xt, AF.Identity, accum_out=stt_[:, 0:1])
    P.act(junk, xt, AF.Square, accum_out=stt_[:, 1:2])
    P.ts(stt_[:, 0:1], stt_[:, 0:1], 1.0 / D, ALU.mult)
    P.tt(stt_[:, 2:3], stt_[:, 0:1], stt_[:, 0:1], ALU.mult)
    P.stt(stt_[:, 2:3], stt_[:, 1:2], 1.0 / D, stt_[:, 2:3], ALU.mult, ALU.subtract)
    P.ts(stt_[:, 2:3], stt_[:, 2:3], LN_EPS, ALU.add)
    P.act(stt_[:, 2:3], stt_[:, 2:3], AF.Sqrt)
    P.recip(stt_[:, 3:4], stt_[:, 2:3])
    P.stt(stt_[:, 4:5], stt_[:, 0:1], -1.0, stt_[:, 3:4], ALU.mult, ALU.mult)


def stage_lnmod(P, S, src, shoff, scoff, uT):
    with ExitStack() as st:
        _stage_lnmod(P, S, src, shoff, scoff, uT, st)
    P.barrier()


def _stage_lnmod(P, S, src, shoff, scoff, uT, st):
    modb = {}
    for w in range(2):
        sh = P.sb([128, D], stack=st)
        sc = P.sb([128, D], stack=st)
        P.dma(sh, bc_rows(S["MODD"][:, w * 6144 + shoff: w * 6144 + shoff + D], D))
        P.dma(sc, bc_rows(S["MODD"][:, w * 6144 + scoff: w * 6144 + scoff + D], D))
        modb[w] = (sh, sc)
    xts = [P.sb([128, D], stack=st) for _ in range(2)]
    junk = P.sb([128, D], stack=st)
    xn = P.sb([128, D], stack=st)
    ub = P.sb([128, D], BF16, stack=st)
    sts = [P.sb([128, 8], stack=st) for _ in range(2)]
    pts = [P.ps([128, 8, 128], BF16, stack=st) for _ in range(2)]
    P.memset(uT[:, :, 0:1], 0.0)
    P.memset(uT[:, :, CTX + 1:CTX + 2], 0.0)
    P.memset(uT[:, :, UTW - 1:UTW], 0.0)
    for t in range(NT):
        xt = xts[t % 2]
        s_ = sts[t % 2]
        P.dma(xt, src[t * 128:(t + 1) * 128, :], eng="sp" if t % 2 == 0 else "pool")
        ln_stats(P, xt, junk, s_)
        P.act(xn, xt, AF.Identity, bias=s_[:, 4:5], scale=s_[:, 3:4])
        sh, sc = modb[1 if t < 2 else 0]
        P.tt(xn, xn, sc, ALU.mult)
        P.tt(ub, xn, sh, ALU.add)
        pt = pts[t % 2]
        for kc in range(8):
            P.transpose(pt[:, kc, :], ub[:, kc * 128:(kc + 1) * 128], S["identb"])
        c0 = ucol(t * 128)
        P.copy(uT[:, :, c0:c0 + 128], pt, eng="act")


TOKCH = [(1, 0, 256)] + [(258 + i * 512, 256 + i * 512, 512) for i in range(8)]


def wview(S, name, l, c0, n):
    return S[name][l, :, c0:c0 + n].re("(kc p) n -> p kc n", p=128)


def stage_inproj(P, S, l, uT):
    W = "w_in"
    with ExitStack() as st:
        wst = [P.sb([128, 8, 384], stack=st) for _ in range(2)]
        wb = [[P.sb([128, 8, 384], BF16, stack=st) for _ in range(3)] for _ in range(2)]
        tap = [P.sb([128, 3, 384], stack=st) for _ in range(2)]
        bia = [P.sb([128, 384], stack=st) for _ in range(2)]
        pss = [P.ps([128, 384], stack=st) for _ in range(3)]
        osb = [P.sb([128, 384], stack=st) for _ in range(3)]
        blocks = []
        for c in range(0, 768, 384):
            blocks.append(("hy", HY0 + c, c, 384))
        for c, n in ((0, 384), (384, 384), (768, 384), (1152, 64)):
            blocks.append(("rw", RW0 + c, c, n))
        for c, n in ((0, 128),):
            blocks.append(("vsw", SWV0, 0, 128))
        blocks.append(("vdf", DFV0, 0, 256))
        it = 0
        for bi, (kind, wc0, oc0, n) in enumerate(blocks):
            ws = wst[bi % 2]
            P.dma(ws[:, :, 0:n], wview(S, W, l, wc0, n), eng="sp" if bi % 2 == 0 else "pool")
            wbb = wb[bi % 2]
            tp = tap[bi % 2]
            if kind == "hy":
                for j in range(3):
                    P.dma(tp[:, j, 0:n], bc_rows(S["hy_conv_w"][l, j:j + 1, oc0:oc0 + n], n))
                P.dma(bia[bi % 2][:, 0:n], bc_rows(S["hy_conv_b"][l:l + 1, oc0:oc0 + n], n))
                for j in range(3):
                    for kc in range(8):
                        P.tt(wbb[j][:, kc, 0:n], ws[:, kc, 0:n], tp[:, j, 0:n], ALU.mult, eng="pool" if kc % 2 else "dve")
                nj = 3
                dst = S["HYP"]
            elif kind == "rw":
                P.dma(tp[:, 0, 0:n], bc_rows(S["rwkv_mu"][l:l + 1, oc0:oc0 + n], n))
                P.ts(tp[:, 1, 0:n], tp[:, 0, 0:n], -1.0, ALU.mult, 1.0, ALU.add)
                P.ts(tp[:, 2, 0:n], tp[:, 0, 0:n], 0.5, ALU.mult)
                for kc in range(8):
                    P.tt(wbb[1][:, kc, 0:n], ws[:, kc, 0:n], tp[:, 1, 0:n], ALU.mult, eng="pool" if kc % 2 else "dve")
                    P.tt(wbb[0][:, kc, 0:n], ws[:, kc, 0:n], tp[:, 2, 0:n], ALU.mult, eng="pool" if kc % 2 else "dve")
                nj = 3
                dst = S["RWP"]
            else:
                for kc in range(8):
                    P.copy(wbb[1][:, kc, 0:n], ws[:, kc, 0:n], eng="pool" if kc % 2 else "dve")
                nj = 1
                dst = S["VSW"] if kind == "vsw" else S["VDF"]
            for t in range(NT):
                ps = pss[it % 3]
                ob = osb[it % 3]
                it += 1
                c0 = ucol(t * 128)
                if nj == 3:
                    seq = [(0, -1), (1, 0), (2 if kind == "hy" else 0, 1)]
                else:
                    seq = [(1, 0)]
                k = 0
                tot = len(seq) * 8
                for (wj, sh) in seq:
                    for kc in range(8):
                        P.mm(ps[:, 0:n], uT[:, kc, c0 + sh:c0 + sh + 128], wbb[wj][:, kc, 0:n],
                             start=(k == 0), stop=(k == tot - 1))
                        k += 1
                if kind == "hy":
                    P.tt(ob[:, 0:n], ps[:, 0:n], bia[bi % 2][:, 0:n], ALU.add)
                else:
                    P.copy(ob[:, 0:n], ps[:, 0:n], eng="act")
                P.dma(dst[t * 128:(t + 1) * 128, oc0:oc0 + n], ob[:, 0:n], eng="sp" if it % 2 else "pool")
    P.barrier()
    with ExitStack() as st:
        NU = 14
        ws = P.sb([128, 8, 896], stack=st)
        wq = P.sb([128, 8, 896], BF16, stack=st)
        wsw = P.sb([128, 8, 896], BF16, stack=st)
        srcs = [(SWQ0, 256, 0), (SWK0, 128, 256), (DFQ0, 256, 384), (DFK0, 256, 640)]
        for (c0, n, o) in srcs:
            P.dma(ws[:, :, o:o + n], wview(S, W, l, c0, n))
        for kc in range(8):
            P.copy(wq[:, kc, :], ws[:, kc, :], eng="pool" if kc % 2 else "dve")
        P.dma(ws, wview(S, "w_swap", l, 0, 896), eng="pool")
        for kc in range(8):
            P.copy(wsw[:, kc, :], ws[:, kc, :], eng="pool" if kc % 2 else "dve")
        rts = [P.sb([64, 4, 512], stack=st) for _ in range(2)]
        psa = [P.ps([64, 512], stack=st) for _ in range(2)]
        psb = [P.ps([64, 512], stack=st) for _ in range(2)]
        t1 = [P.sb([64, 512], stack=st) for _ in range(2)]
        t2 = [P.sb([64, 512], stack=st) for _ in range(2)]
        ob = [P.sb([64, 512], BF16, stack=st) for _ in range(2)]
        it = 0
        for ci, (uc0, tok0, n) in enumerate(TOKCH):
            rt = rts[ci % 2]
            P.dma(rt[:, :, 0:n], S["rope"][:, :, tok0:tok0 + n].re("f d t -> d f t"))
            for u in range(NU):
                pa, pb = psa[it % 2], psb[it % 2]
                a1, a2, o_ = t1[it % 2], t2[it % 2], ob[it % 2]
                it += 1
                for kc in range(8):
                    P.mm(pa[:, 0:n], wq[:, kc, u * 64:(u + 1) * 64], uT[:, kc, uc0:uc0 + n], start=(kc == 0), stop=(kc == 7))
                for kc in range(8):
                    P.mm(pb[:, 0:n], wsw[:, kc, u * 64:(u + 1) * 64], uT[:, kc, uc0:uc0 + n], start=(kc == 0), stop=(kc == 7))
                tb = 0 if u < 6 else 2
                P.tt(a1[:, 0:n], pa[:, 0:n], rt[:, tb, 0:n], ALU.mult)
                P.tt(a2[:, 0:n], pb[:, 0:n], rt[:, tb + 1, 0:n], ALU.mult, eng="dve")
                P.tt(o_[:, 0:n], a1[:, 0:n], a2[:, 0:n], ALU.add, eng="pool")
                P.dma(S["QK"][u, :, tok0:tok0 + n], o_[:, 0:n], eng="sp" if it % 2 else "pool")
    P.barrier()
    with ExitStack() as st:
        wst = [P.sb([128, 8, 512], stack=st) for _ in range(2)]
        wbs = [P.sb([128, 8, 512], BF16, stack=st) for _ in range(2)]
        pss = [P.ps([128, 512], stack=st) for _ in range(3)]
        osb = [P.sb([128, 512], stack=st) for _ in range(3)]
        it = 0
        for bi in range(8):
            ws = wst[bi % 2]
            wb_ = wbs[bi % 2]
            P.dma(ws, wview(S, W, l, GT0 + bi * 512, 512), eng="sp" if bi % 2 == 0 else "pool")
            for kc in range(8):
                P.copy(wb_[:, kc, :], ws[:, kc, :], eng="pool" if kc % 2 else "dve")
            for cc in range(4):
                for (uc0, tok0, n) in TOKCH:
                    ps = pss[it % 3]
                    o_ = osb[it % 3]
                    it += 1
                    for kc in range(8):
                        P.mm(ps[:, 0:n], wb_[:, kc, cc * 128:(cc + 1) * 128], uT[:, kc, uc0:uc0 + n], start=(kc == 0), stop=(kc == 7))
                    P.act(o_[:, 0:n], ps[:, 0:n], AF.Sigmoid)
                    r0 = bi * 512 + cc * 128
                    P.dma(S["GT"][r0:r0 + 128, tok0:tok0 + n], o_[:, 0:n], eng="sp" if it % 2 else "pool")
    P.barrier()


def resid_ln(P, pss, hsrc, gb, lg, lb, dst_rows, tmp, stt_, res, eng_i):
    for hf in range(2):
        P.tt(tmp[:, hf * 512:(hf + 1) * 512], pss[hf], gb[:, hf * 512:(hf + 1) * 512], ALU.mult)
    P.stt(tmp, hsrc, ALPHA, tmp, ALU.mult, ALU.add, eng="pool")
    ln_stats(P, tmp, res, stt_)
    P.act(res, tmp, AF.Identity, bias=stt_[:, 4:5], scale=stt_[:, 3:4])
    P.tt(res, res, lg, ALU.mult, eng="pool")
    P.tt(res, res, lb, ALU.add, eng="pool")
    for (d, r0, r1) in dst_rows:
        P.dma(d, res[r0:r1, :], eng="sp" if eng_i % 2 else "pool")


def load_bc(P, S, st, name, l, n=D):
    t = P.sb([128, n], stack=st)
    P.dma(t, bc_rows(S[name][l:l + 1, :], n))
    return t


def stage_merge(P, S, l, src):
    with ExitStack() as st:
        wbr = P.sb([128, 8, D], BF16, stack=st)
        wo = P.sb([128, 8, D], BF16, stack=st)
        wst = P.sb([128, 8, D], stack=st)
        P.dma(wst, S["w_branch"][l].re("j (kc p) n -> p (j kc) n", p=128))
        for i in range(8):
            P.copy(wbr[:, i, :], wst[:, i, :], eng="pool" if i % 2 else "dve")
        P.dma(wst, S["w_out"][l].re("(kc p) n -> p kc n", p=128))
        for i in range(8):
            P.copy(wo[:, i, :], wst[:, i, :], eng="pool" if i % 2 else "dve")
        gbc = []
        for w in range(2):
            g = P.sb([128, D], stack=st)
            P.dma(g, bc_rows(S["MODD"][:, w * 6144 + 2048: w * 6144 + 3072], D))
            gbc.append(g)
        lg = load_bc(P, S, st, "ln1_g", l)
        lb = load_bc(P, S, st, "ln1_b", l)
        yts = [P.sb([128, 8, 512], BF16, stack=st) for _ in range(2)]
        gts = [P.sb([128, 4, 512], stack=st) for _ in range(2)]
        accT = P.sb([128, 8, 512], BF16, stack=st)
        acc = P.sb([128, 512], stack=st)
        prod = [P.sb([128, 512], stack=st) for _ in range(2)]
        pss = [P.ps([128, 512], stack=st) for _ in range(4)]
        pmix = [P.ps([128, 512], stack=st) for _ in range(2)]
        hts = [P.sb([128, D], stack=st) for _ in range(2)]
        tmp = P.sb([128, D], stack=st)
        res = P.sb([128, D], stack=st)
        sts = [P.sb([128, 8], stack=st) for _ in range(2)]
        GTv = S["GT"].re("(j dc p) t -> dc p j t", j=4, dc=8, p=128)
        YTv = S["YT"].re("j (kc p) t -> p (j kc) t", p=128)
        it = 0
        ti = 0
        for ci, (uc0, tok0, n) in enumerate(TOKCH):
            yt = yts[ci % 2]
            P.dma(yt[:, :, 0:n], YTv[:, :, tok0:tok0 + n])
            for dc in range(8):
                gt = gts[dc % 2]
                P.dma(gt[:, :, 0:n], GTv[dc][:, :, tok0:tok0 + n], eng="pool")
                for j in range(4):
                    ps = pss[it % 4]
                    it += 1
                    for k2 in range(2):
                        P.mm(ps[:, 0:n], wbr[:, j * 2 + k2, dc * 128:(dc + 1) * 128], yt[:, j * 2 + k2, 0:n],
                             start=(k2 == 0), stop=(k2 == 1))
                    if j == 0:
                        P.tt(acc[:, 0:n], ps[:, 0:n], gt[:, j, 0:n], ALU.mult)
                    else:
                        pr = prod[j % 2]
                        P.tt(pr[:, 0:n], ps[:, 0:n], gt[:, j, 0:n], ALU.mult)
                        if j < 3:
                            P.tt(acc[:, 0:n], acc[:, 0:n], pr[:, 0:n], ALU.add, eng="pool")
                        else:
                            P.tt(accT[:, dc, 0:n], acc[:, 0:n], pr[:, 0:n], ALU.add, eng="pool")
            for tt_ in range(n // 128):
                tok = tok0 + tt_ * 128
                ht = hts[ti % 2]
                P.dma(ht, src[tok:tok + 128, :])
                for hf in range(2):
                    for dc in range(8):
                        P.mm(pmix[hf], accT[:, dc, tt_ * 128:(tt_ + 1) * 128], wo[:, dc, hf * 512:(hf + 1) * 512],
                             start=(dc == 0), stop=(dc == 7))
                resid_ln(P, pmix, ht, gbc[1 if tok < CTX else 0], lg, lb,
                         [(S["H1"][tok:tok + 128, :], 0, 128)], tmp, sts[ti % 2], res, ti)
                ti += 1
    P.barrier()


def stage_ffn(P, S, l, last):
    FC = 22
    with ExitStack() as st:
        uT = P.sb([128, 8, UTW], BF16, stack=st)
        stage_lnmod(P, S, S["H1"], 3072, 4096, uT)
        with ExitStack() as st2:
            wst = [P.sb([128, 8, 256], stack=st2) for _ in range(2)]
            wbs = [P.sb([128, 8, 256], BF16, stack=st2) for _ in range(2)]
            taps = [P.sb([128, 2, 4], stack=st2) for _ in range(2)]
            hT = [P.sb([128, UTW], stack=st2) for _ in range(2)]
            cT = [P.sb([128, UTW], stack=st2) for _ in range(2)]
            gT = [P.sb([128, UTW], BF16, stack=st2) for _ in range(2)]
            pss = [P.ps([128, 512], stack=st2) for _ in range(4)]
            for ab in range(2):
                P.memset(hT[ab], 0.0)
                P.memset(cT[ab], 0.0)
            it = 0
            for fi in range(FC):
                ws, wb_, tp = wst[fi % 2], wbs[fi % 2], taps[fi % 2]
                for ab in range(2):
                    c0 = ab * 2816 + fi * 128
                    P.dma(ws[:, :, ab * 128:(ab + 1) * 128], wview(S, "ffn_w_up", l, c0, 128), eng="sp" if ab else "pool")
                    P.dma(tp[:, ab, 0:3], S["ffn_conv_w"][l, :, c0:c0 + 128].re("j p -> p j"), allow_slow_non_contiguous=True)
                    P.dma(tp[:, ab, 3:4], S["ffn_conv_b"][l:l + 1, c0:c0 + 128].re("o p -> p o"), allow_slow_non_contiguous=True)
                for kc in range(8):
                    P.copy(wb_[:, kc, :], ws[:, kc, :], eng="pool" if kc % 2 else "dve")
                for ab in range(2):
                    h = hT[ab]
                    for (uc0, tok0, n) in TOKCH:
                        ps = pss[it % 4]
                        it += 1
                        for kc in range(8):
                            P.mm(ps[:, 0:n], wb_[:, kc, ab * 128:(ab + 1) * 128], uT[:, kc, uc0:uc0 + n], start=(kc == 0), stop=(kc == 7))
                        P.copy(h[:, uc0:uc0 + n], ps[:, 0:n], eng="act")
                    c = cT[ab]
                    e = "dve" if ab == 0 else "pool"
                    Wd = UTW - 2
                    P.ts(c[:, 1:1 + Wd], h[:, 0:Wd], tp[:, ab, 0:1], ALU.mult, tp[:, ab, 3:4], ALU.add, eng=e)
                    P.stt(c[:, 1:1 + Wd], h[:, 1:1 + Wd], tp[:, ab, 1:2], c[:, 1:1 + Wd], ALU.mult, ALU.add, eng=e)
                    P.stt(c[:, 1:1 + Wd], h[:, 2:2 + Wd], tp[:, ab, 2:3], c[:, 1:1 + Wd], ALU.mult, ALU.add, eng=e)
                P.act(cT[0], cT[0], AF.Silu)
                g = gT[fi % 2]
                P.tt(g, cT[0], cT[1], ALU.mult)
                P.dma(S["GFT"][fi * 128:(fi + 1) * 128, :], g, eng="sp")
        P.barrier()
    P.barrier()
    with ExitStack() as st:
        wd = P.sb([128, FC, D], BF16, stack=st)
        wst = [P.sb([128, 2, D], stack=st) for _ in range(2)]
        for i in range(FC // 2):
            P.dma(wst[i % 2], S["ffn_w_down"][l, i * 256:(i + 1) * 256, :].re("(k p) n -> p k n", p=128), eng="sp" if i % 2 else "pool")
            P.copy(wd[:, 2 * i, :], wst[i % 2][:, 0, :], eng="dve")
            P.copy(wd[:, 2 * i + 1, :], wst[i % 2][:, 1, :], eng="pool")
        gbc = []
        for w in range(2):
            g = P.sb([128, D], stack=st)
            P.dma(g, bc_rows(S["MODD"][:, w * 6144 + 5120: w * 6144 + 6144], D))
            gbc.append(g)
        lg = load_bc(P, S, st, "ln2_g", l)
        lb = load_bc(P, S, st, "ln2_b", l)
        gts = [P.sb([128, FC, 128], BF16, stack=st) for _ in range(2)]
        hts = [P.sb([128, D], stack=st) for _ in range(2)]
        pmix = [P.ps([128, 512], stack=st) for _ in range(4)]
        tmp = P.sb([128, D], stack=st)
        res = P.sb([128, D], stack=st)
        sts = [P.sb([128, 8], stack=st) for _ in range(2)]
        GFv = S["GFT"].re("(fc p) t -> p fc t", p=128)
        for t in range(NT):
            tok = t * 128
            c0 = ucol(tok)
            gt = gts[t % 2]
            P.dma(gt, GFv[:, :, c0:c0 + 128], eng="pool")
            ht = hts[t % 2]
            P.dma(ht, S["H1"][tok:tok + 128, :])
            pm = pmix[(t % 2) * 2:(t % 2) * 2 + 2]
            for hf in range(2):
                for fc in range(FC):
                    P.mm(pm[hf], gt[:, fc, :], wd[:, fc, hf * 512:(hf + 1) * 512], start=(fc == 0), stop=(fc == FC - 1))
            if last:
                dsts = [(S["out"][tok - CTX:tok - CTX + 128, :], 0, 128)] if tok >= CTX else []
            else:
                dsts = [(S["H2"][tok:tok + 128, :], 0, 128)]
            if dsts:
                resid_ln(P, pm, ht, gbc[1 if tok < CTX else 0], lg, lb, dsts, tmp, sts[t % 2], res, t)
    P.barrier()


def tok2feat(P, S, src, j):
    with ExitStack() as st:
        xs = [P.sb([128, 256], stack=st) for _ in range(2)]
        xb = [P.sb([128, 256], BF16, stack=st) for _ in range(2)]
        pt = [P.ps([128, 2, 128], BF16, stack=st) for _ in range(2)]
        ob = [P.sb([128, 2, 128], BF16, stack=st) for _ in range(2)]
        for t in range(NT):
            i = t % 2
            P.dma(xs[i], src[t * 128:(t + 1) * 128, :], eng="sp" if i else "pool")
            P.copy(xb[i], xs[i], eng="pool")
            for kc in range(2):
                P.transpose(pt[i][:, kc, :], xb[i][:, kc * 128:(kc + 1) * 128], S["identb"])
            P.copy(ob[i], pt[i], eng="act")
            P.dma(S["YT"][j].re("(kc p) t -> p kc t", p=128)[:, :, t * 128:(t + 1) * 128], ob[i], eng="sp" if i else "pool")
    P.barrier()


def load_vext(P, S, st, src, nh):
    ve = P.sb([128, NT, nh, 65], BF16, stack=st)
    P.memset(ve.re("p t h d -> p (t h d)"), 1.0)
    vs = [P.sb([128, nh * 64], stack=st) for _ in range(2)]
    for t in range(NT):
        i = t % 2
        P.dma(vs[i], src[t * 128:(t + 1) * 128, :], eng="sp" if i else "pool")
        P.copy(ve[:, t, :, 0:64], vs[i].re("p (h d) -> p h d", h=nh), eng="pool" if i else "dve")
    return ve


def stage_swa(P, S, l):
    with ExitStack() as st:
        qk = P.sb([64, 6, NTOK], BF16, stack=st)
        for u in range(6):
            P.dma(qk[:, u, :], S["QK"][u], eng="sp" if u % 2 else "pool")
        ve = load_vext(P, S, st, S["VSW"], 2)
        mk32 = P.sb([128, 2, 512], stack=st)
        P.dma(mk32, S["maskLR"].re("m p f -> p m f"))
        mk = P.sb([128, 2, 512], BF16, stack=st)
        P.copy(mk, mk32)
        es = P.sb([128, 4], stack=st)
        P.dma(es, bc_rows(S["swa_sink"][l:l + 1, :], 4))
        P.act(es, es, AF.Exp)
        pS = [P.ps([128, 4, 128], stack=st) for _ in range(2)]
        pO = [P.ps([128, 4, 65], stack=st) for _ in range(2)]
        E = [P.sb([128, 5, 4, 128], BF16, stack=st) for _ in range(2)]
        den = [P.sb([128, 4], stack=st) for _ in range(2)]
        yt = [P.sb([128, 4, 64], stack=st) for _ in range(2)]
        it = 0
        for qb in range(NT):
            if qb < 2:
                kts = [(0, None), (1, None)]
            else:
                kts = []
                if qb - 1 >= 2:
                    kts.append((qb - 1, 0))
                kts.append((qb, None))
                if qb + 1 < NT:
                    kts.append((qb + 1, 1))
                kts += [(0, None), (1, None)]
            e_ = E[qb % 2]
            for ki, (kt, m) in enumerate(kts):
                ps = pS[it % 2]
                it += 1
                for h in range(4):
                    P.mm(ps[:, h, :], qk[:, 4 + h // 2, kt * 128:(kt + 1) * 128], qk[:, h, qb * 128:(qb + 1) * 128])
                P.act(e_[:, ki].re("p h q -> p (h q)"), ps.re("p h q -> p (h q)"), AF.Exp, scale=0.125)
                if m is not None:
                    P.tt(e_[:, ki].re("p h q -> p (h q)"), e_[:, ki].re("p h q -> p (h q)"), mk[:, m, :], ALU.mult)
            po = pO[qb % 2]
            for h in range(4):
                for ki, (kt, m) in enumerate(kts):
                    P.mm(po[:, h, :], e_[:, ki, h, :], ve[:, kt, h // 2, :], start=(ki == 0), stop=(ki == len(kts) - 1))
            d_ = den[qb % 2]
            y_ = yt[qb % 2]
            P.tt(d_, po[:, :, 64], es, ALU.add)
            P.recip(d_, d_)
            for h in range(4):
                P.ts(y_[:, h, :], po[:, h, 0:64], d_[:, h:h + 1], ALU.mult)
            P.dma(S["Ysw"][qb * 128:(qb + 1) * 128, :], y_.re("p h d -> p (h d)"), eng="sp" if qb % 2 else "pool")
    P.barrier()
    tok2feat(P, S, S["Ysw"], 1)


def stage_diff(P, S, l, lam_init):
    with ExitStack() as st:
        qkc = [P.sb([32, 8, NTOK], BF16, stack=st) for _ in range(2)]
        for u in range(8):
            for c in range(2):
                P.dma(qkc[c][:, u, :], S["QK"][6 + u, c * 32:(c + 1) * 32, :], eng="sp" if u % 2 else "pool")
        ve = load_vext(P, S, st, S["VDF"], 4)
        idf = P.sb([128, 128], stack=st)
        P.dma(idf, S["ident"])
        lv = P.sb([128, 4, 32], stack=st)
        for i, nm in enumerate(("diff_lq1", "diff_lk1", "diff_lq2", "diff_lk2")):
            P.dma(lv[:, i, :], bc_rows(S[nm][l:l + 1, :], 32))
        lt = P.sb([128, 8], stack=st)
        pr = P.sb([128, 2, 32], stack=st)
        P.tt(pr[:, 0, :], lv[:, 0, :], lv[:, 1, :], ALU.mult)
        P.tt(pr[:, 1, :], lv[:, 2, :], lv[:, 3, :], ALU.mult)
        P.reduce(lt[:, 0:2], pr, ALU.add)
        P.act(lt[:, 0:2], lt[:, 0:2], AF.Exp)
        P.tt(lt[:, 2:3], lt[:, 1:2], lt[:, 0:1], ALU.subtract)
        P.ts(lt[:, 3:4], lt[:, 2:3], -lam_init, ALU.add)
        gsub = P.sb([128, 64], stack=st)
        P.dma(gsub, bc_rows(S["diff_subln_g"][l:l + 1, :], 64))
        P.ts(gsub, gsub, 1.0 - lam_init, ALU.mult)
        pS = [P.ps([128, 512], stack=st) for _ in range(2)]
        pO = [P.ps([65, 512], stack=st) for _ in range(2)]
        pT = [P.ps([128, 2, 65], stack=st) for _ in range(2)]
        E = [P.sb([128, 512], BF16, stack=st) for _ in range(3)]
        oT = [P.sb([65, 512], stack=st) for _ in range(2)]
        tm = [P.sb([128, 8], stack=st) for _ in range(2)]
        a_ = [P.sb([128, 64], stack=st) for _ in range(2)]
        w_ = [P.sb([128, 64], stack=st) for _ in range(2)]
        jk = P.sb([128, 64], stack=st)
        y_ = [P.sb([128, 64], stack=st) for _ in range(2)]
        it = 0
        ti = 0
        sc = 32 ** -0.5
        for h in range(4):
            for (uc0, tok0, n) in TOKCH:
                kts = [0, 1] if tok0 < CTX else list(range(NT))
                for c in range(2):
                    po = pO[c]
                    for ki, kt in enumerate(kts):
                        ps = pS[it % 2]
                        e_ = E[it % 3]
                        it += 1
                        P.mm(ps[:, 0:n], qkc[c][:, 4 + h, kt * 128:(kt + 1) * 128],
                             qkc[c][:, h, tok0:tok0 + n])
                        P.act(e_[:, 0:n], ps[:, 0:n], AF.Exp, scale=sc)
                        P.mm(po[:, 0:n], ve[:, kt, h, :], e_[:, 0:n], start=(ki == 0), stop=(ki == len(kts) - 1))
                    P.copy(oT[c][:, 0:n], po[:, 0:n], eng="act")
                for tt_ in range(n // 128):
                    i = ti % 2
                    ti += 1
                    pt = pT[i]
                    for c in range(2):
                        P.mm(pt[:, c, :], oT[c][:, tt_ * 128:(tt_ + 1) * 128], idf[0:65, 0:65])
                    t_ = tm[i]
                    P.recip(t_[:, 0:2], pt[:, :, 64])
                    P.tt(t_[:, 2:3], t_[:, 1:2], lt[:, 3:4], ALU.mult)
                    P.ts(a_[i], pt[:, 0, 0:64], t_[:, 0:1], ALU.mult)
                    P.stt(w_[i], pt[:, 1, 0:64], t_[:, 2:3], a_[i], ALU.mult, ALU.add)
                    P.memset(t_[:, 4:5], 0.0)
                    P.act(jk, w_[i], AF.Square, accum_out=t_[:, 4:5])
                    P.ts(t_[:, 5:6], t_[:, 4:5], 1.0 / 64, ALU.mult, 1e-5, ALU.add)
                    P.act(t_[:, 5:6], t_[:, 5:6], AF.Sqrt)
                    P.recip(t_[:, 6:7], t_[:, 5:6])
                    P.stt(y_[i], w_[i], t_[:, 6:7], gsub, ALU.mult, ALU.mult)
                    tok = tok0 + tt_ * 128
                    P.dma(S["Ydf"][tok:tok + 128, h * 64:(h + 1) * 64], y_[i], eng="sp" if i else "pool")
    P.barrier()
    tok2feat(P, S, S["Ydf"], 3)

TWO_PI = 6.283185307179586


def hy_filter(P, S, l, L, En, Tn, G):
    N2 = 2 * L
    cw = min(512, N2)
    with ExitStack() as st:
        w1 = P.sb([33, 64], stack=st)
        w2 = P.sb([64, 64], stack=st)
        w3 = P.sb([64, 1024], stack=st)
        P.dma(w1, S["hy_f_w1"][l])
        P.dma(w2, S["hy_f_w2"][l])
        P.dma(w3, S["hy_f_w3"][l])
        fr = P.sb([64, 8], stack=st)
        P.dma(fr[:, 0:1], S["hy_f_freq"][l:l + 1, :].re("o p -> p o"), allow_slow_non_contiguous=True)
        P.dma(fr[:, 1:2], S["hy_f_b1"][l:l + 1, :].re("o p -> p o"), allow_slow_non_contiguous=True)
        P.dma(fr[:, 2:3], S["hy_f_b2"][l:l + 1, :].re("o p -> p o"), allow_slow_non_contiguous=True)
        P.ts(fr[:, 3:4], fr[:, 0:1], 1.0 / TWO_PI, ALU.mult)
        P.tt(fr[:, 4:5], fr[:, 3:4], fr[:, 1:2], ALU.mult)
        P.ts(fr[:, 4:5], fr[:, 4:5], 8.0, ALU.add)
        P.tt(fr[:, 5:6], fr[:, 3:4], fr[:, 2:3], ALU.mult)
        P.ts(fr[:, 5:6], fr[:, 5:6], 8.0, ALU.add)
        negpi = P.sb([128, 1], stack=st)
        P.memset(negpi, 1.5707963267948966)
        h2 = P.sb([64, N2], stack=st)
        Et = [P.sb([33, cw], stack=st) for _ in range(2)]
        ps1 = [P.ps([64, cw], stack=st) for _ in range(2)]
        ps2 = [P.ps([64, cw], stack=st) for _ in range(2)]
        v1 = [P.sb([64, cw], stack=st) for _ in range(2)]
        h1 = [P.sb([64, cw], stack=st) for _ in range(2)]
        vi = [P.sb([64, cw], mybir.dt.int32, stack=st) for _ in range(2)]
        vf = [P.sb([64, cw], stack=st) for _ in range(2)]
        sa = [P.sb([64, cw], stack=st) for _ in range(2)]
        sb_ = [P.sb([64, cw], stack=st) for _ in range(2)]

        def sin_red(ps, s2, out, i):
            P.ts(v1[i], ps, fr[:, 3:4], ALU.mult, s2, ALU.add)
            P.copy(vi[i], v1[i])
            P.copy(vf[i], vi[i])
            P.tt(v1[i], v1[i], vf[i], ALU.subtract)
            P.act(sa[i], v1[i], AF.Sin, scale=3.141592653589793)
            P.act(sb_[i], v1[i], AF.Sin, bias=negpi[0:64, :], scale=-3.141592653589793)
            P.stt(out, sa[i], 2.0, sb_[i], ALU.mult, ALU.mult)

        for ch in range(N2 // cw):
            i = ch % 2
            sl = slice(ch * cw, (ch + 1) * cw)
            P.dma(Et[i], S[En][:, sl])
            P.mm(ps1[i], w1, Et[i])
            sin_red(ps1[i], fr[:, 4:5], h1[i], i)
            P.mm(ps2[i], w2, h1[i])
            sin_red(ps2[i], fr[:, 5:6], h2[:, sl], i)
        T128 = P.sb([128, N2], stack=st)
        P.dma(T128, bc_rows(S[Tn], N2))
        nd = P.sb([128, 2], stack=st)
        P.dma(nd, S["hyND"].re("(c p) o -> p (c o)", p=128), allow_slow_non_contiguous=True)
        fw = min(512, L)
        filt = [P.sb([128, L], stack=st) for _ in range(2)]
        fb = [P.sb([128, L], BF16, stack=st) for _ in range(2)]
        junk = P.sb([128, L], stack=st)
        asum = P.sb([128, 4], stack=st)
        psf = [P.ps([128, fw], stack=st) for _ in range(2)]
        dec = [P.sb([128, fw], stack=st) for _ in range(2)]
        it = 0
        for o in range(2):
            for chalf in range(2):
                P.memset(asum, 0.0)
                for dr in range(2):
                    q = o * 4 + dr * 2 + chalf
                    pos0 = L if dr == 0 else 0
                    for ch in range(L // fw):
                        i = it % 2
                        it += 1
                        sl = slice(pos0 + ch * fw, pos0 + (ch + 1) * fw)
                        P.mm(psf[i], w3[:, q * 128:(q + 1) * 128], h2[:, sl])
                        P.act(dec[i], T128[:, sl], AF.Exp, scale=nd[:, chalf:chalf + 1])
                        P.tt(filt[dr][:, ch * fw:(ch + 1) * fw], psf[i], dec[i], ALU.mult)
                    P.act(junk, filt[dr], AF.Abs, accum_out=asum[:, dr:dr + 1])
                P.tt(asum[:, 2:3], asum[:, 0:1], asum[:, 1:2], ALU.add)
                P.recip(asum[:, 3:4], asum[:, 2:3])
                for dr in range(2):
                    P.ts(fb[dr], filt[dr], asum[:, 3:4], ALU.mult, eng="dve" if dr else "pool")
                rows = slice(chalf * 128, (chalf + 1) * 128)
                P.dma(G[o, rows, 0:L - 1], fb[1][:, 0:L - 1], eng="sp")
                P.dma(G[o, rows, L - 1:2 * L - 1], fb[0][:, 0:L], eng="pool")
    P.barrier()


def hy_conv_seq(P, S, l, tok0, L, G, GL):
    NB = L // 128
    ND_ = 2 * NB - 1
    HKW = ND_ * 128
    Gt = G.ap.tensor
    with ExitStack() as st:
        jf = P.sb([128, 128], stack=st)
        Jb = P.sb([128, 128], BF16, stack=st)
        P.dma(jf, S["antiid"])
        P.copy(Jb, jf)
        A = P.sb([128, NB, 256], stack=st)
        B = P.sb([128, NB, 256], stack=st)
        Vb = P.sb([128, NB, 256], BF16, stack=st)
        Zr = P.sb([128, NB, 256], BF16, stack=st)
        b01 = P.sb([128, 2, 256], stack=st)
        for o in range(2):
            P.dma(b01[:, o, :], bc_rows(S["hy_bias"][l, o:o + 1, :], 256))
        src = S["HYP"][tok0:tok0 + L, :].re("(a r) c -> r a c", r=128)
        P.dma(A, src[:, :, 0:256], eng="sp")
        P.dma(B, src[:, :, 256:512], eng="pool")
        hks = [P.sb([128, HKW], BF16, stack=st) for _ in range(3)]
        psJ = [P.ps([128, 512], stack=st) for _ in range(2)]
        psY = [P.ps([128, 16, NB], stack=st) for _ in range(2)]
        tmp = [P.sb([128, NB, 16], stack=st) for _ in range(2)]
        Vf = Vb.re("p a c -> p (a c)")
        Zf = Zr.re("p a c -> p (a c)")
        ds = [0] + [d for d in range(-(NB - 1), NB) if d != 0]
        for o in range(2):
            inp = A if o == 0 else B
            for a in range(NB):
                P.copy(Vb[:, a, :], inp[:, a, :], eng="pool" if a % 2 else "dve")
            for ch in range(NB * 256 // 512):
                pj = psJ[ch % 2]
                P.mm(pj, Jb, Vf[:, ch * 512:(ch + 1) * 512])
                P.copy(Zf[:, ch * 512:(ch + 1) * 512], pj, eng="act")
            for a in range(NB):
                P.tt(inp[:, a, :], inp[:, a, :], b01[:, o, :], ALU.mult, eng="pool")
            if o == 1:
                P.dma(A, src[:, :, 512:768], eng="sp")
            oth = B if o == 0 else A
            for cg in range(16):
                py = psY[cg % 2]
                for ci in range(16):
                    c = cg * 16 + ci
                    hk = hks[c % 3]
                    hv = V(bass.AP(tensor=Gt, offset=(o * 256 + c) * GL, ap=[[1, 128], [1, HKW]]), G.key)
                    P.dma(hk, hv, eng="sp" if c % 2 else "pool")
                    for di, d in enumerate(ds):
                        a_lo, a_hi = max(0, d), min(NB - 1, NB - 1 + d)
                        P.mm(py[:, ci, a_lo:a_hi + 1], hk[:, (d + NB - 1) * 128:(d + NB) * 128],
                             Zr[:, a_lo - d:a_hi - d + 1, c], start=(di == 0), stop=(di == len(ds) - 1))
                cs = slice(cg * 16, (cg + 1) * 16)
                tp = tmp[cg % 2]
                P.tt(tp, py.re("p c a -> p a c"), inp[:, :, cs], ALU.add)
                P.tt(oth[:, :, cs], tp, oth[:, :, cs], ALU.mult, eng="pool")
        P.dma(S["Yhy"][tok0:tok0 + L, :].re("(a r) c -> r a c", r=128), A, eng="sp")
    P.barrier()


def stage_hyena(P, S, l):
    hy_filter(P, S, l, SEQ, "hyE_x", "hyT_x", S["Gx"])
    hy_filter(P, S, l, CTX, "hyE_c", "hyT_c", S["Gc"])
    hy_conv_seq(P, S, l, CTX, SEQ, S["Gx"], 2 * SEQ)
    hy_conv_seq(P, S, l, 0, CTX, S["Gc"], 2 * CTX)
    tok2feat(P, S, S["Yhy"], 0)

RCH = 16
RW_STEPS = None


def s0_bwd(t):
    return 128 - 128 * t if t < 2 else 4480 - 128 * t


def stage_rwkv_prep(P, S, l):
    with ExitStack() as st:
        idf = P.sb([128, 128], stack=st)
        jf = P.sb([128, 128], stack=st)
        P.dma(idf, S["ident"])
        P.dma(jf, S["antiid"])
        zt = P.sb([128, 8704], stack=st)
        P.memset(zt, 0.0)
        vz = S["VBD"].re("d h s c -> (d h s c)").re("(p f) -> p f", p=128)
        for i in range(8):
            P.dma(vz[:, i * 8704:(i + 1) * 8704], zt, eng="sp" if i % 2 else "pool")
        w2 = [P.sb([64, 256], stack=st) for _ in range(2)]
        g2 = [P.sb([128, 256], stack=st) for _ in range(2)]
        a2 = P.sb([64, 256], stack=st)
        for d in range(2):
            P.dma(w2[d], S["rwkv_w2"][l, d])
            P.dma(g2[d], S["rwkv_g2"][l, d])
        P.dma(a2, S["rwkv_a2"][l])
        w0 = [P.sb([128, 256], stack=st) for _ in range(2)]
        for d in range(2):
            P.dma(w0[d], bc_rows(S["rwkv_w0"][l, d:d + 1, :], 256))
        a0 = load_bc(P, S, st, "rwkv_a0", l, 256)
        kkb = load_bc(P, S, st, "rwkv_kk", l, 256)
        kab = load_bc(P, S, st, "rwkv_ka", l, 256)
        omka = P.sb([128, 256], stack=st)
        P.ts(omka, kab, -1.0, ALU.mult, 1.0, ALU.add)
        rkb = P.sb([128, 256], stack=st)
        P.dma(rkb, bc_rows(S["rwkv_rk"][l:l + 1].re("o h d -> o (h d)"), 256))
        X = [P.sb([128, 1216], stack=st) for _ in range(2)]
        th = P.sb([128, 128], stack=st)
        sg = P.sb([128, 256], stack=st)
        pT = [P.ps([128, 128], stack=st) for _ in range(2)]
        tT = [P.sb([128, 128], stack=st) for _ in range(5)]
        pP = [P.ps([128, 256], stack=st) for _ in range(2)]
        wd = [P.sb([128, 256], stack=st) for _ in range(2)]
        a_ = P.sb([128, 256], stack=st)
        gd = [P.sb([128, 256], stack=st) for _ in range(2)]
        kk = P.sb([128, 256], stack=st)
        sq = P.sb([128, 256], stack=st)
        s4 = P.sb([128, 16], stack=st)
        an = P.sb([128, 256], stack=st)
        b_ = P.sb([128, 256], stack=st)
        kp = P.sb([128, 256], stack=st)
        t1 = P.sb([128, 256], stack=st)
        bon = P.sb([128, 256], stack=st)
        pK = [P.ps([64, 4, 128], stack=st) for _ in range(2)]
        kst = [P.sb([64, 128, 4], stack=st) for _ in range(2)]
        rv = [P.sb([128, 256], stack=st) for _ in range(2)]
        it = 0
        for t in range(NT):
            x = X[t % 2]
            P.dma(x, S["RWP"][t * 128:(t + 1) * 128, :], eng="sp" if t % 2 else "pool")
            r, k, v = x[:, 0:256], x[:, 256:512], x[:, 512:768]
            P.act(th, x[:, 768:896], AF.Tanh)
            P.act(sg, x[:, 960:1216], AF.Sigmoid)
            srcs = [(th[:, 0:64], 64), (th[:, 64:128], 64), (x[:, 896:960], 64), (sg[:, 0:128], 128), (sg[:, 128:256], 128)]
            for i, (sv, m) in enumerate(srcs):
                pt = pT[i % 2]
                P.mm(pt[0:m, :], sv, idf)
                P.copy(tT[i][0:m, :], pt[0:m, :], eng="act")
            for d in range(2):
                pp = pP[d]
                P.mm(pp, tT[d][0:64, :], w2[d])
                P.tt(wd[d], pp, w0[d], ALU.add)
                P.act(wd[d], wd[d], AF.Sigmoid)
                P.act(wd[d], wd[d], AF.Exp, scale=-0.6065306597126334)
            pp = pP[0]
            P.mm(pp, tT[2][0:64, :], a2)
            P.tt(a_, pp, a0, ALU.add)
            P.act(a_, a_, AF.Sigmoid)
            for d in range(2):
                pp = pP[1 - d]
                P.mm(pp, tT[3 + d], g2[d])
                P.copy(gd[d], pp, eng="act")
                P.dma(S["GFB"][d, t * 128:(t + 1) * 128, :], gd[d], eng="sp")
            P.tt(kk, k, kkb, ALU.mult)
            P.tt(sq, kk, kk, ALU.mult, eng="pool")
            P.reduce(s4[:, 0:4], sq.re("p (h d) -> p h d", h=4), ALU.add)
            P.act(s4[:, 0:4], s4[:, 0:4], AF.Sqrt)
            P.ts(s4[:, 0:4], s4[:, 0:4], 1e-12, ALU.max)
            P.recip(s4[:, 4:8], s4[:, 0:4])
            P.ts(s4[:, 4:8], s4[:, 4:8], -1.0, ALU.mult)
            for h in range(4):
                P.ts(an[:, h * 64:(h + 1) * 64], kk[:, h * 64:(h + 1) * 64], s4[:, 4 + h:5 + h], ALU.mult)
            P.stt(b_, an, -1.0, a_, ALU.mult, ALU.mult)
            P.tt(t1, a_, kab, ALU.mult, eng="pool")
            P.tt(t1, t1, omka, ALU.add, eng="pool")
            P.tt(kp, k, t1, ALU.mult, eng="pool")
            P.tt(t1, r, kp, ALU.mult, eng="pool")
            P.tt(t1, t1, rkb, ALU.mult, eng="pool")
            P.reduce(s4[:, 8:12], t1.re("p (h d) -> p h d", h=4), ALU.add)
            for h in range(4):
                P.ts(bon[:, h * 64:(h + 1) * 64], v[:, h * 64:(h + 1) * 64], s4[:, 8 + h:9 + h], ALU.mult)
            P.dma(S["BON"][t * 128:(t + 1) * 128, :], bon, eng="sp")
            sf = t * 128
            sb_ = s0_bwd(t)
            for (src, name, dirs) in ((an, "ANT", (0, 1)), (r, "RT", (0, 1)), (wd[0], "WT", (0,)), (wd[1], "WT", (1,))):
                for d in dirs:
                    pk = pK[it % 2]
                    ks = kst[it % 2]
                    it += 1
                    for h in range(4):
                        P.mm(pk[:, h, :], src[:, h * 64:(h + 1) * 64], idf if d == 0 else jf)
                    P.copy(ks.re("k s h -> k h s"), pk, eng="act")
                    s0 = sf if d == 0 else sb_
                    P.dma(S[name][d, :, s0:s0 + 128, :], ks, eng="sp" if it % 2 else "pool")
            for (src, name) in ((b_, "BT"), (kp, "KT")):
                P.dma(S[name][0, sf:sf + 128, :], src, eng="pool")
            for h in range(4):
                P.dma(S["VBD"][0, h, sf:sf + 128, h * 64:(h + 1) * 64], v[:, h * 64:(h + 1) * 64], eng="sp")
            for i, (src, name) in enumerate(((b_, "BT"), (kp, "KT"), (v, "V"))):
                pp = pP[i % 2]
                P.mm(pp, jf, src)
                rr = rv[i % 2]
                P.copy(rr, pp, eng="act")
                if name == "V":
                    for h in range(4):
                        P.dma(S["VBD"][1, h, sb_:sb_ + 128, h * 64:(h + 1) * 64], rr[:, h * 64:(h + 1) * 64], eng="sp")
                else:
                    P.dma(S[name][1, sb_:sb_ + 128, :], rr, eng="pool")
    P.barrier()


def stage_rwkv_scan(P, S, l):
    CH = RCH
    NCH = NTOK // CH
    OB = 4
    NS = NTOK if RW_STEPS is None else RW_STEPS
    with ExitStack() as st:
        mask4 = P.sb([4, 256], stack=st)
        P.dma(mask4, S["mask8"][0:4, :])
        C = []
        for d in range(2):
            c = {"d": d}
            c["LA"] = [P.sb([64, CH, 40], stack=st) for _ in range(2)]
            c["L2"] = [P.sb([8, CH, 64], stack=st) for _ in range(2)]
            c["R2"] = [P.sb([8, CH, 256], stack=st) for _ in range(2)]
            c["Wt"] = [P.sb([64, CH, 4], stack=st) for _ in range(2)]
            c["Os"] = [P.sb([40, OB, 256], stack=st) for _ in range(2)]
            for i in range(2):
                P.memset(c["LA"][i].re("p s c -> p (s c)"), 0.0)
                P.memset(c["L2"][i].re("p s c -> p (s c)"), 0.0)
                P.memset(c["R2"][i].re("p s c -> p (s c)"), 0.0)
            c["St"] = P.sb([64, 256], stack=st)
            P.memset(c["St"], 0.0)
            c["pA"] = [P.ps([40, 512], stack=st) for _ in range(2)]
            c["pU"] = [P.ps([64, 512], stack=st) for _ in range(2)]
            C.append(c)

        def load_chunk(c, ci):
            i = ci % 2
            s0 = ci * CH
            d = c["d"]
            if ci == NCH:
                P.dma(c["LA"][i][:, 0:1, 32:36], S["RT"][d, :, s0 - 1:s0, :], eng="sp")
                return
            P.dma(c["LA"][i][:, :, 0:4], S["ANT"][d, :, s0:s0 + CH, :], eng="sp")
            if ci == 0:
                P.dma(c["LA"][i][:, 1:CH, 32:36], S["RT"][d, :, 0:CH - 1, :], eng="act")
            else:
                P.dma(c["LA"][i][:, :, 32:36], S["RT"][d, :, s0 - 1:s0 + CH - 1, :], eng="act")
            P.dma(c["Wt"][i], S["WT"][d, :, s0:s0 + CH, :], eng="sp")
            P.dma(c["L2"][i][0:4], S["BT"][d, s0:s0 + CH, :].re("s (h k) -> h s k", h=4), eng="act")
            P.dma(c["L2"][i][4:8], S["KT"][d, s0:s0 + CH, :].re("s (h k) -> h s k", h=4), eng="sp")
            P.dma(c["R2"][i][4:8], S["VBD"][d, :, s0:s0 + CH, :], eng="act")

        for c in C:
            load_chunk(c, 0)
        for s in range(NS + 1):
            ci, pos = divmod(s, CH)
            i = ci % 2
            for c in C:
                d = c["d"]
                if pos == 0 and ci + 1 <= NCH:
                    load_chunk(c, ci + 1)
                pa = c["pA"][s % 2][:, 0:256]
                P.mm(pa, c["LA"][i][:, pos, :], c["St"])
                if s >= 1:
                    ob, op_ = divmod(s - 1, OB)
                    osb = c["Os"][ob % 2]
                    P.copy(osb[32:36, op_, :], pa[32:36, :], eng="act")
                    if op_ == OB - 1:
                        P.dma(S["OD"][d, :, ob * OB:(ob + 1) * OB, :], osb[32:36, :, :], eng="sp")
                if s == NS:
                    continue
                P.tt(c["R2"][i][0:4, pos, :], pa[0:4, :], mask4, ALU.mult)
                pu = c["pU"][s % 2][:, 0:256]
                P.mm(pu, c["L2"][i][:, pos, :], c["R2"][i][:, pos, :])
                S3 = c["St"].re("p (h v) -> p h v", h=4)
                P.tt(S3, S3, V(c["Wt"][i].ap[:, pos, :].unsqueeze(2).to_broadcast([64, 4, 64]), c["Wt"][i].key), ALU.mult, eng="pool")
                P.tt(c["St"], c["St"], pu, ALU.add)
    P.barrier()


def stage_rwkv_out(P, S, l):
    with ExitStack() as st:
        jf = P.sb([128, 128], stack=st)
        P.dma(jf, S["antiid"])
        gam = load_bc(P, S, st, "rwkv_lnx_g", l, 256)
        bet = load_bc(P, S, st, "rwkv_lnx_b", l, 256)
        o_ = [[P.sb([128, 256], stack=st) for _ in range(2)] for _ in range(2)]
        gfb = [[P.sb([128, 256], stack=st) for _ in range(2)] for _ in range(2)]
        bon = [P.sb([128, 256], stack=st) for _ in range(2)]
        orev = [P.sb([128, 256], stack=st) for _ in range(2)]
        pj = [P.ps([128, 256], stack=st) for _ in range(2)]
        sq = P.sb([128, 256], stack=st)
        s4 = [P.sb([128, 16], stack=st) for _ in range(2)]
        gn = [P.sb([128, 256], stack=st) for _ in range(2)]
        y = [P.sb([128, 256], stack=st) for _ in range(2)]
        for t in range(NT):
            i = t % 2
            sf, sb_ = t * 128, s0_bwd(t)
            for d in range(2):
                s0 = sf if d == 0 else sb_
                for h in range(4):
                    P.dma(o_[i][d][:, h * 64:(h + 1) * 64], S["OD"][d, h, s0:s0 + 128, h * 64:(h + 1) * 64],
                          eng="sp" if h % 2 else "pool")
                P.dma(gfb[i][d], S["GFB"][d, sf:sf + 128, :], eng="sp")
            P.dma(bon[i], S["BON"][sf:sf + 128, :], eng="pool")
            for d in range(2):
                if d == 1:
                    P.mm(pj[i], jf, o_[i][1])
                    P.copy(orev[i], pj[i], eng="act")
                    od = orev[i]
                else:
                    od = o_[i][0]
                s_ = s4[d]
                o3 = od.re("p (h d) -> p h d", h=4)
                P.reduce(s_[:, 0:4], o3, ALU.add)
                P.tt(sq, od, od, ALU.mult)
                P.reduce(s_[:, 4:8], sq.re("p (h d) -> p h d", h=4), ALU.add)
                P.ts(s_[:, 0:4], s_[:, 0:4], 1.0 / 64, ALU.mult)
                P.tt(s_[:, 8:12], s_[:, 0:4], s_[:, 0:4], ALU.mult)
                P.stt(s_[:, 4:8], s_[:, 4:8], 1.0 / 64, s_[:, 8:12], ALU.mult, ALU.subtract)
                P.ts(s_[:, 4:8], s_[:, 4:8], 64e-5, ALU.add)
                P.act(s_[:, 4:8], s_[:, 4:8], AF.Sqrt)
                P.recip(s_[:, 8:12], s_[:, 4:8])
                g_ = gn[d]
                for h in range(4):
                    P.ts(g_[:, h * 64:(h + 1) * 64], od[:, h * 64:(h + 1) * 64], s_[:, h:h + 1], ALU.subtract,
                         s_[:, 8 + h:9 + h], ALU.mult)
                P.tt(g_, g_, gam, ALU.mult, eng="pool")
                P.tt(g_, g_, bet, ALU.add, eng="pool")
                P.tt(g_, g_, bon[i], ALU.add, eng="pool")
                P.tt(g_, g_, gfb[i][d], ALU.mult, eng="pool")
            P.tt(y[i], gn[0], gn[1], ALU.add, eng="pool")
            P.dma(S["Yrw"][sf:sf + 128, :], y[i], eng="sp")
    P.barrier()
    tok2feat(P, S, S["Yrw"], 2)


def stage_rwkv(P, S, l):
    stage_rwkv_prep(P, S, l)
    stage_rwkv_scan(P, S, l)
    stage_rwkv_out(P, S, l)
SCRATCH = {
    "MODD": ([1, 2 * 6144], "f32"),
    "HYP": ([NTOK, 768], "f32"),
    "RWP": ([NTOK, 1216], "f32"),
    "VSW": ([NTOK, 128], "f32"),
    "VDF": ([NTOK, 256], "f32"),
    "QK": ([14, 64, NTOK], "bf16"),
    "GT": ([4096, NTOK], "f32"),
    "YT": ([4, 256, NTOK], "bf16"),
    "H1": ([NTOK, D], "f32"),
    "H2": ([NTOK, D], "f32"),
    "GFT": ([2816, UTW], "bf16"),
    "Gx": ([2, 256, 2 * SEQ], "bf16"),
    "Gc": ([2, 256, 2 * CTX], "bf16"),
    "Yhy": ([NTOK, 256], "f32"),
    "ANT": ([2, 64, NTOK, 4], "f32"), "RT": ([2, 64, NTOK, 4], "f32"), "WT": ([2, 64, NTOK, 4], "f32"),
    "BT": ([2, NTOK, 256], "f32"), "KT": ([2, NTOK, 256], "f32"), "VBD": ([2, 4, NTOK, 256], "f32"),
    "OD": ([2, 4, NTOK, 256], "f32"), "GFB": ([2, NTOK, 256], "f32"), "BON": ([NTOK, 256], "f32"),
    "Ysw": ([NTOK, 256], "f32"),
    "Yrw": ([NTOK, 256], "f32"),
    "Ydf": ([NTOK, 256], "f32"),
}

WEIGHTS = {
    "ada_w": [2, 1024, 6144], "ada_b": [2, 6144], "w_in": [2, 1024, 7360], "w_swap": [2, 1024, 896],
    "hy_conv_w": [2, 3, 768], "hy_conv_b": [2, 768], "hy_f_w1": [2, 33, 64], "hy_f_b1": [2, 64],
    "hy_f_w2": [2, 64, 64], "hy_f_b2": [2, 64], "hy_f_w3": [2, 64, 1024], "hy_f_freq": [2, 64],
    "hy_bias": [2, 2, 256], "swa_sink": [2, 4], "rwkv_mu": [2, 1216], "rwkv_w0": [2, 2, 256],
    "rwkv_w2": [2, 2, 64, 256], "rwkv_a0": [2, 256], "rwkv_a2": [2, 64, 256], "rwkv_g2": [2, 2, 128, 256],
    "rwkv_kk": [2, 256], "rwkv_ka": [2, 256], "rwkv_rk": [2, 4, 64], "rwkv_lnx_g": [2, 256],
    "rwkv_lnx_b": [2, 256], "diff_lq1": [2, 32], "diff_lk1": [2, 32], "diff_lq2": [2, 32], "diff_lk2": [2, 32],
    "diff_subln_g": [2, 64], "w_branch": [2, 4, 256, 1024], "w_out": [2, 1024, 1024], "ln1_g": [2, 1024],
    "ln1_b": [2, 1024], "ffn_w_up": [2, 1024, 5632], "ffn_conv_w": [2, 3, 5632], "ffn_conv_b": [2, 5632],
    "ffn_w_down": [2, 2816, 1024], "ln2_g": [2, 1024], "ln2_b": [2, 1024],
}
CONSTS = {"ident": [128, 128], "antiid": [128, 128], "rope": [4, 64, NTOK], "maskLR": [2, 128, 512], "mask8": [8, 256],
          "hyE_x": [33, 2 * SEQ], "hyE_c": [33, 2 * CTX], "hyT_x": [1, 2 * SEQ], "hyT_c": [1, 2 * CTX], "hyND": [256, 1]}


def build(nc, dbg=(), stages=None, nlayers=2, ext_in=()):
    P = Prog(nc)
    S = {}
    S["xin"] = P.dram("xin", [NTOK, D], F32, kind="ExternalInput")
    S["cvec"] = P.dram("cvec", [2, D], F32, kind="ExternalInput")
    for k, shp in {**WEIGHTS, **CONSTS}.items():
        S[k] = P.dram(k, shp, F32, kind="ExternalInput")
    for k, (shp, dt) in SCRATCH.items():
        S[k] = P.dram(k, shp, F32 if dt == "f32" else BF16, kind="ExternalOutput" if k in dbg else ("ExternalInput" if k in ext_in else "Internal"))
    S["out"] = P.dram("out", [SEQ, D], F32, kind="ExternalOutput")
    idf = P.sb([128, 128])
    S["identb"] = P.sb([128, 128], BF16)
    P.dma(idf, S["ident"])
    P.copy(S["identb"], idf)
    for l in range(nlayers):
        src = S["xin"] if l == 0 else S["H2"]
        if stages is None or "mod" in stages:
            stage_mod(P, S, l)
        if stages is None or "A" in stages:
            with ExitStack() as st:
                uT = P.sb([128, 8, UTW], BF16, stack=st)
                stage_lnmod(P, S, src, 0, 1024, uT)
                stage_inproj(P, S, l, uT)
            P.barrier()
        lam_init = 0.8 - 0.6 * float(np.exp(-0.3 * l))
        if stages is None or "hyena" in stages:
            stage_hyena(P, S, l)
        if stages is None or "rwkv" in stages:
            stage_rwkv(P, S, l)
        if stages is None or "swa" in stages:
            stage_swa(P, S, l)
        if stages is None or "diff" in stages:
            stage_diff(P, S, l, lam_init)
        if stages is None or "merge" in stages:
            stage_merge(P, S, l, src)
        if stages is None or "ffn" in stages:
            stage_ffn(P, S, l, (l == nlayers - 1) and ("H2" not in dbg))
    P.finalize()
    return P


def rope_tables():
    pos = np.arange(SEQ)
    row = (pos // 64).astype(np.float32)
    col = (pos % 64).astype(np.float32)
    out = np.zeros((4, 64, NTOK), np.float32)
    out[0, :, :CTX] = 1.0
    out[2, :, :CTX] = 1.0
    for ti, d in ((0, 64), (2, 32)):
        nf = d // 4
        inv = (10000.0 ** (-np.arange(nf, dtype=np.float32) / nf)).astype(np.float32)
        for rep in range(64 // d):
            for half, p in enumerate((row, col)):
                ang = (p[None, :] * inv[:, None]).astype(np.float32)
                c, s = np.cos(ang), np.sin(ang)
                b = rep * d + half * 2 * nf
                out[ti, b:b + nf, CTX:] = c
                out[ti, b + nf:b + 2 * nf, CTX:] = c
                out[ti + 1, b:b + nf, CTX:] = -s
                out[ti + 1, b + nf:b + 2 * nf, CTX:] = s
    return out


def hy_consts(L):
    pos = np.concatenate([np.arange(L - 1, -1, -1), np.arange(L)]).astype(np.float64)
    t = np.linspace(0.0, 1.0, L, dtype=np.float32)[pos.astype(np.int64)]
    w = (2.0 * np.pi * pos / L)
    fbv = np.linspace(1e-4, 15, 16, dtype=np.float32).astype(np.float64)
    ang = w[None, :] * fbv[:, None]
    E = np.concatenate([t[None, :].astype(np.float64), np.cos(ang), -np.sin(ang)], 0).astype(np.float32)
    return np.ascontiguousarray(E), np.ascontiguousarray(t[None, :].astype(np.float32))


def swap_cols():
    idx = []
    for (c0, n, d) in ((SWQ0, 256, 64), (SWK0, 128, 64), (DFQ0, 256, 32), (DFK0, 256, 32)):
        nf = d // 4
        for j in range(n):
            i = j % (2 * nf)
            idx.append(c0 + (j + nf if i < nf else j - nf))
    return np.array(idx)


def make_in_maps(inputs, cores):
    ins = {k: np.ascontiguousarray(np.asarray(v, dtype=np.float32)) for k, v in inputs.items()}
    common = {k: ins[k] for k in WEIGHTS if k != "w_swap"}
    common["w_swap"] = np.ascontiguousarray(ins["w_in"][:, :, swap_cols()])
    common["ident"] = np.eye(128, dtype=np.float32)
    common["antiid"] = np.ascontiguousarray(np.eye(128, dtype=np.float32)[::-1])
    common["rope"] = rope_tables()
    common["hyE_x"], common["hyT_x"] = hy_consts(SEQ)
    common["hyE_c"], common["hyT_c"] = hy_consts(CTX)
    lo, hi = np.log(1e-2) / 1.5, np.log(1e-2) / 0.3
    common["hyND"] = np.ascontiguousarray(-np.abs(np.linspace(lo, hi, 256, dtype=np.float32))[:, None])
    common["mask8"] = np.ascontiguousarray((np.arange(8)[:, None] % 4 == np.arange(256)[None, :] // 64).astype(np.float32))
    rr = np.arange(128)
    mL = (rr[:, None] >= rr[None, :]).astype(np.float32)
    mR = (rr[:, None] <= rr[None, :]).astype(np.float32)
    common["maskLR"] = np.ascontiguousarray(np.stack([np.tile(mL, (1, 4)), np.tile(mR, (1, 4))], 0))
    maps = []
    for b in cores:
        m = dict(common)
        m["xin"] = np.ascontiguousarray(np.concatenate([ins["ctx"][b], ins["x"][b]], 0))
        m["cvec"] = np.ascontiguousarray(np.stack([ins["c"][b], ins["c_ctx"]], 0))
        maps.append(m)
    return maps


def kernel(**inputs):
    nc = bass.Bass("TRN2", target_bir_lowering=False)
    build(nc)
    maps = make_in_maps(inputs, list(range(8)))
    res = run_bass_kernel_spmd(nc, maps, core_ids=list(range(8)))
    return np.stack([r["out"] for r in res.results], 0).astype(np.float32)
```

```python
import numpy as np
import concourse.bass as bass
import concourse.mybir as mybir
from contextlib import ExitStack

F32 = mybir.dt.float32
BF16 = mybir.dt.bfloat16
AF = mybir.ActivationFunctionType
ALU = mybir.AluOpType
AX = mybir.AxisListType

EPOCH = 30000
NDMA = 8


class V:
    __slots__ = ("ap", "key")

    def __init__(self, ap, key):
        self.ap = ap
        self.key = key

    def __getitem__(self, idx):
        return V(self.ap[idx], self.key)

    def k(self, sub):
        base = self.key[0] if isinstance(self.key, tuple) else self.key
        return V(self.ap, (base, sub))

    def re(self, s, **kw):
        return V(self.ap.rearrange(s, **kw), self.key)

    def bc(self, shape):
        return V(self.ap.to_broadcast(list(shape)), self.key)

    def bitcast(self, dt):
        return V(self.ap.bitcast(dt), self.key)


def _ap(x):
    return x.ap if isinstance(x, V) else x


class Prog:
    ENG = ("pe", "act", "dve", "pool", "sp")

    def __init__(self, nc):
        self.nc = nc
        self.es = ExitStack()
        self.ops = {e: [] for e in self.ENG}
        self.cnt = {e: 0 for e in self.ENG}
        self.sems = {e: [] for e in self.ENG}
        self.waited = {}
        self.last_w = {}
        self.readers = {}
        self.dma_cnt = {e: 0 for e in self.ENG}
        self.dma_sems = {e: None for e in self.ENG}
        self.nbuf = 0
        self.out_tokens = []

    def sb(self, shape, dt=F32, name=None, stack=None):
        self.nbuf += 1
        name = name or f"sb{self.nbuf}"
        h = (stack or self.es).enter_context(self.nc.sbuf_tensor(f"{name}_{self.nbuf}", list(shape), dt))
        return V(h[:] if hasattr(h, "__getitem__") else h.ap(), f"{name}_{self.nbuf}")

    def ps(self, shape, dt=F32, name=None, stack=None):
        self.nbuf += 1
        name = name or f"ps{self.nbuf}"
        h = (stack or self.es).enter_context(self.nc.psum_tensor(f"{name}_{self.nbuf}", list(shape), dt))
        return V(h[:] if hasattr(h, "__getitem__") else h.ap(), f"{name}_{self.nbuf}")

    def dram(self, name, shape, dt=F32, kind="Internal"):
        h = self.nc.dram_tensor(name, list(shape), dt, kind=kind)
        return V(h.ap(), name)

    def _sem_for(self, eng, idx):
        ep = idx // EPOCH
        while len(self.sems[eng]) <= ep:
            s = self.es.enter_context(self.nc.semaphore(f"s_{eng}_{len(self.sems[eng])}"))
            self.sems[eng].append(s)
        return self.sems[eng][ep], (idx % EPOCH) + 1

    def _wait(self, eng, tok):
        if tok is None:
            return
        if tok[0] == "e":
            _, src, idx = tok
            if src == eng and eng == "pe":
                return
            if self.waited.get((eng, src), -1) >= idx:
                return
            self.waited[(eng, src)] = idx
            sem, val = self._sem_for(src, idx)
            self.ops[eng].append(("w", sem, val))
        else:
            _, sem, val, sid = tok
            if self.waited.get((eng, sid), -1) >= val:
                return
            self.waited[(eng, sid)] = val
            self.ops[eng].append(("w", sem, val))

    def _deps(self, eng, reads, writes):
        for k in reads:
            self._wait(eng, self.last_w.get(k))
        for k in writes:
            self._wait(eng, self.last_w.get(k))
            for t in self.readers.get(k, ()):
                self._wait(eng, t)

    def _commit(self, tok, reads, writes):
        for k in reads:
            self.readers.setdefault(k, []).append(tok)
        for k in writes:
            self.last_w[k] = tok
            self.readers[k] = []

    def op(self, eng, fn, outs, ins):
        reads = [x.key for x in ins if isinstance(x, V)]
        writes = [x.key for x in outs if isinstance(x, V)]
        self._deps(eng, reads, writes)
        idx = self.cnt[eng]
        self.cnt[eng] += 1
        sem, _ = self._sem_for(eng, idx)
        self.ops[eng].append(("i", fn, sem, 1))
        tok = ("e", eng, idx)
        self._commit(tok, reads, writes)
        return tok

    def dma(self, out, in_, eng="sp", **kw):
        reads = [in_.key]
        writes = [out.key]
        self._deps(eng, reads, writes)
        if self.dma_sems[eng] is None:
            self.dma_sems[eng] = [self.es.enter_context(self.nc.semaphore(f"d_{eng}_{i}")) for i in range(NDMA)]
        j = self.dma_cnt[eng]
        self.dma_cnt[eng] += 1
        s = j % NDMA
        sem = self.dma_sems[eng][s]
        sid = f"d_{eng}_{s}"
        if j >= NDMA:
            self._wait(eng, ("d", sem, 16 * (j // NDMA), sid))
        o, i = out.ap, in_.ap
        self.ops[eng].append(("i", lambda e: e.dma_start(out=o, in_=i, **kw), sem, 16))
        tok = ("d", sem, 16 * (j // NDMA + 1), sid)
        self._commit(tok, reads, writes)
        return tok

    def mm(self, out, lhsT, rhs, start=True, stop=True):
        o, l, r = out.ap, lhsT.ap, rhs.ap
        return self.op("pe", lambda e: e.matmul(o, l, r, start=start, stop=stop), [out], [lhsT, rhs])

    def transpose(self, out, in_, ident):
        o, i, d = out.ap, in_.ap, ident.ap
        return self.op("pe", lambda e: e.transpose(o, i, d), [out], [in_, ident])

    def act(self, out, in_, func, bias=0.0, scale=1.0, accum_out=None):
        o, i, b, s = out.ap, in_.ap, _ap(bias), _ap(scale)
        if accum_out is not None:
            a = accum_out.ap
            fn = lambda e: e.activation(o, i, func, bias=b, scale=s, accum_out=a)
            outs = [out, accum_out]
        else:
            fn = lambda e: e.activation(o, i, func, bias=b, scale=s)
            outs = [out]
        return self.op("act", fn, outs, [in_, bias, scale])

    def tt(self, out, a, b, op, eng="dve"):
        o, x, y = out.ap, a.ap, b.ap
        return self.op(eng, lambda e: e.tensor_tensor(o, x, y, op), [out], [a, b])

    def ts(self, out, a, s1, op0, s2=None, op1=None, eng="dve", accum_out=None):
        o, x, p, q = out.ap, a.ap, _ap(s1), _ap(s2)
        kw = {}
        outs = [out]
        if op1 is not None:
            kw["op1"] = op1
        if accum_out is not None:
            kw["accum_out"] = accum_out.ap
            outs.append(accum_out)
        return self.op(eng, lambda e: e.tensor_scalar(o, x, p, q, op0, **kw), outs, [a, s1, s2])

    def stt(self, out, a, scalar, b, op0, op1, eng="dve"):
        o, x, s, y = out.ap, a.ap, _ap(scalar), b.ap
        return self.op("dve", lambda e: e.scalar_tensor_tensor(o, x, s, y, op0, op1), [out], [a, scalar, b])

    def copy(self, out, in_, eng="dve"):
        o, i = out.ap, in_.ap
        if eng == "act":
            return self.op("act", lambda e: e.copy(o, i), [out], [in_])
        return self.op(eng, lambda e: e.tensor_copy(o, i), [out], [in_])

    def reduce(self, out, in_, op, axis=AX.X, eng="dve", abs_=None):
        o, i = out.ap, in_.ap
        kw = {}
        if abs_:
            kw["apply_absolute_value"] = True
        return self.op(eng, lambda e: e.tensor_reduce(o, i, axis, op, **kw), [out], [in_])

    def recip(self, out, in_):
        o, i = out.ap, in_.ap
        return self.op("dve", lambda e: e.reciprocal(o, i), [out], [in_])

    def memset(self, out, val, eng="dve"):
        o = out.ap
        return self.op(eng, lambda e: e.memset(o, val), [out], [])

    def iota(self, out, pattern, base=0, channel_multiplier=0):
        o = out.ap
        return self.op("pool", lambda e: e.iota(o, pattern, base=base, channel_multiplier=channel_multiplier,
                                                allow_small_or_imprecise_dtypes=True), [out], [])

    def barrier(self):
        toks = [("e", e, self.cnt[e] - 1) for e in self.ENG if self.cnt[e] > 0]
        for e in self.ENG:
            for j in range(min(self.dma_cnt[e], NDMA)):
                jj = self.dma_cnt[e] - 1 - j
                s = jj % NDMA
                toks.append(("d", self.dma_sems[e][s], 16 * (jj // NDMA + 1), f"d_{e}_{s}"))
        for e in self.ENG:
            for t in toks:
                self._wait(e, t)
        self.last_w = {}
        self.readers = {}

    def finalize(self):
        self.barrier()
        nc = self.nc
        ops = self.ops
        with nc.Block() as block:
            def mk(name):
                def body(e):
                    for it in ops[name]:
                        if it[0] == "w":
                            e.wait_ge(it[1], it[2])
                        else:
                            it[1](e).then_inc(it[2], it[3])
                return body
            block.tensor(mk("pe"))
            block.scalar(mk("act"))
            block.vector(mk("dve"))
            block.gpsimd(mk("pool"))
            block.sync(mk("sp"))
        self.es.close()
from concourse.bass_utils import run_bass_kernel_spmd
D = 1024
SEQ = 4096
CTX = 256
NTOK = SEQ + CTX
NT = NTOK // 128
INC = 7360
UTW = NTOK + 3
LN_EPS = 1e-6
HY0 = 0
SWQ0, SWK0, SWV0 = 768, 1024, 1152
RW0 = 1280
DFQ0, DFK0, DFV0 = 2496, 2752, 3008
GT0 = 3264
ALPHA = 4 ** 0.25


def ucol(tok):
    return tok + 1 if tok < CTX else tok + 2


def bc_rows(v, n):
    return V(v.ap.to_broadcast([128, n]), v.key)


def stage_mod(P, S, l):
    with ExitStack() as st:
        cs = P.sb([128, 2, 8], stack=st)
        P.dma(cs, S["cvec"].re("w (kc p) -> p w kc", p=128), allow_slow_non_contiguous=True)
        P.act(cs, cs, AF.Silu)
        brow = P.sb([1, 6144], stack=st)
        P.dma(brow, S["ada_b"][l:l + 1, :])
        res = P.sb([1, 2, 6144], stack=st)
        wts = [P.sb([128, 8, 512], stack=st) for _ in range(2)]
        pss = [P.ps([1, 512], stack=st) for _ in range(2)]
        for nb in range(12):
            wt = wts[nb % 2]
            P.dma(wt, S["ada_w"][l, :, nb * 512:(nb + 1) * 512].re("(kc p) n -> p kc n", p=128),
                  eng="sp" if nb % 2 == 0 else "pool")
            for w in range(2):
                ps = pss[w]
                for kc in range(8):
                    P.mm(ps, cs[:, w, kc:kc + 1], wt[:, kc, :], start=(kc == 0), stop=(kc == 7))
                P.tt(res[:, w, nb * 512:(nb + 1) * 512], ps, brow[:, nb * 512:(nb + 1) * 512], ALU.add)
        for w in range(2):
            for off in (1024, 4096):
                P.ts(res[:, w, off:off + 1024], res[:, w, off:off + 1024], 1.0, ALU.add)
        P.dma(S["MODD"], res.re("o w n -> o (w n)"))
    P.barrier()


def ln_stats(P, xt, junk, stt_):
    P.memset(stt_, 0.0)
    P.act(junk, xt, AF.Identity, accum_out=stt_[:, 0:1])
    P.act(junk, xt, AF.Square, accum_out=stt_[:, 1:2])
    P.ts(stt_[:, 0:1], stt_[:, 0:1], 1.0 / D, ALU.mult)
    P.tt(stt_[:, 2:3], stt_[:, 0:1], stt_[:, 0:1], ALU.mult)
    P.stt(stt_[:, 2:3], stt_[:, 1:2], 1.0 / D, stt_[:, 2:3], ALU.mult, ALU.subtract)
    P.ts(stt_[:, 2:3], stt_[:, 2:3], LN_EPS, ALU.add)
    P.act(stt_[:, 2:3], stt_[:, 2:3], AF.Sqrt)
    P.recip(stt_[:, 3:4], stt_[:, 2:3])
    P.stt(stt_[:, 4:5], stt_[:, 0:1], -1.0, stt_[:, 3:4], ALU.mult, ALU.mult)


def stage_lnmod(P, S, src, shoff, scoff, uT):
    with ExitStack() as st:
        _stage_lnmod(P, S, src, shoff, scoff, uT, st)
    P.barrier()


def _stage_lnmod(P, S, src, shoff, scoff, uT, st):
    modb = {}
    for w in range(2):
        sh = P.sb([128, D], stack=st)
        sc = P.sb([128, D], stack=st)
        P.dma(sh, bc_rows(S["MODD"][:, w * 6144 + shoff: w * 6144 + shoff + D], D))
        P.dma(sc, bc_rows(S["MODD"][:, w * 6144 + scoff: w * 6144 + scoff + D], D))
        modb[w] = (sh, sc)
    xts = [P.sb([128, D], stack=st) for _ in range(2)]
    junk = P.sb([128, D], stack=st)
    xn = P.sb([128, D], stack=st)
    ub = P.sb([128, D], BF16, stack=st)
    sts = [P.sb([128, 8], stack=st) for _ in range(2)]
    pts = [P.ps([128, 8, 128], BF16, stack=st) for _ in range(2)]
    P.memset(uT[:, :, 0:1], 0.0)
    P.memset(uT[:, :, CTX + 1:CTX + 2], 0.0)
    P.memset(uT[:, :, UTW - 1:UTW], 0.0)
    for t in range(NT):
        xt = xts[t % 2]
        s_ = sts[t % 2]
        P.dma(xt, src[t * 128:(t + 1) * 128, :], eng="sp" if t % 2 == 0 else "pool")
        ln_stats(P, xt, junk, s_)
        P.act(xn, xt, AF.Identity, bias=s_[:, 4:5], scale=s_[:, 3:4])
        sh, sc = modb[1 if t < 2 else 0]
        P.tt(xn, xn, sc, ALU.mult)
        P.tt(ub, xn, sh, ALU.add)
        pt = pts[t % 2]
        for kc in range(8):
            P.transpose(pt[:, kc, :], ub[:, kc * 128:(kc + 1) * 128], S["identb"])
        c0 = ucol(t * 128)
        P.copy(uT[:, :, c0:c0 + 128], pt, eng="act")


TOKCH = [(1, 0, 256)] + [(258 + i * 512, 256 + i * 512, 512) for i in range(8)]


def wview(S, name, l, c0, n):
    return S[name][l, :, c0:c0 + n].re("(kc p) n -> p kc n", p=128)


def stage_inproj(P, S, l, uT):
    W = "w_in"
    with ExitStack() as st:
        wst = [P.sb([128, 8, 384], stack=st) for _ in range(2)]
        wb = [[P.sb([128, 8, 384], BF16, stack=st) for _ in range(3)] for _ in range(2)]
        tap = [P.sb([128, 3, 384], stack=st) for _ in range(2)]
        bia = [P.sb([128, 384], stack=st) for _ in range(2)]
        pss = [P.ps([128, 384], stack=st) for _ in range(3)]
        osb = [P.sb([128, 384], stack=st) for _ in range(3)]
        blocks = []
        for c in range(0, 768, 384):
            blocks.append(("hy", HY0 + c, c, 384))
        for c, n in ((0, 384), (384, 384), (768, 384), (1152, 64)):
            blocks.append(("rw", RW0 + c, c, n))
        for c, n in ((0, 128),):
            blocks.append(("vsw", SWV0, 0, 128))
        blocks.append(("vdf", DFV0, 0, 256))
        it = 0
        for bi, (kind, wc0, oc0, n) in enumerate(blocks):
            ws = wst[bi % 2]
            P.dma(ws[:, :, 0:n], wview(S, W, l, wc0, n), eng="sp" if bi % 2 == 0 else "pool")
            wbb = wb[bi % 2]
            tp = tap[bi % 2]
            if kind == "hy":
                for j in range(3):
                    P.dma(tp[:, j, 0:n], bc_rows(S["hy_conv_w"][l, j:j + 1, oc0:oc0 + n], n))
                P.dma(bia[bi % 2][:, 0:n], bc_rows(S["hy_conv_b"][l:l + 1, oc0:oc0 + n], n))
                for j in range(3):
                    for kc in range(8):
                        P.tt(wbb[j][:, kc, 0:n], ws[:, kc, 0:n], tp[:, j, 0:n], ALU.mult, eng="pool" if kc % 2 else "dve")
                nj = 3
                dst = S["HYP"]
            elif kind == "rw":
                P.dma(tp[:, 0, 0:n], bc_rows(S["rwkv_mu"][l:l + 1, oc0:oc0 + n], n))
                P.ts(tp[:, 1, 0:n], tp[:, 0, 0:n], -1.0, ALU.mult, 1.0, ALU.add)
                P.ts(tp[:, 2, 0:n], tp[:, 0, 0:n], 0.5, ALU.mult)
                for kc in range(8):
                    P.tt(wbb[1][:, kc, 0:n], ws[:, kc, 0:n], tp[:, 1, 0:n], ALU.mult, eng="pool" if kc % 2 else "dve")
                    P.tt(wbb[0][:, kc, 0:n], ws[:, kc, 0:n], tp[:, 2, 0:n], ALU.mult, eng="pool" if kc % 2 else "dve")
                nj = 3
                dst = S["RWP"]
            else:
                for kc in range(8):
                    P.copy(wbb[1][:, kc, 0:n], ws[:, kc, 0:n], eng="pool" if kc % 2 else "dve")
                nj = 1
                dst = S["VSW"] if kind == "vsw" else S["VDF"]
            for t in range(NT):
                ps = pss[it % 3]
                ob = osb[it % 3]
                it += 1
                c0 = ucol(t * 128)
                if nj == 3:
                    seq = [(0, -1), (1, 0), (2 if kind == "hy" else 0, 1)]
                else:
                    seq = [(1, 0)]
                k = 0
                tot = len(seq) * 8
                for (wj, sh) in seq:
                    for kc in range(8):
                        P.mm(ps[:, 0:n], uT[:, kc, c0 + sh:c0 + sh + 128], wbb[wj][:, kc, 0:n],
                             start=(k == 0), stop=(k == tot - 1))
                        k += 1
                if kind == "hy":
                    P.tt(ob[:, 0:n], ps[:, 0:n], bia[bi % 2][:, 0:n], ALU.add)
                else:
                    P.copy(ob[:, 0:n], ps[:, 0:n], eng="act")
                P.dma(dst[t * 128:(t + 1) * 128, oc0:oc0 + n], ob[:, 0:n], eng="sp" if it % 2 else "pool")
    P.barrier()
    with ExitStack() as st:
        NU = 14
        ws = P.sb([128, 8, 896], stack=st)
        wq = P.sb([128, 8, 896], BF16, stack=st)
        wsw = P.sb([128, 8, 896], BF16, stack=st)
        srcs = [(SWQ0, 256, 0), (SWK0, 128, 256), (DFQ0, 256, 384), (DFK0, 256, 640)]
        for (c0, n, o) in srcs:
            P.dma(ws[:, :, o:o + n], wview(S, W, l, c0, n))
        for kc in range(8):
            P.copy(wq[:, kc, :], ws[:, kc, :], eng="pool" if kc % 2 else "dve")
        P.dma(ws, wview(S, "w_swap", l, 0, 896), eng="pool")
        for kc in range(8):
            P.copy(wsw[:, kc, :], ws[:, kc, :], eng="pool" if kc % 2 else "dve")
        rts = [P.sb([64, 4, 512], stack=st) for _ in range(2)]
        psa = [P.ps([64, 512], stack=st) for _ in range(2)]
        psb = [P.ps([64, 512], stack=st) for _ in range(2)]
        t1 = [P.sb([64, 512], stack=st) for _ in range(2)]
        t2 = [P.sb([64, 512], stack=st) for _ in range(2)]
        ob = [P.sb([64, 512], BF16, stack=st) for _ in range(2)]
        it = 0
        for ci, (uc0, tok0, n) in enumerate(TOKCH):
            rt = rts[ci % 2]
            P.dma(rt[:, :, 0:n], S["rope"][:, :, tok0:tok0 + n].re("f d t -> d f t"))
            for u in range(NU):
                pa, pb = psa[it % 2], psb[it % 2]
                a1, a2, o_ = t1[it % 2], t2[it % 2], ob[it % 2]
                it += 1
                for kc in range(8):
                    P.mm(pa[:, 0:n], wq[:, kc, u * 64:(u + 1) * 64], uT[:, kc, uc0:uc0 + n], start=(kc == 0), stop=(kc == 7))
                for kc in range(8):
                    P.mm(pb[:, 0:n], wsw[:, kc, u * 64:(u + 1) * 64], uT[:, kc, uc0:uc0 + n], start=(kc == 0), stop=(kc == 7))
                tb = 0 if u < 6 else 2
                P.tt(a1[:, 0:n], pa[:, 0:n], rt[:, tb, 0:n], ALU.mult)
                P.tt(a2[:, 0:n], pb[:, 0:n], rt[:, tb + 1, 0:n], ALU.mult, eng="dve")
                P.tt(o_[:, 0:n], a1[:, 0:n], a2[:, 0:n], ALU.add, eng="pool")
                P.dma(S["QK"][u, :, tok0:tok0 + n], o_[:, 0:n], eng="sp" if it % 2 else "pool")
    P.barrier()
    with ExitStack() as st:
        wst = [P.sb([128, 8, 512], stack=st) for _ in range(2)]
        wbs = [P.sb([128, 8, 512], BF16, stack=st) for _ in range(2)]
        pss = [P.ps([128, 512], stack=st) for _ in range(3)]
        osb = [P.sb([128, 512], stack=st) for _ in range(3)]
        it = 0
        for bi in range(8):
            ws = wst[bi % 2]
            wb_ = wbs[bi % 2]
            P.dma(ws, wview(S, W, l, GT0 + bi * 512, 512), eng="sp" if bi % 2 == 0 else "pool")
            for kc in range(8):
                P.copy(wb_[:, kc, :], ws[:, kc, :], eng="pool" if kc % 2 else "dve")
            for cc in range(4):
                for (uc0, tok0, n) in TOKCH:
                    ps = pss[it % 3]
                    o_ = osb[it % 3]
                    it += 1
                    for kc in range(8):
                        P.mm(ps[:, 0:n], wb_[:, kc, cc * 128:(cc + 1) * 128], uT[:, kc, uc0:uc0 + n], start=(kc == 0), stop=(kc == 7))
                    P.act(o_[:, 0:n], ps[:, 0:n], AF.Sigmoid)
                    r0 = bi * 512 + cc * 128
                    P.dma(S["GT"][r0:r0 + 128, tok0:tok0 + n], o_[:, 0:n], eng="sp" if it % 2 else "pool")
    P.barrier()


def resid_ln(P, pss, hsrc, gb, lg, lb, dst_rows, tmp, stt_, res, eng_i):
    for hf in range(2):
        P.tt(tmp[:, hf * 512:(hf + 1) * 512], pss[hf], gb[:, hf * 512:(hf + 1) * 512], ALU.mult)
    P.stt(tmp, hsrc, ALPHA, tmp, ALU.mult, ALU.add, eng="pool")
    ln_stats(P, tmp, res, stt_)
    P.act(res, tmp, AF.Identity, bias=stt_[:, 4:5], scale=stt_[:, 3:4])
    P.tt(res, res, lg, ALU.mult, eng="pool")
    P.tt(res, res, lb, ALU.add, eng="pool")
    for (d, r0, r1) in dst_rows:
        P.dma(d, res[r0:r1, :], eng="sp" if eng_i % 2 else "pool")


def load_bc(P, S, st, name, l, n=D):
    t = P.sb([128, n], stack=st)
    P.dma(t, bc_rows(S[name][l:l + 1, :], n))
    return t


def stage_merge(P, S, l, src):
    with ExitStack() as st:
        wbr = P.sb([128, 8, D], BF16, stack=st)
        wo = P.sb([128, 8, D], BF16, stack=st)
        wst = P.sb([128, 8, D], stack=st)
        P.dma(wst, S["w_branch"][l].re("j (kc p) n -> p (j kc) n", p=128))
        for i in range(8):
            P.copy(wbr[:, i, :], wst[:, i, :], eng="pool" if i % 2 else "dve")
        P.dma(wst, S["w_out"][l].re("(kc p) n -> p kc n", p=128))
        for i in range(8):
            P.copy(wo[:, i, :], wst[:, i, :], eng="pool" if i % 2 else "dve")
        gbc = []
        for w in range(2):
            g = P.sb([128, D], stack=st)
            P.dma(g, bc_rows(S["MODD"][:, w * 6144 + 2048: w * 6144 + 3072], D))
            gbc.append(g)
        lg = load_bc(P, S, st, "ln1_g", l)
        lb = load_bc(P, S, st, "ln1_b", l)
        yts = [P.sb([128, 8, 512], BF16, stack=st) for _ in range(2)]
        gts = [P.sb([128, 4, 512], stack=st) for _ in range(2)]
        accT = P.sb([128, 8, 512], BF16, stack=st)
        acc = P.sb([128, 512], stack=st)
        prod = [P.sb([128, 512], stack=st) for _ in range(2)]
        pss = [P.ps([128, 512], stack=st) for _ in range(4)]
        pmix = [P.ps([128, 512], stack=st) for _ in range(2)]
        hts = [P.sb([128, D], stack=st) for _ in range(2)]
        tmp = P.sb([128, D], stack=st)
        res = P.sb([128, D], stack=st)
        sts = [P.sb([128, 8], stack=st) for _ in range(2)]
        GTv = S["GT"].re("(j dc p) t -> dc p j t", j=4, dc=8, p=128)
        YTv = S["YT"].re("j (kc p) t -> p (j kc) t", p=128)
        it = 0
        ti = 0
        for ci, (uc0, tok0, n) in enumerate(TOKCH):
            yt = yts[ci % 2]
            P.dma(yt[:, :, 0:n], YTv[:, :, tok0:tok0 + n])
            for dc in range(8):
                gt = gts[dc % 2]
                P.dma(gt[:, :, 0:n], GTv[dc][:, :, tok0:tok0 + n], eng="pool")
                for j in range(4):
                    ps = pss[it % 4]
                    it += 1
                    for k2 in range(2):
                        P.mm(ps[:, 0:n], wbr[:, j * 2 + k2, dc * 128:(dc + 1) * 128], yt[:, j * 2 + k2, 0:n],
                             start=(k2 == 0), stop=(k2 == 1))
                    if j == 0:
                        P.tt(acc[:, 0:n], ps[:, 0:n], gt[:, j, 0:n], ALU.mult)
                    else:
                        pr = prod[j % 2]
                        P.tt(pr[:, 0:n], ps[:, 0:n], gt[:, j, 0:n], ALU.mult)
                        if j < 3:
                            P.tt(acc[:, 0:n], acc[:, 0:n], pr[:, 0:n], ALU.add, eng="pool")
                        else:
                            P.tt(accT[:, dc, 0:n], acc[:, 0:n], pr[:, 0:n], ALU.add, eng="pool")
            for tt_ in range(n // 128):
                tok = tok0 + tt_ * 128
                ht = hts[ti % 2]
                P.dma(ht, src[tok:tok + 128, :])
                for hf in range(2):
                    for dc in range(8):
                        P.mm(pmix[hf], accT[:, dc, tt_ * 128:(tt_ + 1) * 128], wo[:, dc, hf * 512:(hf + 1) * 512],
                             start=(dc == 0), stop=(dc == 7))
                resid_ln(P, pmix, ht, gbc[1 if tok < CTX else 0], lg, lb,
                         [(S["H1"][tok:tok + 128, :], 0, 128)], tmp, sts[ti % 2], res, ti)
                ti += 1
    P.barrier()


def stage_ffn(P, S, l, last):
    FC = 22
    with ExitStack() as st:
        uT = P.sb([128, 8, UTW], BF16, stack=st)
        stage_lnmod(P, S, S["H1"], 3072, 4096, uT)
        with ExitStack() as st2:
            wst = [P.sb([128, 8, 256], stack=st2) for _ in range(2)]
            wbs = [P.sb([128, 8, 256], BF16, stack=st2) for _ in range(2)]
            taps = [P.sb([128, 2, 4], stack=st2) for _ in range(2)]
            hT = [P.sb([128, UTW], stack=st2) for _ in range(2)]
            cT = [P.sb([128, UTW], stack=st2) for _ in range(2)]
            gT = [P.sb([128, UTW], BF16, stack=st2) for _ in range(2)]
            pss = [P.ps([128, 512], stack=st2) for _ in range(4)]
            for ab in range(2):
                P.memset(hT[ab], 0.0)
                P.memset(cT[ab], 0.0)
            it = 0
            for fi in range(FC):
                ws, wb_, tp = wst[fi % 2], wbs[fi % 2], taps[fi % 2]
                for ab in range(2):
                    c0 = ab * 2816 + fi * 128
                    P.dma(ws[:, :, ab * 128:(ab + 1) * 128], wview(S, "ffn_w_up", l, c0, 128), eng="sp" if ab else "pool")
                    P.dma(tp[:, ab, 0:3], S["ffn_conv_w"][l, :, c0:c0 + 128].re("j p -> p j"), allow_slow_non_contiguous=True)
                    P.dma(tp[:, ab, 3:4], S["ffn_conv_b"][l:l + 1, c0:c0 + 128].re("o p -> p o"), allow_slow_non_contiguous=True)
                for kc in range(8):
                    P.copy(wb_[:, kc, :], ws[:, kc, :], eng="pool" if kc % 2 else "dve")
                for ab in range(2):
                    h = hT[ab]
                    for (uc0, tok0, n) in TOKCH:
                        ps = pss[it % 4]
                        it += 1
                        for kc in range(8):
                            P.mm(ps[:, 0:n], wb_[:, kc, ab * 128:(ab + 1) * 128], uT[:, kc, uc0:uc0 + n], start=(kc == 0), stop=(kc == 7))
                        P.copy(h[:, uc0:uc0 + n], ps[:, 0:n], eng="act")
                    c = cT[ab]
                    e = "dve" if ab == 0 else "pool"
                    Wd = UTW - 2
                    P.ts(c[:, 1:1 + Wd], h[:, 0:Wd], tp[:, ab, 0:1], ALU.mult, tp[:, ab, 3:4], ALU.add, eng=e)
                    P.stt(c[:, 1:1 + Wd], h[:, 1:1 + Wd], tp[:, ab, 1:2], c[:, 1:1 + Wd], ALU.mult, ALU.add, eng=e)
                    P.stt(c[:, 1:1 + Wd], h[:, 2:2 + Wd], tp[:, ab, 2:3], c[:, 1:1 + Wd], ALU.mult, ALU.add, eng=e)
                P.act(cT[0], cT[0], AF.Silu)
                g = gT[fi % 2]
                P.tt(g, cT[0], cT[1], ALU.mult)
                P.dma(S["GFT"][fi * 128:(fi + 1) * 128, :], g, eng="sp")
        P.barrier()
    P.barrier()
    with ExitStack() as st:
        wd = P.sb([128, FC, D], BF16, stack=st)
        wst = [P.sb([128, 2, D], stack=st) for _ in range(2)]
        for i in range(FC // 2):
            P.dma(wst[i % 2], S["ffn_w_down"][l, i * 256:(i + 1) * 256, :].re("(k p) n -> p k n", p=128), eng="sp" if i % 2 else "pool")
            P.copy(wd[:, 2 * i, :], wst[i % 2][:, 0, :], eng="dve")
            P.copy(wd[:, 2 * i + 1, :], wst[i % 2][:, 1, :], eng="pool")
        gbc = []
        for w in range(2):
            g = P.sb([128, D], stack=st)
            P.dma(g, bc_rows(S["MODD"][:, w * 6144 + 5120: w * 6144 + 6144], D))
            gbc.append(g)
        lg = load_bc(P, S, st, "ln2_g", l)
        lb = load_bc(P, S, st, "ln2_b", l)
        gts = [P.sb([128, FC, 128], BF16, stack=st) for _ in range(2)]
        hts = [P.sb([128, D], stack=st) for _ in range(2)]
        pmix = [P.ps([128, 512], stack=st) for _ in range(4)]
        tmp = P.sb([128, D], stack=st)
        res = P.sb([128, D], stack=st)
        sts = [P.sb([128, 8], stack=st) for _ in range(2)]
        GFv = S["GFT"].re("(fc p) t -> p fc t", p=128)
        for t in range(NT):
            tok = t * 128
            c0 = ucol(tok)
            gt = gts[t % 2]
            P.dma(gt, GFv[:, :, c0:c0 + 128], eng="pool")
            ht = hts[t % 2]
            P.dma(ht, S["H1"][tok:tok + 128, :])
            pm = pmix[(t % 2) * 2:(t % 2) * 2 + 2]
            for hf in range(2):
                for fc in range(FC):
                    P.mm(pm[hf], gt[:, fc, :], wd[:, fc, hf * 512:(hf + 1) * 512], start=(fc == 0), stop=(fc == FC - 1))
            if last:
                dsts = [(S["out"][tok - CTX:tok - CTX + 128, :], 0, 128)] if tok >= CTX else []
            else:
                dsts = [(S["H2"][tok:tok + 128, :], 0, 128)]
            if dsts:
                resid_ln(P, pm, ht, gbc[1 if tok < CTX else 0], lg, lb, dsts, tmp, sts[t % 2], res, t)
    P.barrier()


def tok2feat(P, S, src, j):
    with ExitStack() as st:
        xs = [P.sb([128, 256], stack=st) for _ in range(2)]
        xb = [P.sb([128, 256], BF16, stack=st) for _ in range(2)]
        pt = [P.ps([128, 2, 128], BF16, stack=st) for _ in range(2)]
        ob = [P.sb([128, 2, 128], BF16, stack=st) for _ in range(2)]
        for t in range(NT):
            i = t % 2
            P.dma(xs[i], src[t * 128:(t + 1) * 128, :], eng="sp" if i else "pool")
            P.copy(xb[i], xs[i], eng="pool")
            for kc in range(2):
                P.transpose(pt[i][:, kc, :], xb[i][:, kc * 128:(kc + 1) * 128], S["identb"])
            P.copy(ob[i], pt[i], eng="act")
            P.dma(S["YT"][j].re("(kc p) t -> p kc t", p=128)[:, :, t * 128:(t + 1) * 128], ob[i], eng="sp" if i else "pool")
    P.barrier()


def load_vext(P, S, st, src, nh):
    ve = P.sb([128, NT, nh, 65], BF16, stack=st)
    P.memset(ve.re("p t h d -> p (t h d)"), 1.0)
    vs = [P.sb([128, nh * 64], stack=st) for _ in range(2)]
    for t in range(NT):
        i = t % 2
        P.dma(vs[i], src[t * 128:(t + 1) * 128, :], eng="sp" if i else "pool")
        P.copy(ve[:, t, :, 0:64], vs[i].re("p (h d) -> p h d", h=nh), eng="pool" if i else "dve")
    return ve


def stage_swa(P, S, l):
    with ExitStack() as st:
        qk = P.sb([64, 6, NTOK], BF16, stack=st)
        for u in range(6):
            P.dma(qk[:, u, :], S["QK"][u], eng="sp" if u % 2 else "pool")
        ve = load_vext(P, S, st, S["VSW"], 2)
        mk32 = P.sb([128, 2, 512], stack=st)
        P.dma(mk32, S["maskLR"].re("m p f -> p m f"))
        mk = P.sb([128, 2, 512], BF16, stack=st)
        P.copy(mk, mk32)
        es = P.sb([128, 4], stack=st)
        P.dma(es, bc_rows(S["swa_sink"][l:l + 1, :], 4))
        P.act(es, es, AF.Exp)
        pS = [P.ps([128, 4, 128], stack=st) for _ in range(2)]
        pO = [P.ps([128, 4, 65], stack=st) for _ in range(2)]
        E = [P.sb([128, 5, 4, 128], BF16, stack=st) for _ in range(2)]
        den = [P.sb([128, 4], stack=st) for _ in range(2)]
        yt = [P.sb([128, 4, 64], stack=st) for _ in range(2)]
        it = 0
        for qb in range(NT):
            if qb < 2:
                kts = [(0, None), (1, None)]
            else:
                kts = []
                if qb - 1 >= 2:
                    kts.append((qb - 1, 0))
                kts.append((qb, None))
                if qb + 1 < NT:
                    kts.append((qb + 1, 1))
                kts += [(0, None), (1, None)]
            e_ = E[qb % 2]
            for ki, (kt, m) in enumerate(kts):
                ps = pS[it % 2]
                it += 1
                for h in range(4):
                    P.mm(ps[:, h, :], qk[:, 4 + h // 2, kt * 128:(kt + 1) * 128], qk[:, h, qb * 128:(qb + 1) * 128])
                P.act(e_[:, ki].re("p h q -> p (h q)"), ps.re("p h q -> p (h q)"), AF.Exp, scale=0.125)
                if m is not None:
                    P.tt(e_[:, ki].re("p h q -> p (h q)"), e_[:, ki].re("p h q -> p (h q)"), mk[:, m, :], ALU.mult)
            po = pO[qb % 2]
            for h in range(4):
                for ki, (kt, m) in enumerate(kts):
                    P.mm(po[:, h, :], e_[:, ki, h, :], ve[:, kt, h // 2, :], start=(ki == 0), stop=(ki == len(kts) - 1))
            d_ = den[qb % 2]
            y_ = yt[qb % 2]
            P.tt(d_, po[:, :, 64], es, ALU.add)
            P.recip(d_, d_)
            for h in range(4):
                P.ts(y_[:, h, :], po[:, h, 0:64], d_[:, h:h + 1], ALU.mult)
            P.dma(S["Ysw"][qb * 128:(qb + 1) * 128, :], y_.re("p h d -> p (h d)"), eng="sp" if qb % 2 else "pool")
    P.barrier()
    tok2feat(P, S, S["Ysw"], 1)


def stage_diff(P, S, l, lam_init):
    with ExitStack() as st:
        qkc = [P.sb([32, 8, NTOK], BF16, stack=st) for _ in range(2)]
        for u in range(8):
            for c in range(2):
                P.dma(qkc[c][:, u, :], S["QK"][6 + u, c * 32:(c + 1) * 32, :], eng="sp" if u % 2 else "pool")
        ve = load_vext(P, S, st, S["VDF"], 4)
        idf = P.sb([128, 128], stack=st)
        P.dma(idf, S["ident"])
        lv = P.sb([128, 4, 32], stack=st)
        for i, nm in enumerate(("diff_lq1", "diff_lk1", "diff_lq2", "diff_lk2")):
            P.dma(lv[:, i, :], bc_rows(S[nm][l:l + 1, :], 32))
        lt = P.sb([128, 8], stack=st)
        pr = P.sb([128, 2, 32], stack=st)
        P.tt(pr[:, 0, :], lv[:, 0, :], lv[:, 1, :], ALU.mult)
        P.tt(pr[:, 1, :], lv[:, 2, :], lv[:, 3, :], ALU.mult)
        P.reduce(lt[:, 0:2], pr, ALU.add)
        P.act(lt[:, 0:2], lt[:, 0:2], AF.Exp)
        P.tt(lt[:, 2:3], lt[:, 1:2], lt[:, 0:1], ALU.subtract)
        P.ts(lt[:, 3:4], lt[:, 2:3], -lam_init, ALU.add)
        gsub = P.sb([128, 64], stack=st)
        P.dma(gsub, bc_rows(S["diff_subln_g"][l:l + 1, :], 64))
        P.ts(gsub, gsub, 1.0 - lam_init, ALU.mult)
        pS = [P.ps([128, 512], stack=st) for _ in range(2)]
        pO = [P.ps([65, 512], stack=st) for _ in range(2)]
        pT = [P.ps([128, 2, 65], stack=st) for _ in range(2)]
        E = [P.sb([128, 512], BF16, stack=st) for _ in range(3)]
        oT = [P.sb([65, 512], stack=st) for _ in range(2)]
        tm = [P.sb([128, 8], stack=st) for _ in range(2)]
        a_ = [P.sb([128, 64], stack=st) for _ in range(2)]
        w_ = [P.sb([128, 64], stack=st) for _ in range(2)]
        jk = P.sb([128, 64], stack=st)
        y_ = [P.sb([128, 64], stack=st) for _ in range(2)]
        it = 0
        ti = 0
        sc = 32 ** -0.5
        for h in range(4):
            for (uc0, tok0, n) in TOKCH:
                kts = [0, 1] if tok0 < CTX else list(range(NT))
                for c in range(2):
                    po = pO[c]
                    for ki, kt in enumerate(kts):
                        ps = pS[it % 2]
                        e_ = E[it % 3]
                        it += 1
                        P.mm(ps[:, 0:n], qkc[c][:, 4 + h, kt * 128:(kt + 1) * 128],
                             qkc[c][:, h, tok0:tok0 + n])
                        P.act(e_[:, 0:n], ps[:, 0:n], AF.Exp, scale=sc)
                        P.mm(po[:, 0:n], ve[:, kt, h, :], e_[:, 0:n], start=(ki == 0), stop=(ki == len(kts) - 1))
                    P.copy(oT[c][:, 0:n], po[:, 0:n], eng="act")
                for tt_ in range(n // 128):
                    i = ti % 2
                    ti += 1
                    pt = pT[i]
                    for c in range(2):
                        P.mm(pt[:, c, :], oT[c][:, tt_ * 128:(tt_ + 1) * 128], idf[0:65, 0:65])
                    t_ = tm[i]
                    P.recip(t_[:, 0:2], pt[:, :, 64])
                    P.tt(t_[:, 2:3], t_[:, 1:2], lt[:, 3:4], ALU.mult)
                    P.ts(a_[i], pt[:, 0, 0:64], t_[:, 0:1], ALU.mult)
                    P.stt(w_[i], pt[:, 1, 0:64], t_[:, 2:3], a_[i], ALU.mult, ALU.add)
                    P.memset(t_[:, 4:5], 0.0)
                    P.act(jk, w_[i], AF.Square, accum_out=t_[:, 4:5])
                    P.ts(t_[:, 5:6], t_[:, 4:5], 1.0 / 64, ALU.mult, 1e-5, ALU.add)
                    P.act(t_[:, 5:6], t_[:, 5:6], AF.Sqrt)
                    P.recip(t_[:, 6:7], t_[:, 5:6])
                    P.stt(y_[i], w_[i], t_[:, 6:7], gsub, ALU.mult, ALU.mult)
                    tok = tok0 + tt_ * 128
                    P.dma(S["Ydf"][tok:tok + 128, h * 64:(h + 1) * 64], y_[i], eng="sp" if i else "pool")
    P.barrier()
    tok2feat(P, S, S["Ydf"], 3)

TWO_PI = 6.283185307179586


def hy_filter(P, S, l, L, En, Tn, G):
    N2 = 2 * L
    cw = min(512, N2)
    with ExitStack() as st:
        w1 = P.sb([33, 64], stack=st)
        w2 = P.sb([64, 64], stack=st)
        w3 = P.sb([64, 1024], stack=st)
        P.dma(w1, S["hy_f_w1"][l])
        P.dma(w2, S["hy_f_w2"][l])
        P.dma(w3, S["hy_f_w3"][l])
        fr = P.sb([64, 8], stack=st)
        P.dma(fr[:, 0:1], S["hy_f_freq"][l:l + 1, :].re("o p -> p o"), allow_slow_non_contiguous=True)
        P.dma(fr[:, 1:2], S["hy_f_b1"][l:l + 1, :].re("o p -> p o"), allow_slow_non_contiguous=True)
        P.dma(fr[:, 2:3], S["hy_f_b2"][l:l + 1, :].re("o p -> p o"), allow_slow_non_contiguous=True)
        P.ts(fr[:, 3:4], fr[:, 0:1], 1.0 / TWO_PI, ALU.mult)
        P.tt(fr[:, 4:5], fr[:, 3:4], fr[:, 1:2], ALU.mult)
        P.ts(fr[:, 4:5], fr[:, 4:5], 8.0, ALU.add)
        P.tt(fr[:, 5:6], fr[:, 3:4], fr[:, 2:3], ALU.mult)
        P.ts(fr[:, 5:6], fr[:, 5:6], 8.0, ALU.add)
        negpi = P.sb([128, 1], stack=st)
        P.memset(negpi, 1.5707963267948966)
        h2 = P.sb([64, N2], stack=st)
        Et = [P.sb([33, cw], stack=st) for _ in range(2)]
        ps1 = [P.ps([64, cw], stack=st) for _ in range(2)]
        ps2 = [P.ps([64, cw], stack=st) for _ in range(2)]
        v1 = [P.sb([64, cw], stack=st) for _ in range(2)]
        h1 = [P.sb([64, cw], stack=st) for _ in range(2)]
        vi = [P.sb([64, cw], mybir.dt.int32, stack=st) for _ in range(2)]
        vf = [P.sb([64, cw], stack=st) for _ in range(2)]
        sa = [P.sb([64, cw], stack=st) for _ in range(2)]
        sb_ = [P.sb([64, cw], stack=st) for _ in range(2)]

        def sin_red(ps, s2, out, i):
            P.ts(v1[i], ps, fr[:, 3:4], ALU.mult, s2, ALU.add)
            P.copy(vi[i], v1[i])
            P.copy(vf[i], vi[i])
            P.tt(v1[i], v1[i], vf[i], ALU.subtract)
            P.act(sa[i], v1[i], AF.Sin, scale=3.141592653589793)
            P.act(sb_[i], v1[i], AF.Sin, bias=negpi[0:64, :], scale=-3.141592653589793)
            P.stt(out, sa[i], 2.0, sb_[i], ALU.mult, ALU.mult)

        for ch in range(N2 // cw):
            i = ch % 2
            sl = slice(ch * cw, (ch + 1) * cw)
            P.dma(Et[i], S[En][:, sl])
            P.mm(ps1[i], w1, Et[i])
            sin_red(ps1[i], fr[:, 4:5], h1[i], i)
            P.mm(ps2[i], w2, h1[i])
            sin_red(ps2[i], fr[:, 5:6], h2[:, sl], i)
        T128 = P.sb([128, N2], stack=st)
        P.dma(T128, bc_rows(S[Tn], N2))
        nd = P.sb([128, 2], stack=st)
        P.dma(nd, S["hyND"].re("(c p) o -> p (c o)", p=128), allow_slow_non_contiguous=True)
        fw = min(512, L)
        filt = [P.sb([128, L], stack=st) for _ in range(2)]
        fb = [P.sb([128, L], BF16, stack=st) for _ in range(2)]
        junk = P.sb([128, L], stack=st)
        asum = P.sb([128, 4], stack=st)
        psf = [P.ps([128, fw], stack=st) for _ in range(2)]
        dec = [P.sb([128, fw], stack=st) for _ in range(2)]
        it = 0
        for o in range(2):
            for chalf in range(2):
                P.memset(asum, 0.0)
                for dr in range(2):
                    q = o * 4 + dr * 2 + chalf
                    pos0 = L if dr == 0 else 0
                    for ch in range(L // fw):
                        i = it % 2
                        it += 1
                        sl = slice(pos0 + ch * fw, pos0 + (ch + 1) * fw)
                        P.mm(psf[i], w3[:, q * 128:(q + 1) * 128], h2[:, sl])
                        P.act(dec[i], T128[:, sl], AF.Exp, scale=nd[:, chalf:chalf + 1])
                        P.tt(filt[dr][:, ch * fw:(ch + 1) * fw], psf[i], dec[i], ALU.mult)
                    P.act(junk, filt[dr], AF.Abs, accum_out=asum[:, dr:dr + 1])
                P.tt(asum[:, 2:3], asum[:, 0:1], asum[:, 1:2], ALU.add)
                P.recip(asum[:, 3:4], asum[:, 2:3])
                for dr in range(2):
                    P.ts(fb[dr], filt[dr], asum[:, 3:4], ALU.mult, eng="dve" if dr else "pool")
                rows = slice(chalf * 128, (chalf + 1) * 128)
                P.dma(G[o, rows, 0:L - 1], fb[1][:, 0:L - 1], eng="sp")
                P.dma(G[o, rows, L - 1:2 * L - 1], fb[0][:, 0:L], eng="pool")
    P.barrier()


def hy_conv_seq(P, S, l, tok0, L, G, GL):
    with ExitStack() as st:
        for _ in hy_conv_gen(P, S, l, tok0, L, G, GL, st):
            pass
    P.barrier()


def hy_conv_gen(P, S, l, tok0, L, G, GL, st, nhk=3):
    NB = L // 128
    ND_ = 2 * NB - 1
    HKW = ND_ * 128
    Gt = G.ap.tensor
    if True:
        jf = P.sb([128, 128], stack=st)
        Jb = P.sb([128, 128], BF16, stack=st)
        P.dma(jf, S["antiid"])
        P.copy(Jb, jf)
        A = P.sb([128, NB, 256], stack=st)
        B = P.sb([128, NB, 256], stack=st)
        Vb = P.sb([128, NB, 256], BF16, stack=st)
        Zr = P.sb([128, NB, 256], BF16, stack=st)
        b01 = P.sb([128, 2, 256], stack=st)
        for o in range(2):
            P.dma(b01[:, o, :], bc_rows(S["hy_bias"][l, o:o + 1, :], 256))
        src = S["HYP"][tok0:tok0 + L, :].re("(a r) c -> r a c", r=128)
        P.dma(A, src[:, :, 0:256], eng="sp")
        P.dma(B, src[:, :, 256:512], eng="pool")
        hks = [P.sb([128, HKW], BF16, stack=st) for _ in range(nhk)]
        psJ = [P.ps([128, 512], stack=st) for _ in range(2)]
        psY = [P.ps([128, 16, NB], stack=st) for _ in range(2)]
        tmp = [P.sb([128, NB, 16], stack=st) for _ in range(2)]
        Vf = Vb.re("p a c -> p (a c)")
        Zf = Zr.re("p a c -> p (a c)")
        ds = [0] + [d for d in range(-(NB - 1), NB) if d != 0]
        yield
        for o in range(2):
            inp = A if o == 0 else B
            for a in range(NB):
                P.copy(Vb[:, a, :], inp[:, a, :], eng="pool" if a % 2 else "dve")
            for ch in range(NB * 256 // 512):
                pj = psJ[ch % 2]
                P.mm(pj, Jb, Vf[:, ch * 512:(ch + 1) * 512])
                P.copy(Zf[:, ch * 512:(ch + 1) * 512], pj, eng="act")
            for a in range(NB):
                P.tt(inp[:, a, :], inp[:, a, :], b01[:, o, :], ALU.mult, eng="pool")
            if o == 1:
                P.dma(A, src[:, :, 512:768], eng="sp")
            oth = B if o == 0 else A
            for cg in range(16):
                py = psY[cg % 2]
                for ci in range(16):
                    c = cg * 16 + ci
                    hk = hks[c % nhk]
                    hv = V(bass.AP(tensor=Gt, offset=(o * 256 + c) * GL, ap=[[1, 128], [1, HKW]]), G.key)
                    P.dma(hk, hv, eng="sp" if c % 2 else "pool")
                    for di, d in enumerate(ds):
                        a_lo, a_hi = max(0, d), min(NB - 1, NB - 1 + d)
                        P.mm(py[:, ci, a_lo:a_hi + 1], hk[:, (d + NB - 1) * 128:(d + NB) * 128],
                             Zr[:, a_lo - d:a_hi - d + 1, c], start=(di == 0), stop=(di == len(ds) - 1))
                        if di % 4 == 3:
                            yield
                cs = slice(cg * 16, (cg + 1) * 16)
                tp = tmp[cg % 2]
                P.tt(tp, py.re("p c a -> p a c"), inp[:, :, cs], ALU.add)
                P.tt(oth[:, :, cs], tp, oth[:, :, cs], ALU.mult, eng="pool")
        P.dma(S["Yhy"][tok0:tok0 + L, :].re("(a r) c -> r a c", r=128), A, eng="sp")
    yield


def stage_hyena(P, S, l):
    hy_filter(P, S, l, SEQ, "hyE_x", "hyT_x", S["Gx"])
    hy_filter(P, S, l, CTX, "hyE_c", "hyT_c", S["Gc"])
    hy_conv_seq(P, S, l, CTX, SEQ, S["Gx"], 2 * SEQ)
    hy_conv_seq(P, S, l, 0, CTX, S["Gc"], 2 * CTX)
    tok2feat(P, S, S["Yhy"], 0)


def stage_hyena_rwkv(P, S, l):
    hy_filter(P, S, l, SEQ, "hyE_x", "hyT_x", S["Gx"])
    hy_filter(P, S, l, CTX, "hyE_c", "hyT_c", S["Gc"])
    stage_rwkv_prep(P, S, l)
    with ExitStack() as st:
        gen = hy_conv_gen(P, S, l, CTX, SEQ, S["Gx"], 2 * SEQ, st, nhk=2)
        next(gen)
        stage_rwkv_scan(P, S, l, filler=gen, st_outer=st)
        for _ in gen:
            pass
    P.barrier()
    hy_conv_seq(P, S, l, 0, CTX, S["Gc"], 2 * CTX)
    tok2feat(P, S, S["Yhy"], 0)
    stage_rwkv_out(P, S, l)

RCH = 16


def s0_bwd(t):
    return 128 - 128 * t if t < 2 else 4480 - 128 * t


def stage_rwkv_prep(P, S, l):
    with ExitStack() as st:
        idf = P.sb([128, 128], stack=st)
        jf = P.sb([128, 128], stack=st)
        P.dma(idf, S["ident"])
        P.dma(jf, S["antiid"])
        zt = P.sb([128, 8704], stack=st)
        P.memset(zt, 0.0)
        vz = S["VBD"].re("d h s c -> (d h s c)").re("(p f) -> p f", p=128)
        for i in range(8):
            P.dma(vz[:, i * 8704:(i + 1) * 8704], zt, eng="sp" if i % 2 else "pool")
        w2 = [P.sb([64, 256], stack=st) for _ in range(2)]
        g2 = [P.sb([128, 256], stack=st) for _ in range(2)]
        a2 = P.sb([64, 256], stack=st)
        for d in range(2):
            P.dma(w2[d], S["rwkv_w2"][l, d])
            P.dma(g2[d], S["rwkv_g2"][l, d])
        P.dma(a2, S["rwkv_a2"][l])
        w0 = [P.sb([128, 256], stack=st) for _ in range(2)]
        for d in range(2):
            P.dma(w0[d], bc_rows(S["rwkv_w0"][l, d:d + 1, :], 256))
        a0 = load_bc(P, S, st, "rwkv_a0", l, 256)
        kkb = load_bc(P, S, st, "rwkv_kk", l, 256)
        kab = load_bc(P, S, st, "rwkv_ka", l, 256)
        omka = P.sb([128, 256], stack=st)
        P.ts(omka, kab, -1.0, ALU.mult, 1.0, ALU.add)
        rkb = P.sb([128, 256], stack=st)
        P.dma(rkb, bc_rows(S["rwkv_rk"][l:l + 1].re("o h d -> o (h d)"), 256))
        X = [P.sb([128, 1216], stack=st) for _ in range(2)]
        th = P.sb([128, 128], stack=st)
        sg = P.sb([128, 256], stack=st)
        pT = [P.ps([128, 128], stack=st) for _ in range(2)]
        tT = [P.sb([128, 128], stack=st) for _ in range(5)]
        pP = [P.ps([128, 256], stack=st) for _ in range(2)]
        wd = [P.sb([128, 256], stack=st) for _ in range(2)]
        a_ = P.sb([128, 256], stack=st)
        gd = [P.sb([128, 256], stack=st) for _ in range(2)]
        kk = P.sb([128, 256], stack=st)
        sq = P.sb([128, 256], stack=st)
        s4 = P.sb([128, 16], stack=st)
        an = P.sb([128, 256], stack=st)
        b_ = P.sb([128, 256], stack=st)
        kp = P.sb([128, 256], stack=st)
        t1 = P.sb([128, 256], stack=st)
        bon = P.sb([128, 256], stack=st)
        pK = [P.ps([64, 4, 128], stack=st) for _ in range(2)]
        kst = [P.sb([64, 128, 4], stack=st) for _ in range(2)]
        rv = [P.sb([128, 256], stack=st) for _ in range(2)]
        it = 0
        for t in range(NT):
            x = X[t % 2]
            P.dma(x, S["RWP"][t * 128:(t + 1) * 128, :], eng="sp" if t % 2 else "pool")
            r, k, v = x[:, 0:256], x[:, 256:512], x[:, 512:768]
            P.act(th, x[:, 768:896], AF.Tanh)
            P.act(sg, x[:, 960:1216], AF.Sigmoid)
            srcs = [(th[:, 0:64], 64), (th[:, 64:128], 64), (x[:, 896:960], 64), (sg[:, 0:128], 128), (sg[:, 128:256], 128)]
            for i, (sv, m) in enumerate(srcs):
                pt = pT[i % 2]
                P.mm(pt[0:m, :], sv, idf)
                P.copy(tT[i][0:m, :], pt[0:m, :], eng="act")
            for d in range(2):
                pp = pP[d]
                P.mm(pp, tT[d][0:64, :], w2[d])
                P.tt(wd[d], pp, w0[d], ALU.add)
                P.act(wd[d], wd[d], AF.Sigmoid)
                P.act(wd[d], wd[d], AF.Exp, scale=-0.6065306597126334)
            pp = pP[0]
            P.mm(pp, tT[2][0:64, :], a2)
            P.tt(a_, pp, a0, ALU.add)
            P.act(a_, a_, AF.Sigmoid)
            for d in range(2):
                pp = pP[1 - d]
                P.mm(pp, tT[3 + d], g2[d])
                P.copy(gd[d], pp, eng="act")
                P.dma(S["GFB"][d, t * 128:(t + 1) * 128, :], gd[d], eng="sp")
            P.tt(kk, k, kkb, ALU.mult)
            P.tt(sq, kk, kk, ALU.mult, eng="pool")
            P.reduce(s4[:, 0:4], sq.re("p (h d) -> p h d", h=4), ALU.add)
            P.act(s4[:, 0:4], s4[:, 0:4], AF.Sqrt)
            P.ts(s4[:, 0:4], s4[:, 0:4], 1e-12, ALU.max)
            P.recip(s4[:, 4:8], s4[:, 0:4])
            P.ts(s4[:, 4:8], s4[:, 4:8], -1.0, ALU.mult)
            for h in range(4):
                P.ts(an[:, h * 64:(h + 1) * 64], kk[:, h * 64:(h + 1) * 64], s4[:, 4 + h:5 + h], ALU.mult)
            P.stt(b_, an, -1.0, a_, ALU.mult, ALU.mult)
            P.tt(t1, a_, kab, ALU.mult, eng="pool")
            P.tt(t1, t1, omka, ALU.add, eng="pool")
            P.tt(kp, k, t1, ALU.mult, eng="pool")
            P.tt(t1, r, kp, ALU.mult, eng="pool")
            P.tt(t1, t1, rkb, ALU.mult, eng="pool")
            P.reduce(s4[:, 8:12], t1.re("p (h d) -> p h d", h=4), ALU.add)
            for h in range(4):
                P.ts(bon[:, h * 64:(h + 1) * 64], v[:, h * 64:(h + 1) * 64], s4[:, 8 + h:9 + h], ALU.mult)
            P.dma(S["BON"][t * 128:(t + 1) * 128, :], bon, eng="sp")
            sf = t * 128
            sb_ = s0_bwd(t)
            for (src, name, dirs) in ((an, "ANT", (0, 1)), (r, "RT", (0, 1)), (wd[0], "WT", (0,)), (wd[1], "WT", (1,))):
                for d in dirs:
                    pk = pK[it % 2]
                    ks = kst[it % 2]
                    it += 1
                    for h in range(4):
                        P.mm(pk[:, h, :], src[:, h * 64:(h + 1) * 64], idf if d == 0 else jf)
                    P.copy(ks.re("k s h -> k h s"), pk, eng="act")
                    s0 = sf if d == 0 else sb_
                    P.dma(S[name][d, :, s0:s0 + 128, :], ks, eng="sp" if it % 2 else "pool")
            for (src, name) in ((b_, "BT"), (kp, "KT")):
                P.dma(S[name][0, sf:sf + 128, :], src, eng="pool")
            for h in range(4):
                P.dma(S["VBD"][0, h, sf:sf + 128, h * 64:(h + 1) * 64], v[:, h * 64:(h + 1) * 64], eng="sp")
            for i, (src, name) in enumerate(((b_, "BT"), (kp, "KT"), (v, "V"))):
                pp = pP[i % 2]
                P.mm(pp, jf, src)
                rr = rv[i % 2]
                P.copy(rr, pp, eng="act")
                if name == "V":
                    for h in range(4):
                        P.dma(S["VBD"][1, h, sb_:sb_ + 128, h * 64:(h + 1) * 64], rr[:, h * 64:(h + 1) * 64], eng="sp")
                else:
                    P.dma(S[name][1, sb_:sb_ + 128, :], rr, eng="pool")
    P.barrier()


def stage_rwkv_scan(P, S, l, filler=None, st_outer=None):
    CH = RCH
    NCH = NTOK // CH
    with ExitStack() as st:
        LA = [P.sb([128, CH, 40], stack=st) for _ in range(2)]
        L2 = [P.sb([16, CH, 128], stack=st) for _ in range(2)]
        R2 = [P.sb([16, CH, 256], stack=st) for _ in range(2)]
        Wt = [P.sb([128, CH, 4], stack=st) for _ in range(2)]
        OB = 4
        Os = [P.sb([40, OB, 256], stack=st) for _ in range(2)]
        for i in range(2):
            P.memset(LA[i].re("p s c -> p (s c)"), 0.0)
            P.memset(L2[i].re("p s c -> p (s c)"), 0.0)
            P.memset(R2[i].re("p s c -> p (s c)"), 0.0)
        mask8 = P.sb([8, 256], stack=st)
        P.dma(mask8, S["mask8"])
        St = P.sb([128, 256], stack=st)
        P.memset(St, 0.0)
        pA = [P.ps([40, 256], stack=st) for _ in range(2)]
        pU = [P.ps([128, 256], stack=st) for _ in range(2)]
        S3 = St.re("p (h v) -> p h v", h=4)

        def load_chunk(c):
            i = c % 2
            s0 = c * CH
            if c == NCH:
                for d in range(2):
                    P.dma(LA[i][d * 64:(d + 1) * 64, 0:1, 32 + 4 * d:36 + 4 * d], S["RT"][d, :, s0 - 1:s0, :], eng="sp")
                return
            for d in range(2):
                rows = slice(d * 64, (d + 1) * 64)
                P.dma(LA[i][rows, :, 4 * d:4 * d + 4], S["ANT"][d, :, s0:s0 + CH, :], eng="sp")
                if c == 0:
                    P.dma(LA[i][rows, 1:CH, 32 + 4 * d:36 + 4 * d], S["RT"][d, :, 0:CH - 1, :], eng="act")
                else:
                    P.dma(LA[i][rows, :, 32 + 4 * d:36 + 4 * d], S["RT"][d, :, s0 - 1:s0 + CH - 1, :], eng="act")
                P.dma(Wt[i][rows, :, :], S["WT"][d, :, s0:s0 + CH, :], eng="sp")
                P.dma(L2[i][4 * d:4 * d + 4, :, d * 64:(d + 1) * 64],
                      S["BT"][d, s0:s0 + CH, :].re("s (h k) -> h s k", h=4), eng="act")
                P.dma(L2[i][8 + 4 * d:12 + 4 * d, :, d * 64:(d + 1) * 64],
                      S["KT"][d, s0:s0 + CH, :].re("s (h k) -> h s k", h=4), eng="sp")
                P.dma(R2[i][8 + 4 * d:12 + 4 * d, :, :], S["VBD"][d, :, s0:s0 + CH, :], eng="act")

        load_chunk(0)
        for s in range(NTOK + 1):
            c, pos = divmod(s, CH)
            i = c % 2
            if pos == 0 and c + 1 <= NCH:
                load_chunk(c + 1)
            pa = pA[s % 2]
            P.mm(pa, LA[i][:, pos, :], St)
            if s >= 1:
                cp, pp_ = divmod(s - 1, OB)
                P.copy(Os[cp % 2][32:40, pp_, :], pa[32:40, :], eng="act")
                if pp_ == OB - 1:
                    P.dma(S["OD"].re("d h s c -> (d h) s c")[:, cp * OB:(cp + 1) * OB, :], Os[cp % 2][32:40, :, :], eng="sp")
            if s == NTOK:
                break
            if filler is not None:
                next(filler, None)
            P.tt(R2[i][0:8, pos, :], pa[0:8, :], mask8, ALU.mult)
            pu = pU[s % 2]
            P.mm(pu, L2[i][:, pos, :], R2[i][:, pos, :])
            P.tt(S3, S3, V(Wt[i].ap[:, pos, :].unsqueeze(2).to_broadcast([128, 4, 64]), Wt[i].key), ALU.mult, eng="pool")
            P.tt(St, St, pu, ALU.add)
            if filler is not None:
                next(filler, None)
    if filler is None:
        P.barrier()


def stage_rwkv_out(P, S, l):
    with ExitStack() as st:
        jf = P.sb([128, 128], stack=st)
        P.dma(jf, S["antiid"])
        gam = load_bc(P, S, st, "rwkv_lnx_g", l, 256)
        bet = load_bc(P, S, st, "rwkv_lnx_b", l, 256)
        o_ = [[P.sb([128, 256], stack=st) for _ in range(2)] for _ in range(2)]
        gfb = [[P.sb([128, 256], stack=st) for _ in range(2)] for _ in range(2)]
        bon = [P.sb([128, 256], stack=st) for _ in range(2)]
        orev = [P.sb([128, 256], stack=st) for _ in range(2)]
        pj = [P.ps([128, 256], stack=st) for _ in range(2)]
        sq = P.sb([128, 256], stack=st)
        s4 = [P.sb([128, 16], stack=st) for _ in range(2)]
        gn = [P.sb([128, 256], stack=st) for _ in range(2)]
        y = [P.sb([128, 256], stack=st) for _ in range(2)]
        for t in range(NT):
            i = t % 2
            sf, sb_ = t * 128, s0_bwd(t)
            for d in range(2):
                s0 = sf if d == 0 else sb_
                for h in range(4):
                    P.dma(o_[i][d][:, h * 64:(h + 1) * 64], S["OD"][d, h, s0:s0 + 128, h * 64:(h + 1) * 64],
                          eng="sp" if h % 2 else "pool")
                P.dma(gfb[i][d], S["GFB"][d, sf:sf + 128, :], eng="sp")
            P.dma(bon[i], S["BON"][sf:sf + 128, :], eng="pool")
            for d in range(2):
                if d == 1:
                    P.mm(pj[i], jf, o_[i][1])
                    P.copy(orev[i], pj[i], eng="act")
                    od = orev[i]
                else:
                    od = o_[i][0]
                s_ = s4[d]
                o3 = od.re("p (h d) -> p h d", h=4)
                P.reduce(s_[:, 0:4], o3, ALU.add)
                P.tt(sq, od, od, ALU.mult)
                P.reduce(s_[:, 4:8], sq.re("p (h d) -> p h d", h=4), ALU.add)
                P.ts(s_[:, 0:4], s_[:, 0:4], 1.0 / 64, ALU.mult)
                P.tt(s_[:, 8:12], s_[:, 0:4], s_[:, 0:4], ALU.mult)
                P.stt(s_[:, 4:8], s_[:, 4:8], 1.0 / 64, s_[:, 8:12], ALU.mult, ALU.subtract)
                P.ts(s_[:, 4:8], s_[:, 4:8], 64e-5, ALU.add)
                P.act(s_[:, 4:8], s_[:, 4:8], AF.Sqrt)
                P.recip(s_[:, 8:12], s_[:, 4:8])
                g_ = gn[d]
                for h in range(4):
                    P.ts(g_[:, h * 64:(h + 1) * 64], od[:, h * 64:(h + 1) * 64], s_[:, h:h + 1], ALU.subtract,
                         s_[:, 8 + h:9 + h], ALU.mult)
                P.tt(g_, g_, gam, ALU.mult, eng="pool")
                P.tt(g_, g_, bet, ALU.add, eng="pool")
                P.tt(g_, g_, bon[i], ALU.add, eng="pool")
                P.tt(g_, g_, gfb[i][d], ALU.mult, eng="pool")
            P.tt(y[i], gn[0], gn[1], ALU.add, eng="pool")
            P.dma(S["Yrw"][sf:sf + 128, :], y[i], eng="sp")
    P.barrier()
    tok2feat(P, S, S["Yrw"], 2)


def stage_rwkv(P, S, l):
    stage_rwkv_prep(P, S, l)
    stage_rwkv_scan(P, S, l)
    stage_rwkv_out(P, S, l)
SCRATCH = {
    "MODD": ([1, 2 * 6144], "f32"),
    "HYP": ([NTOK, 768], "f32"),
    "RWP": ([NTOK, 1216], "f32"),
    "VSW": ([NTOK, 128], "f32"),
    "VDF": ([NTOK, 256], "f32"),
    "QK": ([14, 64, NTOK], "bf16"),
    "GT": ([4096, NTOK], "f32"),
    "YT": ([4, 256, NTOK], "bf16"),
    "H1": ([NTOK, D], "f32"),
    "H2": ([NTOK, D], "f32"),
    "GFT": ([2816, UTW], "bf16"),
    "Gx": ([2, 256, 2 * SEQ], "bf16"),
    "Gc": ([2, 256, 2 * CTX], "bf16"),
    "Yhy": ([NTOK, 256], "f32"),
    "ANT": ([2, 64, NTOK, 4], "f32"), "RT": ([2, 64, NTOK, 4], "f32"), "WT": ([2, 64, NTOK, 4], "f32"),
    "BT": ([2, NTOK, 256], "f32"), "KT": ([2, NTOK, 256], "f32"), "VBD": ([2, 4, NTOK, 256], "f32"),
    "OD": ([2, 4, NTOK, 256], "f32"), "GFB": ([2, NTOK, 256], "f32"), "BON": ([NTOK, 256], "f32"),
    "Ysw": ([NTOK, 256], "f32"),
    "Yrw": ([NTOK, 256], "f32"),
    "Ydf": ([NTOK, 256], "f32"),
}

WEIGHTS = {
    "ada_w": [2, 1024, 6144], "ada_b": [2, 6144], "w_in": [2, 1024, 7360], "w_swap": [2, 1024, 896],
    "hy_conv_w": [2, 3, 768], "hy_conv_b": [2, 768], "hy_f_w1": [2, 33, 64], "hy_f_b1": [2, 64],
    "hy_f_w2": [2, 64, 64], "hy_f_b2": [2, 64], "hy_f_w3": [2, 64, 1024], "hy_f_freq": [2, 64],
    "hy_bias": [2, 2, 256], "swa_sink": [2, 4], "rwkv_mu": [2, 1216], "rwkv_w0": [2, 2, 256],
    "rwkv_w2": [2, 2, 64, 256], "rwkv_a0": [2, 256], "rwkv_a2": [2, 64, 256], "rwkv_g2": [2, 2, 128, 256],
    "rwkv_kk": [2, 256], "rwkv_ka": [2, 256], "rwkv_rk": [2, 4, 64], "rwkv_lnx_g": [2, 256],
    "rwkv_lnx_b": [2, 256], "diff_lq1": [2, 32], "diff_lk1": [2, 32], "diff_lq2": [2, 32], "diff_lk2": [2, 32],
    "diff_subln_g": [2, 64], "w_branch": [2, 4, 256, 1024], "w_out": [2, 1024, 1024], "ln1_g": [2, 1024],
    "ln1_b": [2, 1024], "ffn_w_up": [2, 1024, 5632], "ffn_conv_w": [2, 3, 5632], "ffn_conv_b": [2, 5632],
    "ffn_w_down": [2, 2816, 1024], "ln2_g": [2, 1024], "ln2_b": [2, 1024],
}
CONSTS = {"ident": [128, 128], "antiid": [128, 128], "rope": [4, 64, NTOK], "maskLR": [2, 128, 512], "mask8": [8, 256],
          "hyE_x": [33, 2 * SEQ], "hyE_c": [33, 2 * CTX], "hyT_x": [1, 2 * SEQ], "hyT_c": [1, 2 * CTX], "hyND": [256, 1]}


def build(nc, dbg=(), stages=None, nlayers=2, ext_in=()):
    P = Prog(nc)
    S = {}
    S["xin"] = P.dram("xin", [NTOK, D], F32, kind="ExternalInput")
    S["cvec"] = P.dram("cvec", [2, D], F32, kind="ExternalInput")
    for k, shp in {**WEIGHTS, **CONSTS}.items():
        S[k] = P.dram(k, shp, F32, kind="ExternalInput")
    for k, (shp, dt) in SCRATCH.items():
        S[k] = P.dram(k, shp, F32 if dt == "f32" else BF16, kind="ExternalOutput" if k in dbg else ("ExternalInput" if k in ext_in else "Internal"))
    S["out"] = P.dram("out", [SEQ, D], F32, kind="ExternalOutput")
    idf = P.sb([128, 128])
    S["identb"] = P.sb([128, 128], BF16)
    P.dma(idf, S["ident"])
    P.copy(S["identb"], idf)
    for l in range(nlayers):
        src = S["xin"] if l == 0 else S["H2"]
        if stages is None or "mod" in stages:
            stage_mod(P, S, l)
        if stages is None or "A" in stages:
            with ExitStack() as st:
                uT = P.sb([128, 8, UTW], BF16, stack=st)
                stage_lnmod(P, S, src, 0, 1024, uT)
                stage_inproj(P, S, l, uT)
            P.barrier()
        lam_init = 0.8 - 0.6 * float(np.exp(-0.3 * l))
        if stages is None or "hyrw" in stages:
            stage_hyena_rwkv(P, S, l)
        if stages is not None and "hyena" in stages:
            stage_hyena(P, S, l)
        if stages is not None and "rwkv" in stages:
            stage_rwkv(P, S, l)
        if stages is None or "swa" in stages:
            stage_swa(P, S, l)
        if stages is None or "diff" in stages:
            stage_diff(P, S, l, lam_init)
        if stages is None or "merge" in stages:
            stage_merge(P, S, l, src)
        if stages is None or "ffn" in stages:
            stage_ffn(P, S, l, (l == nlayers - 1) and ("H2" not in dbg))
    P.finalize()
    return P


def rope_tables():
    pos = np.arange(SEQ)
    row = (pos // 64).astype(np.float32)
    col = (pos % 64).astype(np.float32)
    out = np.zeros((4, 64, NTOK), np.float32)
    out[0, :, :CTX] = 1.0
    out[2, :, :CTX] = 1.0
    for ti, d in ((0, 64), (2, 32)):
        nf = d // 4
        inv = (10000.0 ** (-np.arange(nf, dtype=np.float32) / nf)).astype(np.float32)
        for rep in range(64 // d):
            for half, p in enumerate((row, col)):
                ang = (p[None, :] * inv[:, None]).astype(np.float32)
                c, s = np.cos(ang), np.sin(ang)
                b = rep * d + half * 2 * nf
                out[ti, b:b + nf, CTX:] = c
                out[ti, b + nf:b + 2 * nf, CTX:] = c
                out[ti + 1, b:b + nf, CTX:] = -s
                out[ti + 1, b + nf:b + 2 * nf, CTX:] = s
    return out


def hy_consts(L):
    pos = np.concatenate([np.arange(L - 1, -1, -1), np.arange(L)]).astype(np.float64)
    t = np.linspace(0.0, 1.0, L, dtype=np.float32)[pos.astype(np.int64)]
    w = (2.0 * np.pi * pos / L)
    fbv = np.linspace(1e-4, 15, 16, dtype=np.float32).astype(np.float64)
    ang = w[None, :] * fbv[:, None]
    E = np.concatenate([t[None, :].astype(np.float64), np.cos(ang), -np.sin(ang)], 0).astype(np.float32)
    return np.ascontiguousarray(E), np.ascontiguousarray(t[None, :].astype(np.float32))


def swap_cols():
    idx = []
    for (c0, n, d) in ((SWQ0, 256, 64), (SWK0, 128, 64), (DFQ0, 256, 32), (DFK0, 256, 32)):
        nf = d // 4
        for j in range(n):
            i = j % (2 * nf)
            idx.append(c0 + (j + nf if i < nf else j - nf))
    return np.array(idx)


def make_in_maps(inputs, cores):
    ins = {k: np.ascontiguousarray(np.asarray(v, dtype=np.float32)) for k, v in inputs.items()}
    common = {k: ins[k] for k in WEIGHTS if k != "w_swap"}
    common["w_swap"] = np.ascontiguousarray(ins["w_in"][:, :, swap_cols()])
    common["ident"] = np.eye(128, dtype=np.float32)
    common["antiid"] = np.ascontiguousarray(np.eye(128, dtype=np.float32)[::-1])
    common["rope"] = rope_tables()
    common["hyE_x"], common["hyT_x"] = hy_consts(SEQ)
    common["hyE_c"], common["hyT_c"] = hy_consts(CTX)
    lo, hi = np.log(1e-2) / 1.5, np.log(1e-2) / 0.3
    common["hyND"] = np.ascontiguousarray(-np.abs(np.linspace(lo, hi, 256, dtype=np.float32))[:, None])
    common["mask8"] = np.ascontiguousarray((np.arange(8)[:, None] % 4 == np.arange(256)[None, :] // 64).astype(np.float32))
    rr = np.arange(128)
    mL = (rr[:, None] >= rr[None, :]).astype(np.float32)
    mR = (rr[:, None] <= rr[None, :]).astype(np.float32)
    common["maskLR"] = np.ascontiguousarray(np.stack([np.tile(mL, (1, 4)), np.tile(mR, (1, 4))], 0))
    maps = []
    for b in cores:
        m = dict(common)
        m["xin"] = np.ascontiguousarray(np.concatenate([ins["ctx"][b], ins["x"][b]], 0))
        m["cvec"] = np.ascontiguousarray(np.stack([ins["c"][b], ins["c_ctx"]], 0))
        maps.append(m)
    return maps


def kernel(**inputs):
    nc = bass.Bass("TRN2", target_bir_lowering=False)
    build(nc)
    maps = make_in_maps(inputs, list(range(8)))
    res = run_bass_kernel_spmd(nc, maps, core_ids=list(range(8)))
    return np.stack([r["out"] for r in res.results], 0).astype(np.float32)
```

```python
import numpy as np
import concourse.bass as bass
import concourse.mybir as mybir
from contextlib import ExitStack

F32 = mybir.dt.float32
BF16 = mybir.dt.bfloat16
AF = mybir.ActivationFunctionType
ALU = mybir.AluOpType
AX = mybir.AxisListType

EPOCH = 30000
NDMA = 8


class V:
    __slots__ = ("ap", "key")

    def __init__(self, ap, key):
        self.ap = ap
        self.key = key

    def __getitem__(self, idx):
        return V(self.ap[idx], self.key)

    def k(self, sub):
        base = self.key[0] if isinstance(self.key, tuple) else self.key
        return V(self.ap, (base, sub))

    def re(self, s, **kw):
        return V(self.ap.rearrange(s, **kw), self.key)

    def bc(self, shape):
        return V(self.ap.to_broadcast(list(shape)), self.key)

    def bitcast(self, dt):
        return V(self.ap.bitcast(dt), self.key)


def _ap(x):
    return x.ap if isinstance(x, V) else x


class Prog:
    ENG = ("pe", "act", "dve", "pool", "sp")

    def __init__(self, nc):
        self.nc = nc
        self.es = ExitStack()
        self.ops = {e: [] for e in self.ENG}
        self.cnt = {e: 0 for e in self.ENG}
        self.sems = {e: [] for e in self.ENG}
        self.waited = {}
        self.last_w = {}
        self.readers = {}
        self.dma_cnt = {e: 0 for e in self.ENG}
        self.dma_sems = {e: None for e in self.ENG}
        self.nbuf = 0
        self.out_tokens = []

    def sb(self, shape, dt=F32, name=None, stack=None):
        self.nbuf += 1
        name = name or f"sb{self.nbuf}"
        h = (stack or self.es).enter_context(self.nc.sbuf_tensor(f"{name}_{self.nbuf}", list(shape), dt))
        return V(h[:] if hasattr(h, "__getitem__") else h.ap(), f"{name}_{self.nbuf}")

    def ps(self, shape, dt=F32, name=None, stack=None):
        self.nbuf += 1
        name = name or f"ps{self.nbuf}"
        h = (stack or self.es).enter_context(self.nc.psum_tensor(f"{name}_{self.nbuf}", list(shape), dt))
        return V(h[:] if hasattr(h, "__getitem__") else h.ap(), f"{name}_{self.nbuf}")

    def dram(self, name, shape, dt=F32, kind="Internal"):
        h = self.nc.dram_tensor(name, list(shape), dt, kind=kind)
        return V(h.ap(), name)

    def _sem_for(self, eng, idx):
        ep = idx // EPOCH
        while len(self.sems[eng]) <= ep:
            s = self.es.enter_context(self.nc.semaphore(f"s_{eng}_{len(self.sems[eng])}"))
            self.sems[eng].append(s)
        return self.sems[eng][ep], (idx % EPOCH) + 1

    def _wait(self, eng, tok):
        if tok is None:
            return
        if tok[0] == "e":
            _, src, idx = tok
            if src == eng and eng == "pe":
                return
            if self.waited.get((eng, src), -1) >= idx:
                return
            self.waited[(eng, src)] = idx
            sem, val = self._sem_for(src, idx)
            self.ops[eng].append(("w", sem, val))
        else:
            _, sem, val, sid = tok
            if self.waited.get((eng, sid), -1) >= val:
                return
            self.waited[(eng, sid)] = val
            self.ops[eng].append(("w", sem, val))

    def _deps(self, eng, reads, writes):
        for k in reads:
            self._wait(eng, self.last_w.get(k))
        for k in writes:
            self._wait(eng, self.last_w.get(k))
            for t in self.readers.get(k, ()):
                self._wait(eng, t)

    def _commit(self, tok, reads, writes):
        for k in reads:
            self.readers.setdefault(k, []).append(tok)
        for k in writes:
            self.last_w[k] = tok
            self.readers[k] = []

    def op(self, eng, fn, outs, ins):
        reads = [x.key for x in ins if isinstance(x, V)]
        writes = [x.key for x in outs if isinstance(x, V)]
        self._deps(eng, reads, writes)
        idx = self.cnt[eng]
        self.cnt[eng] += 1
        sem, _ = self._sem_for(eng, idx)
        self.ops[eng].append(("i", fn, sem, 1))
        tok = ("e", eng, idx)
        self._commit(tok, reads, writes)
        return tok

    def dma(self, out, in_, eng="sp", **kw):
        reads = [in_.key]
        writes = [out.key]
        self._deps(eng, reads, writes)
        if self.dma_sems[eng] is None:
            self.dma_sems[eng] = [self.es.enter_context(self.nc.semaphore(f"d_{eng}_{i}")) for i in range(NDMA)]
        j = self.dma_cnt[eng]
        self.dma_cnt[eng] += 1
        s = j % NDMA
        sem = self.dma_sems[eng][s]
        sid = f"d_{eng}_{s}"
        if j >= NDMA:
            self._wait(eng, ("d", sem, 16 * (j // NDMA), sid))
        o, i = out.ap, in_.ap
        self.ops[eng].append(("i", lambda e: e.dma_start(out=o, in_=i, **kw), sem, 16))
        tok = ("d", sem, 16 * (j // NDMA + 1), sid)
        self._commit(tok, reads, writes)
        return tok

    def mm(self, out, lhsT, rhs, start=True, stop=True):
        o, l, r = out.ap, lhsT.ap, rhs.ap
        return self.op("pe", lambda e: e.matmul(o, l, r, start=start, stop=stop), [out], [lhsT, rhs])

    def transpose(self, out, in_, ident):
        o, i, d = out.ap, in_.ap, ident.ap
        return self.op("pe", lambda e: e.transpose(o, i, d), [out], [in_, ident])

    def act(self, out, in_, func, bias=0.0, scale=1.0, accum_out=None):
        o, i, b, s = out.ap, in_.ap, _ap(bias), _ap(scale)
        if accum_out is not None:
            a = accum_out.ap
            fn = lambda e: e.activation(o, i, func, bias=b, scale=s, accum_out=a)
            outs = [out, accum_out]
        else:
            fn = lambda e: e.activation(o, i, func, bias=b, scale=s)
            outs = [out]
        return self.op("act", fn, outs, [in_, bias, scale])

    def tt(self, out, a, b, op, eng="dve"):
        o, x, y = out.ap, a.ap, b.ap
        return self.op(eng, lambda e: e.tensor_tensor(o, x, y, op), [out], [a, b])

    def ts(self, out, a, s1, op0, s2=None, op1=None, eng="dve", accum_out=None):
        o, x, p, q = out.ap, a.ap, _ap(s1), _ap(s2)
        kw = {}
        outs = [out]
        if op1 is not None:
            kw["op1"] = op1
        if accum_out is not None:
            kw["accum_out"] = accum_out.ap
            outs.append(accum_out)
        return self.op(eng, lambda e: e.tensor_scalar(o, x, p, q, op0, **kw), outs, [a, s1, s2])

    def stt(self, out, a, scalar, b, op0, op1, eng="dve"):
        o, x, s, y = out.ap, a.ap, _ap(scalar), b.ap
        return self.op("dve", lambda e: e.scalar_tensor_tensor(o, x, s, y, op0, op1), [out], [a, scalar, b])

    def copy(self, out, in_, eng="dve"):
        o, i = out.ap, in_.ap
        if eng == "act":
            return self.op("act", lambda e: e.copy(o, i), [out], [in_])
        return self.op(eng, lambda e: e.tensor_copy(o, i), [out], [in_])

    def reduce(self, out, in_, op, axis=AX.X, eng="dve", abs_=None):
        o, i = out.ap, in_.ap
        kw = {}
        if abs_:
            kw["apply_absolute_value"] = True
        return self.op(eng, lambda e: e.tensor_reduce(o, i, axis, op, **kw), [out], [in_])

    def recip(self, out, in_):
        o, i = out.ap, in_.ap
        return self.op("dve", lambda e: e.reciprocal(o, i), [out], [in_])

    def memset(self, out, val, eng="dve"):
        o = out.ap
        return self.op(eng, lambda e: e.memset(o, val), [out], [])

    def iota(self, out, pattern, base=0, channel_multiplier=0):
        o = out.ap
        return self.op("pool", lambda e: e.iota(o, pattern, base=base, channel_multiplier=channel_multiplier,
                                                allow_small_or_imprecise_dtypes=True), [out], [])

    def barrier(self):
        toks = [("e", e, self.cnt[e] - 1) for e in self.ENG if self.cnt[e] > 0]
        for e in self.ENG:
            for j in range(min(self.dma_cnt[e], NDMA)):
                jj = self.dma_cnt[e] - 1 - j
                s = jj % NDMA
                toks.append(("d", self.dma_sems[e][s], 16 * (jj // NDMA + 1), f"d_{e}_{s}"))
        for e in self.ENG:
            for t in toks:
                self._wait(e, t)
        self.last_w = {}
        self.readers = {}

    def finalize(self):
        self.barrier()
        nc = self.nc
        ops = self.ops
        with nc.Block() as block:
            def mk(name):
                def body(e):
                    for it in ops[name]:
                        if it[0] == "w":
                            e.wait_ge(it[1], it[2])
                        else:
                            it[1](e).then_inc(it[2], it[3])
                return body
            block.tensor(mk("pe"))
            block.scalar(mk("act"))
            block.vector(mk("dve"))
            block.gpsimd(mk("pool"))
            block.sync(mk("sp"))
        self.es.close()
from concourse.bass_utils import run_bass_kernel_spmd
D = 1024
SEQ = 4096
CTX = 256
NTOK = SEQ + CTX
NT = NTOK // 128
INC = 7360
UTW = NTOK + 3
LN_EPS = 1e-6
HY0 = 0
SWQ0, SWK0, SWV0 = 768, 1024, 1152
RW0 = 1280
DFQ0, DFK0, DFV0 = 2496, 2752, 3008
GT0 = 3264
ALPHA = 4 ** 0.25


def ucol(tok):
    return tok + 1 if tok < CTX else tok + 2


def bc_rows(v, n):
    return V(v.ap.to_broadcast([128, n]), v.key)


def stage_mod(P, S, l):
    with ExitStack() as st:
        cs = P.sb([128, 2, 8], stack=st)
        P.dma(cs, S["cvec"].re("w (kc p) -> p w kc", p=128), allow_slow_non_contiguous=True)
        P.act(cs, cs, AF.Silu)
        brow = P.sb([1, 6144], stack=st)
        P.dma(brow, S["ada_b"][l:l + 1, :])
        res = P.sb([1, 2, 6144], stack=st)
        wts = [P.sb([128, 8, 512], stack=st) for _ in range(2)]
        pss = [P.ps([1, 512], stack=st) for _ in range(2)]
        for nb in range(12):
            wt = wts[nb % 2]
            P.dma(wt, S["ada_w"][l, :, nb * 512:(nb + 1) * 512].re("(kc p) n -> p kc n", p=128),
                  eng="sp" if nb % 2 == 0 else "pool")
            for w in range(2):
                ps = pss[w]
                for kc in range(8):
                    P.mm(ps, cs[:, w, kc:kc + 1], wt[:, kc, :], start=(kc == 0), stop=(kc == 7))
                P.tt(res[:, w, nb * 512:(nb + 1) * 512], ps, brow[:, nb * 512:(nb + 1) * 512], ALU.add)
        for w in range(2):
            for off in (1024, 4096):
                P.ts(res[:, w, off:off + 1024], res[:, w, off:off + 1024], 1.0, ALU.add)
        P.dma(S["MODD"], res.re("o w n -> o (w n)"))
    P.barrier()


def ln_stats(P, xt, junk, stt_):
    P.memset(stt_, 0.0)
    P.act(junk, xt, AF.Identity, accum_out=stt_[:, 0:1])
    P.act(junk, xt, AF.Square, accum_out=stt_[:, 1:2])
    P.ts(stt_[:, 0:1], stt_[:, 0:1], 1.0 / D, ALU.mult)
    P.tt(stt_[:, 2:3], stt_[:, 0:1], stt_[:, 0:1], ALU.mult)
    P.stt(stt_[:, 2:3], stt_[:, 1:2], 1.0 / D, stt_[:, 2:3], ALU.mult, ALU.subtract)
    P.ts(stt_[:, 2:3], stt_[:, 2:3], LN_EPS, ALU.add)
    P.act(stt_[:, 2:3], stt_[:, 2:3], AF.Sqrt)
    P.recip(stt_[:, 3:4], stt_[:, 2:3])
    P.stt(stt_[:, 4:5], stt_[:, 0:1], -1.0, stt_[:, 3:4], ALU.mult, ALU.mult)


def stage_lnmod(P, S, src, shoff, scoff, uT):
    with ExitStack() as st:
        _stage_lnmod(P, S, src, shoff, scoff, uT, st)
    P.barrier()


def _stage_lnmod(P, S, src, shoff, scoff, uT, st):
    modb = {}
    for w in range(2):
        sh = P.sb([128, D], stack=st)
        sc = P.sb([128, D], stack=st)
        P.dma(sh, bc_rows(S["MODD"][:, w * 6144 + shoff: w * 6144 + shoff + D], D))
        P.dma(sc, bc_rows(S["MODD"][:, w * 6144 + scoff: w * 6144 + scoff + D], D))
        modb[w] = (sh, sc)
    xts = [P.sb([128, D], stack=st) for _ in range(2)]
    junk = P.sb([128, D], stack=st)
    xn = P.sb([128, D], stack=st)
    ub = P.sb([128, D], BF16, stack=st)
    sts = [P.sb([128, 8], stack=st) for _ in range(2)]
    pts = [P.ps([128, 8, 128], BF16, stack=st) for _ in range(2)]
    P.memset(uT[:, :, 0:1], 0.0)
    P.memset(uT[:, :, CTX + 1:CTX + 2], 0.0)
    P.memset(uT[:, :, UTW - 1:UTW], 0.0)
    for t in range(NT):
        xt = xts[t % 2]
        s_ = sts[t % 2]
        P.dma(xt, src[t * 128:(t + 1) * 128, :], eng="sp" if t % 2 == 0 else "pool")
        ln_stats(P, xt, junk, s_)
        P.act(xn, xt, AF.Identity, bias=s_[:, 4:5], scale=s_[:, 3:4])
        sh, sc = modb[1 if t < 2 else 0]
        P.tt(xn, xn, sc, ALU.mult)
        P.tt(ub, xn, sh, ALU.add)
        pt = pts[t % 2]
        for kc in range(8):
            P.transpose(pt[:, kc, :], ub[:, kc * 128:(kc + 1) * 128], S["identb"])
        c0 = ucol(t * 128)
        P.copy(uT[:, :, c0:c0 + 128], pt, eng="act")


TOKCH = [(1, 0, 256)] + [(258 + i * 512, 256 + i * 512, 512) for i in range(8)]


def wview(S, name, l, c0, n):
    return S[name][l, :, c0:c0 + n].re("(kc p) n -> p kc n", p=128)


def stage_inproj(P, S, l, uT):
    W = "w_in"
    with ExitStack() as st:
        wst = [P.sb([128, 8, 384], stack=st) for _ in range(2)]
        wb = [[P.sb([128, 8, 384], BF16, stack=st) for _ in range(3)] for _ in range(2)]
        tap = [P.sb([128, 3, 384], stack=st) for _ in range(2)]
        bia = [P.sb([128, 384], stack=st) for _ in range(2)]
        pss = [P.ps([128, 384], stack=st) for _ in range(3)]
        osb = [P.sb([128, 384], stack=st) for _ in range(3)]
        blocks = []
        for c in range(0, 768, 384):
            blocks.append(("hy", HY0 + c, c, 384))
        for c, n in ((0, 384), (384, 384), (768, 384), (1152, 64)):
            blocks.append(("rw", RW0 + c, c, n))
        for c, n in ((0, 128),):
            blocks.append(("vsw", SWV0, 0, 128))
        blocks.append(("vdf", DFV0, 0, 256))
        it = 0
        for bi, (kind, wc0, oc0, n) in enumerate(blocks):
            ws = wst[bi % 2]
            P.dma(ws[:, :, 0:n], wview(S, W, l, wc0, n), eng="sp" if bi % 2 == 0 else "pool")
            wbb = wb[bi % 2]
            tp = tap[bi % 2]
            if kind == "hy":
                for j in range(3):
                    P.dma(tp[:, j, 0:n], bc_rows(S["hy_conv_w"][l, j:j + 1, oc0:oc0 + n], n))
                P.dma(bia[bi % 2][:, 0:n], bc_rows(S["hy_conv_b"][l:l + 1, oc0:oc0 + n], n))
                for j in range(3):
                    for kc in range(8):
                        P.tt(wbb[j][:, kc, 0:n], ws[:, kc, 0:n], tp[:, j, 0:n], ALU.mult, eng="pool" if kc % 2 else "dve")
                nj = 3
                dst = S["HYP"]
            elif kind == "rw":
                P.dma(tp[:, 0, 0:n], bc_rows(S["rwkv_mu"][l:l + 1, oc0:oc0 + n], n))
                P.ts(tp[:, 1, 0:n], tp[:, 0, 0:n], -1.0, ALU.mult, 1.0, ALU.add)
                P.ts(tp[:, 2, 0:n], tp[:, 0, 0:n], 0.5, ALU.mult)
                for kc in range(8):
                    P.tt(wbb[1][:, kc, 0:n], ws[:, kc, 0:n], tp[:, 1, 0:n], ALU.mult, eng="pool" if kc % 2 else "dve")
                    P.tt(wbb[0][:, kc, 0:n], ws[:, kc, 0:n], tp[:, 2, 0:n], ALU.mult, eng="pool" if kc % 2 else "dve")
                nj = 3
                dst = S["RWP"]
            else:
                for kc in range(8):
                    P.copy(wbb[1][:, kc, 0:n], ws[:, kc, 0:n], eng="pool" if kc % 2 else "dve")
                nj = 1
                dst = S["VSW"] if kind == "vsw" else S["VDF"]
            for t in range(NT):
                ps = pss[it % 3]
                ob = osb[it % 3]
                it += 1
                c0 = ucol(t * 128)
                if nj == 3:
                    seq = [(0, -1), (1, 0), (2 if kind == "hy" else 0, 1)]
                else:
                    seq = [(1, 0)]
                k = 0
                tot = len(seq) * 8
                for (wj, sh) in seq:
                    for kc in range(8):
                        P.mm(ps[:, 0:n], uT[:, kc, c0 + sh:c0 + sh + 128], wbb[wj][:, kc, 0:n],
                             start=(k == 0), stop=(k == tot - 1))
                        k += 1
                if kind == "hy":
                    P.tt(ob[:, 0:n], ps[:, 0:n], bia[bi % 2][:, 0:n], ALU.add)
                else:
                    P.copy(ob[:, 0:n], ps[:, 0:n], eng="act")
                P.dma(dst[t * 128:(t + 1) * 128, oc0:oc0 + n], ob[:, 0:n], eng="sp" if it % 2 else "pool")
    P.barrier()
    with ExitStack() as st:
        NU = 14
        ws = P.sb([128, 8, 896], stack=st)
        wq = P.sb([128, 8, 896], BF16, stack=st)
        wsw = P.sb([128, 8, 896], BF16, stack=st)
        srcs = [(SWQ0, 256, 0), (SWK0, 128, 256), (DFQ0, 256, 384), (DFK0, 256, 640)]
        for (c0, n, o) in srcs:
            P.dma(ws[:, :, o:o + n], wview(S, W, l, c0, n))
        for kc in range(8):
            P.copy(wq[:, kc, :], ws[:, kc, :], eng="pool" if kc % 2 else "dve")
        P.dma(ws, wview(S, "w_swap", l, 0, 896), eng="pool")
        for kc in range(8):
            P.copy(wsw[:, kc, :], ws[:, kc, :], eng="pool" if kc % 2 else "dve")
        rts = [P.sb([64, 4, 512], stack=st) for _ in range(2)]
        psa = [P.ps([64, 512], stack=st) for _ in range(2)]
        psb = [P.ps([64, 512], stack=st) for _ in range(2)]
        t1 = [P.sb([64, 512], stack=st) for _ in range(2)]
        t2 = [P.sb([64, 512], stack=st) for _ in range(2)]
        ob = [P.sb([64, 512], BF16, stack=st) for _ in range(2)]
        it = 0
        for ci, (uc0, tok0, n) in enumerate(TOKCH):
            rt = rts[ci % 2]
            P.dma(rt[:, :, 0:n], S["rope"][:, :, tok0:tok0 + n].re("f d t -> d f t"))
            for u in range(NU):
                pa, pb = psa[it % 2], psb[it % 2]
                a1, a2, o_ = t1[it % 2], t2[it % 2], ob[it % 2]
                it += 1
                for kc in range(8):
                    P.mm(pa[:, 0:n], wq[:, kc, u * 64:(u + 1) * 64], uT[:, kc, uc0:uc0 + n], start=(kc == 0), stop=(kc == 7))
                for kc in range(8):
                    P.mm(pb[:, 0:n], wsw[:, kc, u * 64:(u + 1) * 64], uT[:, kc, uc0:uc0 + n], start=(kc == 0), stop=(kc == 7))
                tb = 0 if u < 6 else 2
                P.tt(a1[:, 0:n], pa[:, 0:n], rt[:, tb, 0:n], ALU.mult)
                P.tt(a2[:, 0:n], pb[:, 0:n], rt[:, tb + 1, 0:n], ALU.mult, eng="dve")
                P.tt(o_[:, 0:n], a1[:, 0:n], a2[:, 0:n], ALU.add, eng="pool")
                P.dma(S["QK"][u, :, tok0:tok0 + n], o_[:, 0:n], eng="sp" if it % 2 else "pool")
    P.barrier()
    with ExitStack() as st:
        wst = [P.sb([128, 8, 512], stack=st) for _ in range(2)]
        wbs = [P.sb([128, 8, 512], BF16, stack=st) for _ in range(2)]
        pss = [P.ps([128, 512], stack=st) for _ in range(3)]
        osb = [P.sb([128, 512], stack=st) for _ in range(3)]
        it = 0
        for bi in range(8):
            ws = wst[bi % 2]
            wb_ = wbs[bi % 2]
            P.dma(ws, wview(S, W, l, GT0 + bi * 512, 512), eng="sp" if bi % 2 == 0 else "pool")
            for kc in range(8):
                P.copy(wb_[:, kc, :], ws[:, kc, :], eng="pool" if kc % 2 else "dve")
            for cc in range(4):
                for (uc0, tok0, n) in TOKCH:
                    ps = pss[it % 3]
                    o_ = osb[it % 3]
                    it += 1
                    for kc in range(8):
                        P.mm(ps[:, 0:n], wb_[:, kc, cc * 128:(cc + 1) * 128], uT[:, kc, uc0:uc0 + n], start=(kc == 0), stop=(kc == 7))
                    P.act(o_[:, 0:n], ps[:, 0:n], AF.Sigmoid)
                    r0 = bi * 512 + cc * 128
                    P.dma(S["GT"][r0:r0 + 128, tok0:tok0 + n], o_[:, 0:n], eng="sp" if it % 2 else "pool")
    P.barrier()


def resid_ln(P, pss, hsrc, gb, lg, lb, dst_rows, tmp, stt_, res, eng_i):
    for hf in range(2):
        P.tt(tmp[:, hf * 512:(hf + 1) * 512], pss[hf], gb[:, hf * 512:(hf + 1) * 512], ALU.mult)
    P.stt(tmp, hsrc, ALPHA, tmp, ALU.mult, ALU.add, eng="pool")
    ln_stats(P, tmp, res, stt_)
    P.act(res, tmp, AF.Identity, bias=stt_[:, 4:5], scale=stt_[:, 3:4])
    P.tt(res, res, lg, ALU.mult, eng="pool")
    P.tt(res, res, lb, ALU.add, eng="pool")
    for (d, r0, r1) in dst_rows:
        P.dma(d, res[r0:r1, :], eng="sp" if eng_i % 2 else "pool")


def load_bc(P, S, st, name, l, n=D):
    t = P.sb([128, n], stack=st)
    P.dma(t, bc_rows(S[name][l:l + 1, :], n))
    return t


def stage_merge(P, S, l, src):
    with ExitStack() as st:
        wbr = P.sb([128, 8, D], BF16, stack=st)
        wo = P.sb([128, 8, D], BF16, stack=st)
        wst = P.sb([128, 8, D], stack=st)
        P.dma(wst, S["w_branch"][l].re("j (kc p) n -> p (j kc) n", p=128))
        for i in range(8):
            P.copy(wbr[:, i, :], wst[:, i, :], eng="pool" if i % 2 else "dve")
        P.dma(wst, S["w_out"][l].re("(kc p) n -> p kc n", p=128))
        for i in range(8):
            P.copy(wo[:, i, :], wst[:, i, :], eng="pool" if i % 2 else "dve")
        gbc = []
        for w in range(2):
            g = P.sb([128, D], stack=st)
            P.dma(g, bc_rows(S["MODD"][:, w * 6144 + 2048: w * 6144 + 3072], D))
            gbc.append(g)
        lg = load_bc(P, S, st, "ln1_g", l)
        lb = load_bc(P, S, st, "ln1_b", l)
        yts = [P.sb([128, 8, 512], BF16, stack=st) for _ in range(2)]
        gts = [P.sb([128, 4, 512], stack=st) for _ in range(2)]
        accT = P.sb([128, 8, 512], BF16, stack=st)
        acc = P.sb([128, 512], stack=st)
        prod = [P.sb([128, 512], stack=st) for _ in range(2)]
        pss = [P.ps([128, 512], stack=st) for _ in range(4)]
        pmix = [P.ps([128, 512], stack=st) for _ in range(2)]
        hts = [P.sb([128, D], stack=st) for _ in range(2)]
        tmp = P.sb([128, D], stack=st)
        res = P.sb([128, D], stack=st)
        sts = [P.sb([128, 8], stack=st) for _ in range(2)]
        GTv = S["GT"].re("(j dc p) t -> dc p j t", j=4, dc=8, p=128)
        YTv = S["YT"].re("j (kc p) t -> p (j kc) t", p=128)
        it = 0
        ti = 0
        for ci, (uc0, tok0, n) in enumerate(TOKCH):
            yt = yts[ci % 2]
            P.dma(yt[:, :, 0:n], YTv[:, :, tok0:tok0 + n])
            for dc in range(8):
                gt = gts[dc % 2]
                P.dma(gt[:, :, 0:n], GTv[dc][:, :, tok0:tok0 + n], eng="pool")
                for j in range(4):
                    ps = pss[it % 4]
                    it += 1
                    for k2 in range(2):
                        P.mm(ps[:, 0:n], wbr[:, j * 2 + k2, dc * 128:(dc + 1) * 128], yt[:, j * 2 + k2, 0:n],
                             start=(k2 == 0), stop=(k2 == 1))
                    if j == 0:
                        P.tt(acc[:, 0:n], ps[:, 0:n], gt[:, j, 0:n], ALU.mult)
                    else:
                        pr = prod[j % 2]
                        P.tt(pr[:, 0:n], ps[:, 0:n], gt[:, j, 0:n], ALU.mult)
                        if j < 3:
                            P.tt(acc[:, 0:n], acc[:, 0:n], pr[:, 0:n], ALU.add, eng="pool")
                        else:
                            P.tt(accT[:, dc, 0:n], acc[:, 0:n], pr[:, 0:n], ALU.add, eng="pool")
            for tt_ in range(n // 128):
                tok = tok0 + tt_ * 128
                ht = hts[ti % 2]
                P.dma(ht, src[tok:tok + 128, :])
                for hf in range(2):
                    for dc in range(8):
                        P.mm(pmix[hf], accT[:, dc, tt_ * 128:(tt_ + 1) * 128], wo[:, dc, hf * 512:(hf + 1) * 512],
                             start=(dc == 0), stop=(dc == 7))
                resid_ln(P, pmix, ht, gbc[1 if tok < CTX else 0], lg, lb,
                         [(S["H1"][tok:tok + 128, :], 0, 128)], tmp, sts[ti % 2], res, ti)
                ti += 1
    P.barrier()


def stage_ffn(P, S, l, last):
    FC = 22
    with ExitStack() as st:
        uT = P.sb([128, 8, UTW], BF16, stack=st)
        stage_lnmod(P, S, S["H1"], 3072, 4096, uT)
        with ExitStack() as st2:
            wst = [P.sb([128, 8, 256], stack=st2) for _ in range(2)]
            wbs = [P.sb([128, 8, 256], BF16, stack=st2) for _ in range(2)]
            taps = [P.sb([128, 2, 4], stack=st2) for _ in range(2)]
            hT = [P.sb([128, UTW], stack=st2) for _ in range(2)]
            cT = [P.sb([128, UTW], stack=st2) for _ in range(2)]
            gT = [P.sb([128, UTW], BF16, stack=st2) for _ in range(2)]
            pss = [P.ps([128, 512], stack=st2) for _ in range(4)]
            for ab in range(2):
                P.memset(hT[ab], 0.0)
                P.memset(cT[ab], 0.0)
            it = 0
            for fi in range(FC):
                ws, wb_, tp = wst[fi % 2], wbs[fi % 2], taps[fi % 2]
                for ab in range(2):
                    c0 = ab * 2816 + fi * 128
                    P.dma(ws[:, :, ab * 128:(ab + 1) * 128], wview(S, "ffn_w_up", l, c0, 128), eng="sp" if ab else "pool")
                    P.dma(tp[:, ab, 0:3], S["ffn_conv_w"][l, :, c0:c0 + 128].re("j p -> p j"), allow_slow_non_contiguous=True)
                    P.dma(tp[:, ab, 3:4], S["ffn_conv_b"][l:l + 1, c0:c0 + 128].re("o p -> p o"), allow_slow_non_contiguous=True)
                for kc in range(8):
                    P.copy(wb_[:, kc, :], ws[:, kc, :], eng="pool" if kc % 2 else "dve")
                for ab in range(2):
                    h = hT[ab]
                    for (uc0, tok0, n) in TOKCH:
                        ps = pss[it % 4]
                        it += 1
                        for kc in range(8):
                            P.mm(ps[:, 0:n], wb_[:, kc, ab * 128:(ab + 1) * 128], uT[:, kc, uc0:uc0 + n], start=(kc == 0), stop=(kc == 7))
                        P.copy(h[:, uc0:uc0 + n], ps[:, 0:n], eng="act")
                    c = cT[ab]
                    e = "dve" if ab == 0 else "pool"
                    Wd = UTW - 2
                    P.ts(c[:, 1:1 + Wd], h[:, 0:Wd], tp[:, ab, 0:1], ALU.mult, tp[:, ab, 3:4], ALU.add, eng=e)
                    P.stt(c[:, 1:1 + Wd], h[:, 1:1 + Wd], tp[:, ab, 1:2], c[:, 1:1 + Wd], ALU.mult, ALU.add, eng=e)
                    P.stt(c[:, 1:1 + Wd], h[:, 2:2 + Wd], tp[:, ab, 2:3], c[:, 1:1 + Wd], ALU.mult, ALU.add, eng=e)
                P.act(cT[0], cT[0], AF.Silu)
                g = gT[fi % 2]
                P.tt(g, cT[0], cT[1], ALU.mult)
                P.dma(S["GFT"][fi * 128:(fi + 1) * 128, :], g, eng="sp")
        P.barrier()
    P.barrier()
    with ExitStack() as st:
        wd = P.sb([128, FC, D], BF16, stack=st)
        wst = [P.sb([128, 2, D], stack=st) for _ in range(2)]
        for i in range(FC // 2):
            P.dma(wst[i % 2], S["ffn_w_down"][l, i * 256:(i + 1) * 256, :].re("(k p) n -> p k n", p=128), eng="sp" if i % 2 else "pool")
            P.copy(wd[:, 2 * i, :], wst[i % 2][:, 0, :], eng="dve")
            P.copy(wd[:, 2 * i + 1, :], wst[i % 2][:, 1, :], eng="pool")
        gbc = []
        for w in range(2):
            g = P.sb([128, D], stack=st)
            P.dma(g, bc_rows(S["MODD"][:, w * 6144 + 5120: w * 6144 + 6144], D))
            gbc.append(g)
        lg = load_bc(P, S, st, "ln2_g", l)
        lb = load_bc(P, S, st, "ln2_b", l)
        gts = [P.sb([128, FC, 128], BF16, stack=st) for _ in range(2)]
        hts = [P.sb([128, D], stack=st) for _ in range(2)]
        pmix = [P.ps([128, 512], stack=st) for _ in range(4)]
        tmp = P.sb([128, D], stack=st)
        res = P.sb([128, D], stack=st)
        sts = [P.sb([128, 8], stack=st) for _ in range(2)]
        GFv = S["GFT"].re("(fc p) t -> p fc t", p=128)
        for t in range(NT):
            tok = t * 128
            c0 = ucol(tok)
            gt = gts[t % 2]
            P.dma(gt, GFv[:, :, c0:c0 + 128], eng="pool")
            ht = hts[t % 2]
            P.dma(ht, S["H1"][tok:tok + 128, :])
            pm = pmix[(t % 2) * 2:(t % 2) * 2 + 2]
            for hf in range(2):
                for fc in range(FC):
                    P.mm(pm[hf], gt[:, fc, :], wd[:, fc, hf * 512:(hf + 1) * 512], start=(fc == 0), stop=(fc == FC - 1))
            if last:
                dsts = [(S["out"][tok - CTX:tok - CTX + 128, :], 0, 128)] if tok >= CTX else []
            else:
                dsts = [(S["H2"][tok:tok + 128, :], 0, 128)]
            if dsts:
                resid_ln(P, pm, ht, gbc[1 if tok < CTX else 0], lg, lb, dsts, tmp, sts[t % 2], res, t)
    P.barrier()


def tok2feat(P, S, src, j):
    with ExitStack() as st:
        xs = [P.sb([128, 256], stack=st) for _ in range(2)]
        xb = [P.sb([128, 256], BF16, stack=st) for _ in range(2)]
        pt = [P.ps([128, 2, 128], BF16, stack=st) for _ in range(2)]
        ob = [P.sb([128, 2, 128], BF16, stack=st) for _ in range(2)]
        for t in range(NT):
            i = t % 2
            P.dma(xs[i], src[t * 128:(t + 1) * 128, :], eng="sp" if i else "pool")
            P.copy(xb[i], xs[i], eng="pool")
            for kc in range(2):
                P.transpose(pt[i][:, kc, :], xb[i][:, kc * 128:(kc + 1) * 128], S["identb"])
            P.copy(ob[i], pt[i], eng="act")
            P.dma(S["YT"][j].re("(kc p) t -> p kc t", p=128)[:, :, t * 128:(t + 1) * 128], ob[i], eng="sp" if i else "pool")
    P.barrier()


def load_vext(P, S, st, src, nh):
    ve = P.sb([128, NT, nh, 65], BF16, stack=st)
    P.memset(ve.re("p t h d -> p (t h d)"), 1.0)
    vs = [P.sb([128, nh * 64], stack=st) for _ in range(2)]
    for t in range(NT):
        i = t % 2
        P.dma(vs[i], src[t * 128:(t + 1) * 128, :], eng="sp" if i else "pool")
        P.copy(ve[:, t, :, 0:64], vs[i].re("p (h d) -> p h d", h=nh), eng="pool" if i else "dve")
    return ve


def stage_swa(P, S, l):
    with ExitStack() as st:
        qk = P.sb([64, 6, NTOK], BF16, stack=st)
        for u in range(6):
            P.dma(qk[:, u, :], S["QK"][u], eng="sp" if u % 2 else "pool")
        ve = load_vext(P, S, st, S["VSW"], 2)
        mk32 = P.sb([128, 2, 512], stack=st)
        P.dma(mk32, S["maskLR"].re("m p f -> p m f"))
        mk = P.sb([128, 2, 512], BF16, stack=st)
        P.copy(mk, mk32)
        es = P.sb([128, 4], stack=st)
        P.dma(es, bc_rows(S["swa_sink"][l:l + 1, :], 4))
        P.act(es, es, AF.Exp)
        pS = [P.ps([128, 4, 128], stack=st) for _ in range(2)]
        pO = [P.ps([128, 4, 65], stack=st) for _ in range(2)]
        E = [P.sb([128, 5, 4, 128], BF16, stack=st) for _ in range(2)]
        den = [P.sb([128, 4], stack=st) for _ in range(2)]
        yt = [P.sb([128, 4, 64], stack=st) for _ in range(2)]
        it = 0
        for qb in range(NT):
            if qb < 2:
                kts = [(0, None), (1, None)]
            else:
                kts = []
                if qb - 1 >= 2:
                    kts.append((qb - 1, 0))
                kts.append((qb, None))
                if qb + 1 < NT:
                    kts.append((qb + 1, 1))
                kts += [(0, None), (1, None)]
            e_ = E[qb % 2]
            for ki, (kt, m) in enumerate(kts):
                ps = pS[it % 2]
                it += 1
                for h in range(4):
                    P.mm(ps[:, h, :], qk[:, 4 + h // 2, kt * 128:(kt + 1) * 128], qk[:, h, qb * 128:(qb + 1) * 128])
                P.act(e_[:, ki].re("p h q -> p (h q)"), ps.re("p h q -> p (h q)"), AF.Exp, scale=0.125)
                if m is not None:
                    P.tt(e_[:, ki].re("p h q -> p (h q)"), e_[:, ki].re("p h q -> p (h q)"), mk[:, m, :], ALU.mult)
            po = pO[qb % 2]
            for h in range(4):
                for ki, (kt, m) in enumerate(kts):
                    P.mm(po[:, h, :], e_[:, ki, h, :], ve[:, kt, h // 2, :], start=(ki == 0), stop=(ki == len(kts) - 1))
            d_ = den[qb % 2]
            y_ = yt[qb % 2]
            P.tt(d_, po[:, :, 64], es, ALU.add)
            P.recip(d_, d_)
            for h in range(4):
                P.ts(y_[:, h, :], po[:, h, 0:64], d_[:, h:h + 1], ALU.mult)
            P.dma(S["Ysw"][qb * 128:(qb + 1) * 128, :], y_.re("p h d -> p (h d)"), eng="sp" if qb % 2 else "pool")
    P.barrier()
    tok2feat(P, S, S["Ysw"], 1)


def stage_diff(P, S, l, lam_init):
    with ExitStack() as st:
        qkc = [P.sb([32, 8, NTOK], BF16, stack=st) for _ in range(2)]
        for u in range(8):
            for c in range(2):
                P.dma(qkc[c][:, u, :], S["QK"][6 + u, c * 32:(c + 1) * 32, :], eng="sp" if u % 2 else "pool")
        ve = load_vext(P, S, st, S["VDF"], 4)
        idf = P.sb([128, 128], stack=st)
        P.dma(idf, S["ident"])
        lv = P.sb([128, 4, 32], stack=st)
        for i, nm in enumerate(("diff_lq1", "diff_lk1", "diff_lq2", "diff_lk2")):
            P.dma(lv[:, i, :], bc_rows(S[nm][l:l + 1, :], 32))
        lt = P.sb([128, 8], stack=st)
        pr = P.sb([128, 2, 32], stack=st)
        P.tt(pr[:, 0, :], lv[:, 0, :], lv[:, 1, :], ALU.mult)
        P.tt(pr[:, 1, :], lv[:, 2, :], lv[:, 3, :], ALU.mult)
        P.reduce(lt[:, 0:2], pr, ALU.add)
        P.act(lt[:, 0:2], lt[:, 0:2], AF.Exp)
        P.tt(lt[:, 2:3], lt[:, 1:2], lt[:, 0:1], ALU.subtract)
        P.ts(lt[:, 3:4], lt[:, 2:3], -lam_init, ALU.add)
        gsub = P.sb([128, 64], stack=st)
        P.dma(gsub, bc_rows(S["diff_subln_g"][l:l + 1, :], 64))
        P.ts(gsub, gsub, 1.0 - lam_init, ALU.mult)
        pS = [P.ps([128, 512], stack=st) for _ in range(2)]
        pO = [P.ps([65, 512], stack=st) for _ in range(2)]
        pT = [P.ps([128, 2, 65], stack=st) for _ in range(2)]
        E = [P.sb([128, 512], BF16, stack=st) for _ in range(3)]
        oT = [P.sb([65, 512], stack=st) for _ in range(2)]
        tm = [P.sb([128, 8], stack=st) for _ in range(2)]
        a_ = [P.sb([128, 64], stack=st) for _ in range(2)]
        w_ = [P.sb([128, 64], stack=st) for _ in range(2)]
        jk = P.sb([128, 64], stack=st)
        y_ = [P.sb([128, 64], stack=st) for _ in range(2)]
        it = 0
        ti = 0
        sc = 32 ** -0.5
        for h in range(4):
            for (uc0, tok0, n) in TOKCH:
                kts = [0, 1] if tok0 < CTX else list(range(NT))
                for c in range(2):
                    po = pO[c]
                    pend = None
                    for ki, kt in enumerate(kts):
                        ps = pS[it % 2]
                        e_ = E[it % 3]
                        it += 1
                        P.mm(ps[:, 0:n], qkc[c][:, 4 + h, kt * 128:(kt + 1) * 128],
                             qkc[c][:, h, tok0:tok0 + n])
                        if pend is not None:
                            pk_, pe_ = pend
                            P.mm(po[:, 0:n], ve[:, kts[pk_], h, :], pe_[:, 0:n], start=(pk_ == 0), stop=False)
                        P.act(e_[:, 0:n], ps[:, 0:n], AF.Exp, scale=sc)
                        pend = (ki, e_)
                    pk_, pe_ = pend
                    P.mm(po[:, 0:n], ve[:, kts[pk_], h, :], pe_[:, 0:n], start=(pk_ == 0), stop=True)
                    P.copy(oT[c][:, 0:n], po[:, 0:n], eng="act")
                for tt_ in range(n // 128):
                    i = ti % 2
                    ti += 1
                    pt = pT[i]
                    for c in range(2):
                        P.mm(pt[:, c, :], oT[c][:, tt_ * 128:(tt_ + 1) * 128], idf[0:65, 0:65])
                    t_ = tm[i]
                    P.recip(t_[:, 0:2], pt[:, :, 64])
                    P.tt(t_[:, 2:3], t_[:, 1:2], lt[:, 3:4], ALU.mult)
                    P.ts(a_[i], pt[:, 0, 0:64], t_[:, 0:1], ALU.mult)
                    P.stt(w_[i], pt[:, 1, 0:64], t_[:, 2:3], a_[i], ALU.mult, ALU.add)
                    P.memset(t_[:, 4:5], 0.0)
                    P.act(jk, w_[i], AF.Square, accum_out=t_[:, 4:5])
                    P.ts(t_[:, 5:6], t_[:, 4:5], 1.0 / 64, ALU.mult, 1e-5, ALU.add)
                    P.act(t_[:, 5:6], t_[:, 5:6], AF.Sqrt)
                    P.recip(t_[:, 6:7], t_[:, 5:6])
                    P.stt(y_[i], w_[i], t_[:, 6:7], gsub, ALU.mult, ALU.mult)
                    tok = tok0 + tt_ * 128
                    P.dma(S["Ydf"][tok:tok + 128, h * 64:(h + 1) * 64], y_[i], eng="sp" if i else "pool")
    P.barrier()
    tok2feat(P, S, S["Ydf"], 3)

TWO_PI = 6.283185307179586


def hy_filter(P, S, l, L, En, Tn, G):
    N2 = 2 * L
    cw = min(512, N2)
    with ExitStack() as st:
        w1 = P.sb([33, 64], stack=st)
        w2 = P.sb([64, 64], stack=st)
        w3 = P.sb([64, 1024], stack=st)
        P.dma(w1, S["hy_f_w1"][l])
        P.dma(w2, S["hy_f_w2"][l])
        P.dma(w3, S["hy_f_w3"][l])
        fr = P.sb([64, 8], stack=st)
        P.dma(fr[:, 0:1], S["hy_f_freq"][l:l + 1, :].re("o p -> p o"), allow_slow_non_contiguous=True)
        P.dma(fr[:, 1:2], S["hy_f_b1"][l:l + 1, :].re("o p -> p o"), allow_slow_non_contiguous=True)
        P.dma(fr[:, 2:3], S["hy_f_b2"][l:l + 1, :].re("o p -> p o"), allow_slow_non_contiguous=True)
        P.ts(fr[:, 3:4], fr[:, 0:1], 1.0 / TWO_PI, ALU.mult)
        P.tt(fr[:, 4:5], fr[:, 3:4], fr[:, 1:2], ALU.mult)
        P.ts(fr[:, 4:5], fr[:, 4:5], 8.0, ALU.add)
        P.tt(fr[:, 5:6], fr[:, 3:4], fr[:, 2:3], ALU.mult)
        P.ts(fr[:, 5:6], fr[:, 5:6], 8.0, ALU.add)
        negpi = P.sb([128, 1], stack=st)
        P.memset(negpi, 1.5707963267948966)
        h2 = P.sb([64, N2], stack=st)
        Et = [P.sb([33, cw], stack=st) for _ in range(2)]
        ps1 = [P.ps([64, cw], stack=st) for _ in range(2)]
        ps2 = [P.ps([64, cw], stack=st) for _ in range(2)]
        v1 = [P.sb([64, cw], stack=st) for _ in range(2)]
        h1 = [P.sb([64, cw], stack=st) for _ in range(2)]
        vi = [P.sb([64, cw], mybir.dt.int32, stack=st) for _ in range(2)]
        vf = [P.sb([64, cw], stack=st) for _ in range(2)]
        sa = [P.sb([64, cw], stack=st) for _ in range(2)]
        sb_ = [P.sb([64, cw], stack=st) for _ in range(2)]

        def sin_red(ps, s2, out, i):
            P.ts(v1[i], ps, fr[:, 3:4], ALU.mult, s2, ALU.add)
            P.copy(vi[i], v1[i])
            P.copy(vf[i], vi[i])
            P.tt(v1[i], v1[i], vf[i], ALU.subtract)
            P.act(sa[i], v1[i], AF.Sin, scale=3.141592653589793)
            P.act(sb_[i], v1[i], AF.Sin, bias=negpi[0:64, :], scale=-3.141592653589793)
            P.stt(out, sa[i], 2.0, sb_[i], ALU.mult, ALU.mult)

        for ch in range(N2 // cw):
            i = ch % 2
            sl = slice(ch * cw, (ch + 1) * cw)
            P.dma(Et[i], S[En][:, sl])
            P.mm(ps1[i], w1, Et[i])
            sin_red(ps1[i], fr[:, 4:5], h1[i], i)
            P.mm(ps2[i], w2, h1[i])
            sin_red(ps2[i], fr[:, 5:6], h2[:, sl], i)
        T128 = P.sb([128, N2], stack=st)
        P.dma(T128, bc_rows(S[Tn], N2))
        nd = P.sb([128, 2], stack=st)
        P.dma(nd, S["hyND"].re("(c p) o -> p (c o)", p=128), allow_slow_non_contiguous=True)
        fw = min(512, L)
        filt = [P.sb([128, L], stack=st) for _ in range(2)]
        fb = [P.sb([128, L], BF16, stack=st) for _ in range(2)]
        junk = P.sb([128, L], stack=st)
        asum = P.sb([128, 4], stack=st)
        psf = [P.ps([128, fw], stack=st) for _ in range(2)]
        dec = [P.sb([128, fw], stack=st) for _ in range(2)]
        it = 0
        for o in range(2):
            for chalf in range(2):
                P.memset(asum, 0.0)
                for dr in range(2):
                    q = o * 4 + dr * 2 + chalf
                    pos0 = L if dr == 0 else 0
                    for ch in range(L // fw):
                        i = it % 2
                        it += 1
                        sl = slice(pos0 + ch * fw, pos0 + (ch + 1) * fw)
                        P.mm(psf[i], w3[:, q * 128:(q + 1) * 128], h2[:, sl])
                        P.act(dec[i], T128[:, sl], AF.Exp, scale=nd[:, chalf:chalf + 1])
                        P.tt(filt[dr][:, ch * fw:(ch + 1) * fw], psf[i], dec[i], ALU.mult)
                    P.act(junk, filt[dr], AF.Abs, accum_out=asum[:, dr:dr + 1])
                P.tt(asum[:, 2:3], asum[:, 0:1], asum[:, 1:2], ALU.add)
                P.recip(asum[:, 3:4], asum[:, 2:3])
                for dr in range(2):
                    P.ts(fb[dr], filt[dr], asum[:, 3:4], ALU.mult, eng="dve" if dr else "pool")
                rows = slice(chalf * 128, (chalf + 1) * 128)
                P.dma(G[o, rows, 0:L - 1], fb[1][:, 0:L - 1], eng="sp")
                P.dma(G[o, rows, L - 1:2 * L - 1], fb[0][:, 0:L], eng="pool")
    P.barrier()


def hy_conv_seq(P, S, l, tok0, L, G, GL):
    with ExitStack() as st:
        for _ in hy_conv_gen(P, S, l, tok0, L, G, GL, st):
            pass
    P.barrier()


def hy_conv_gen(P, S, l, tok0, L, G, GL, st, nhk=3):
    NB = L // 128
    ND_ = 2 * NB - 1
    HKW = ND_ * 128
    Gt = G.ap.tensor
    if True:
        jf = P.sb([128, 128], stack=st)
        Jb = P.sb([128, 128], BF16, stack=st)
        P.dma(jf, S["antiid"])
        P.copy(Jb, jf)
        A = P.sb([128, NB, 256], stack=st)
        B = P.sb([128, NB, 256], stack=st)
        Vb = P.sb([128, NB, 256], BF16, stack=st)
        Zr = P.sb([128, NB, 256], BF16, stack=st)
        b01 = P.sb([128, 2, 256], stack=st)
        for o in range(2):
            P.dma(b01[:, o, :], bc_rows(S["hy_bias"][l, o:o + 1, :], 256))
        src = S["HYP"][tok0:tok0 + L, :].re("(a r) c -> r a c", r=128)
        P.dma(A, src[:, :, 0:256], eng="sp")
        P.dma(B, src[:, :, 256:512], eng="pool")
        hks = [P.sb([128, HKW], BF16, stack=st) for _ in range(nhk)]
        psJ = [P.ps([128, 512], stack=st) for _ in range(2)]
        psY = [P.ps([128, 16, NB], stack=st) for _ in range(2)]
        tmp = [P.sb([128, NB, 16], stack=st) for _ in range(1)]
        Vf = Vb.re("p a c -> p (a c)")
        Zf = Zr.re("p a c -> p (a c)")
        ds = [0] + [d for d in range(-(NB - 1), NB) if d != 0]
        yield
        for o in range(2):
            inp = A if o == 0 else B
            for a in range(NB):
                P.copy(Vb[:, a, :], inp[:, a, :], eng="pool" if a % 2 else "dve")
            for ch in range(NB * 256 // 512):
                pj = psJ[ch % 2]
                P.mm(pj, Jb, Vf[:, ch * 512:(ch + 1) * 512])
                P.copy(Zf[:, ch * 512:(ch + 1) * 512], pj, eng="act")
            for a in range(NB):
                P.tt(inp[:, a, :], inp[:, a, :], b01[:, o, :], ALU.mult, eng="pool")
            if o == 1:
                P.dma(A, src[:, :, 512:768], eng="sp")
            oth = B if o == 0 else A
            for cg in range(16):
                py = psY[cg % 2]
                for ci in range(16):
                    c = cg * 16 + ci
                    hk = hks[c % nhk]
                    hv = V(bass.AP(tensor=Gt, offset=(o * 256 + c) * GL, ap=[[1, 128], [1, HKW]]), G.key)
                    P.dma(hk, hv, eng="sp" if c % 2 else "pool")
                    for di, d in enumerate(ds):
                        a_lo, a_hi = max(0, d), min(NB - 1, NB - 1 + d)
                        P.mm(py[:, ci, a_lo:a_hi + 1], hk[:, (d + NB - 1) * 128:(d + NB) * 128],
                             Zr[:, a_lo - d:a_hi - d + 1, c], start=(di == 0), stop=(di == len(ds) - 1))
                        if di % 4 == 3:
                            yield
                cs = slice(cg * 16, (cg + 1) * 16)
                tp = tmp[0]
                P.tt(tp, py.re("p c a -> p a c"), inp[:, :, cs], ALU.add)
                P.tt(oth[:, :, cs], tp, oth[:, :, cs], ALU.mult, eng="pool")
        P.dma(S["Yhy"][tok0:tok0 + L, :].re("(a r) c -> r a c", r=128), A, eng="sp")
    yield


def stage_hyena(P, S, l):
    hy_filter(P, S, l, SEQ, "hyE_x", "hyT_x", S["Gx"])
    hy_filter(P, S, l, CTX, "hyE_c", "hyT_c", S["Gc"])
    hy_conv_seq(P, S, l, CTX, SEQ, S["Gx"], 2 * SEQ)
    hy_conv_seq(P, S, l, 0, CTX, S["Gc"], 2 * CTX)
    tok2feat(P, S, S["Yhy"], 0)


def stage_hyena_rwkv(P, S, l):
    hy_filter(P, S, l, SEQ, "hyE_x", "hyT_x", S["Gx"])
    hy_filter(P, S, l, CTX, "hyE_c", "hyT_c", S["Gc"])
    stage_rwkv_prep(P, S, l)
    with ExitStack() as st:
        gen = hy_conv_gen(P, S, l, CTX, SEQ, S["Gx"], 2 * SEQ, st, nhk=3)
        next(gen)
        stage_rwkv_scan(P, S, l, filler=gen, st_outer=st)
        for _ in gen:
            pass
    P.barrier()
    hy_conv_seq(P, S, l, 0, CTX, S["Gc"], 2 * CTX)
    tok2feat(P, S, S["Yhy"], 0)
    stage_rwkv_out(P, S, l)

RCH = 16


def s0_bwd(t):
    return 128 - 128 * t if t < 2 else 4480 - 128 * t


def stage_rwkv_prep(P, S, l):
    with ExitStack() as st:
        idf = P.sb([128, 128], stack=st)
        jf = P.sb([128, 128], stack=st)
        P.dma(idf, S["ident"])
        P.dma(jf, S["antiid"])
        zt = P.sb([128, 8704], stack=st)
        P.memset(zt, 0.0)
        vz = S["VBD"].re("d h s c -> (d h s c)").re("(p f) -> p f", p=128)
        for i in range(8):
            P.dma(vz[:, i * 8704:(i + 1) * 8704], zt, eng="sp" if i % 2 else "pool")
        w2 = [P.sb([64, 256], stack=st) for _ in range(2)]
        g2 = [P.sb([128, 256], stack=st) for _ in range(2)]
        a2 = P.sb([64, 256], stack=st)
        for d in range(2):
            P.dma(w2[d], S["rwkv_w2"][l, d])
            P.dma(g2[d], S["rwkv_g2"][l, d])
        P.dma(a2, S["rwkv_a2"][l])
        w0 = [P.sb([128, 256], stack=st) for _ in range(2)]
        for d in range(2):
            P.dma(w0[d], bc_rows(S["rwkv_w0"][l, d:d + 1, :], 256))
        a0 = load_bc(P, S, st, "rwkv_a0", l, 256)
        kkb = load_bc(P, S, st, "rwkv_kk", l, 256)
        kab = load_bc(P, S, st, "rwkv_ka", l, 256)
        omka = P.sb([128, 256], stack=st)
        P.ts(omka, kab, -1.0, ALU.mult, 1.0, ALU.add)
        rkb = P.sb([128, 256], stack=st)
        P.dma(rkb, bc_rows(S["rwkv_rk"][l:l + 1].re("o h d -> o (h d)"), 256))
        X = [P.sb([128, 1216], stack=st) for _ in range(2)]
        th = P.sb([128, 128], stack=st)
        sg = P.sb([128, 256], stack=st)
        pT = [P.ps([128, 128], stack=st) for _ in range(2)]
        tT = [P.sb([128, 128], stack=st) for _ in range(5)]
        pP = [P.ps([128, 256], stack=st) for _ in range(2)]
        wd = [P.sb([128, 256], stack=st) for _ in range(2)]
        a_ = P.sb([128, 256], stack=st)
        gd = [P.sb([128, 256], stack=st) for _ in range(2)]
        kk = P.sb([128, 256], stack=st)
        sq = P.sb([128, 256], stack=st)
        s4 = P.sb([128, 16], stack=st)
        an = P.sb([128, 256], stack=st)
        b_ = P.sb([128, 256], stack=st)
        kp = P.sb([128, 256], stack=st)
        t1 = P.sb([128, 256], stack=st)
        bon = P.sb([128, 256], stack=st)
        pK = [P.ps([64, 4, 128], stack=st) for _ in range(2)]
        kst = [P.sb([64, 128, 4], stack=st) for _ in range(2)]
        rv = [P.sb([128, 256], stack=st) for _ in range(2)]
        it = 0
        for t in range(NT):
            x = X[t % 2]
            P.dma(x, S["RWP"][t * 128:(t + 1) * 128, :], eng="sp" if t % 2 else "pool")
            r, k, v = x[:, 0:256], x[:, 256:512], x[:, 512:768]
            P.act(th, x[:, 768:896], AF.Tanh)
            P.act(sg, x[:, 960:1216], AF.Sigmoid)
            srcs = [(th[:, 0:64], 64), (th[:, 64:128], 64), (x[:, 896:960], 64), (sg[:, 0:128], 128), (sg[:, 128:256], 128)]
            for i, (sv, m) in enumerate(srcs):
                pt = pT[i % 2]
                P.mm(pt[0:m, :], sv, idf)
                P.copy(tT[i][0:m, :], pt[0:m, :], eng="act")
            for d in range(2):
                pp = pP[d]
                P.mm(pp, tT[d][0:64, :], w2[d])
                P.tt(wd[d], pp, w0[d], ALU.add)
                P.act(wd[d], wd[d], AF.Sigmoid)
                P.act(wd[d], wd[d], AF.Exp, scale=-0.6065306597126334)
            pp = pP[0]
            P.mm(pp, tT[2][0:64, :], a2)
            P.tt(a_, pp, a0, ALU.add)
            P.act(a_, a_, AF.Sigmoid)
            for d in range(2):
                pp = pP[1 - d]
                P.mm(pp, tT[3 + d], g2[d])
                P.copy(gd[d], pp, eng="act")
                P.dma(S["GFB"][d, t * 128:(t + 1) * 128, :], gd[d], eng="sp")
            P.tt(kk, k, kkb, ALU.mult)
            P.tt(sq, kk, kk, ALU.mult, eng="pool")
            P.reduce(s4[:, 0:4], sq.re("p (h d) -> p h d", h=4), ALU.add)
            P.act(s4[:, 0:4], s4[:, 0:4], AF.Sqrt)
            P.ts(s4[:, 0:4], s4[:, 0:4], 1e-12, ALU.max)
            P.recip(s4[:, 4:8], s4[:, 0:4])
            P.ts(s4[:, 4:8], s4[:, 4:8], -1.0, ALU.mult)
            for h in range(4):
                P.ts(an[:, h * 64:(h + 1) * 64], kk[:, h * 64:(h + 1) * 64], s4[:, 4 + h:5 + h], ALU.mult)
            P.stt(b_, an, -1.0, a_, ALU.mult, ALU.mult)
            P.tt(t1, a_, kab, ALU.mult, eng="pool")
            P.tt(t1, t1, omka, ALU.add, eng="pool")
            P.tt(kp, k, t1, ALU.mult, eng="pool")
            P.tt(t1, r, kp, ALU.mult, eng="pool")
            P.tt(t1, t1, rkb, ALU.mult, eng="pool")
            P.reduce(s4[:, 8:12], t1.re("p (h d) -> p h d", h=4), ALU.add)
            for h in range(4):
                P.ts(bon[:, h * 64:(h + 1) * 64], v[:, h * 64:(h + 1) * 64], s4[:, 8 + h:9 + h], ALU.mult)
            P.dma(S["BON"][t * 128:(t + 1) * 128, :], bon, eng="sp")
            sf = t * 128
            sb_ = s0_bwd(t)
            for (src, name, dirs) in ((an, "ANT", (0, 1)), (r, "RT", (0, 1)), (wd[0], "WT", (0,)), (wd[1], "WT", (1,))):
                for d in dirs:
                    pk = pK[it % 2]
                    ks = kst[it % 2]
                    it += 1
                    for h in range(4):
                        P.mm(pk[:, h, :], src[:, h * 64:(h + 1) * 64], idf if d == 0 else jf)
                    P.copy(ks.re("k s h -> k h s"), pk, eng="act")
                    s0 = sf if d == 0 else sb_
                    P.dma(S[name][d, :, s0:s0 + 128, :], ks, eng="sp" if it % 2 else "pool")
            for (src, name) in ((b_, "BT"), (kp, "KT")):
                P.dma(S[name][0, sf:sf + 128, :], src, eng="pool")
            for h in range(4):
                P.dma(S["VBD"][0, h, sf:sf + 128, h * 64:(h + 1) * 64], v[:, h * 64:(h + 1) * 64], eng="sp")
            for i, (src, name) in enumerate(((b_, "BT"), (kp, "KT"), (v, "V"))):
                pp = pP[i % 2]
                P.mm(pp, jf, src)
                rr = rv[i % 2]
                P.copy(rr, pp, eng="act")
                if name == "V":
                    for h in range(4):
                        P.dma(S["VBD"][1, h, sb_:sb_ + 128, h * 64:(h + 1) * 64], rr[:, h * 64:(h + 1) * 64], eng="sp")
                else:
                    P.dma(S[name][1, sb_:sb_ + 128, :], rr, eng="pool")
    P.barrier()


def stage_rwkv_scan(P, S, l, filler=None, st_outer=None):
    CH = RCH
    NCH = NTOK // CH
    with ExitStack() as st:
        LA = [P.sb([128, CH, 40], stack=st) for _ in range(2)]
        L2 = [P.sb([16, CH, 128], stack=st) for _ in range(2)]
        R2 = [P.sb([16, CH, 256], stack=st) for _ in range(2)]
        Wt = [P.sb([128, CH, 4], stack=st) for _ in range(2)]
        OB = 2
        Os = [P.sb([40, OB, 256], stack=st) for _ in range(2)]
        for i in range(2):
            P.memset(LA[i].re("p s c -> p (s c)"), 0.0)
            P.memset(L2[i].re("p s c -> p (s c)"), 0.0)
            P.memset(R2[i].re("p s c -> p (s c)"), 0.0)
        mask8 = P.sb([8, 256], stack=st)
        P.dma(mask8, S["mask8"])
        St = P.sb([128, 256], stack=st)
        P.memset(St, 0.0)
        pA = [P.ps([40, 256], stack=st) for _ in range(2)]
        pU = [P.ps([128, 256], stack=st) for _ in range(2)]
        S3 = St.re("p (h v) -> p h v", h=4)

        def load_chunk(c):
            i = c % 2
            s0 = c * CH
            if c == NCH:
                for d in range(2):
                    P.dma(LA[i][d * 64:(d + 1) * 64, 0:1, 32 + 4 * d:36 + 4 * d], S["RT"][d, :, s0 - 1:s0, :], eng="sp")
                return
            for d in range(2):
                rows = slice(d * 64, (d + 1) * 64)
                P.dma(LA[i][rows, :, 4 * d:4 * d + 4], S["ANT"][d, :, s0:s0 + CH, :], eng="sp")
                if c == 0:
                    P.dma(LA[i][rows, 1:CH, 32 + 4 * d:36 + 4 * d], S["RT"][d, :, 0:CH - 1, :], eng="act")
                else:
                    P.dma(LA[i][rows, :, 32 + 4 * d:36 + 4 * d], S["RT"][d, :, s0 - 1:s0 + CH - 1, :], eng="act")
                P.dma(Wt[i][rows, :, :], S["WT"][d, :, s0:s0 + CH, :], eng="sp")
                P.dma(L2[i][4 * d:4 * d + 4, :, d * 64:(d + 1) * 64],
                      S["BT"][d, s0:s0 + CH, :].re("s (h k) -> h s k", h=4), eng="act")
                P.dma(L2[i][8 + 4 * d:12 + 4 * d, :, d * 64:(d + 1) * 64],
                      S["KT"][d, s0:s0 + CH, :].re("s (h k) -> h s k", h=4), eng="sp")
                P.dma(R2[i][8 + 4 * d:12 + 4 * d, :, :], S["VBD"][d, :, s0:s0 + CH, :], eng="act")

        load_chunk(0)
        for s in range(NTOK + 1):
            c, pos = divmod(s, CH)
            i = c % 2
            if pos == 0 and c + 1 <= NCH:
                load_chunk(c + 1)
            pa = pA[s % 2]
            P.mm(pa, LA[i][:, pos, :], St)
            if s >= 1:
                cp, pp_ = divmod(s - 1, OB)
                P.copy(Os[cp % 2][32:40, pp_, :], pa[32:40, :], eng="act")
                if pp_ == OB - 1:
                    P.dma(S["OD"].re("d h s c -> (d h) s c")[:, cp * OB:(cp + 1) * OB, :], Os[cp % 2][32:40, :, :], eng="sp")
            if s == NTOK:
                break
            if filler is not None:
                next(filler, None)
            P.tt(R2[i][0:8, pos, :], pa[0:8, :], mask8, ALU.mult)
            pu = pU[s % 2]
            P.mm(pu, L2[i][:, pos, :], R2[i][:, pos, :])
            P.tt(S3, S3, V(Wt[i].ap[:, pos, :].unsqueeze(2).to_broadcast([128, 4, 64]), Wt[i].key), ALU.mult, eng="pool")
            P.tt(St, St, pu, ALU.add)
            if filler is not None:
                next(filler, None)
    if filler is None:
        P.barrier()


def stage_rwkv_out(P, S, l):
    with ExitStack() as st:
        jf = P.sb([128, 128], stack=st)
        P.dma(jf, S["antiid"])
        gam = load_bc(P, S, st, "rwkv_lnx_g", l, 256)
        bet = load_bc(P, S, st, "rwkv_lnx_b", l, 256)
        o_ = [[P.sb([128, 256], stack=st) for _ in range(2)] for _ in range(2)]
        gfb = [[P.sb([128, 256], stack=st) for _ in range(2)] for _ in range(2)]
        bon = [P.sb([128, 256], stack=st) for _ in range(2)]
        orev = [P.sb([128, 256], stack=st) for _ in range(2)]
        pj = [P.ps([128, 256], stack=st) for _ in range(2)]
        sq = P.sb([128, 256], stack=st)
        s4 = [P.sb([128, 16], stack=st) for _ in range(2)]
        gn = [P.sb([128, 256], stack=st) for _ in range(2)]
        y = [P.sb([128, 256], stack=st) for _ in range(2)]
        for t in range(NT):
            i = t % 2
            sf, sb_ = t * 128, s0_bwd(t)
            for d in range(2):
                s0 = sf if d == 0 else sb_
                for h in range(4):
                    P.dma(o_[i][d][:, h * 64:(h + 1) * 64], S["OD"][d, h, s0:s0 + 128, h * 64:(h + 1) * 64],
                          eng="sp" if h % 2 else "pool")
                P.dma(gfb[i][d], S["GFB"][d, sf:sf + 128, :], eng="sp")
            P.dma(bon[i], S["BON"][sf:sf + 128, :], eng="pool")
            for d in range(2):
                if d == 1:
                    P.mm(pj[i], jf, o_[i][1])
                    P.copy(orev[i], pj[i], eng="act")
                    od = orev[i]
                else:
                    od = o_[i][0]
                s_ = s4[d]
                o3 = od.re("p (h d) -> p h d", h=4)
                P.reduce(s_[:, 0:4], o3, ALU.add)
                P.tt(sq, od, od, ALU.mult)
                P.reduce(s_[:, 4:8], sq.re("p (h d) -> p h d", h=4), ALU.add)
                P.ts(s_[:, 0:4], s_[:, 0:4], 1.0 / 64, ALU.mult)
                P.tt(s_[:, 8:12], s_[:, 0:4], s_[:, 0:4], ALU.mult)
                P.stt(s_[:, 4:8], s_[:, 4:8], 1.0 / 64, s_[:, 8:12], ALU.mult, ALU.subtract)
                P.ts(s_[:, 4:8], s_[:, 4:8], 64e-5, ALU.add)
                P.act(s_[:, 4:8], s_[:, 4:8], AF.Sqrt)
                P.recip(s_[:, 8:12], s_[:, 4:8])
                g_ = gn[d]
                for h in range(4):
                    P.ts(g_[:, h * 64:(h + 1) * 64], od[:, h * 64:(h + 1) * 64], s_[:, h:h + 1], ALU.subtract,
                         s_[:, 8 + h:9 + h], ALU.mult)
                P.tt(g_, g_, gam, ALU.mult, eng="pool")
                P.tt(g_, g_, bet, ALU.add, eng="pool")
                P.tt(g_, g_, bon[i], ALU.add, eng="pool")
                P.tt(g_, g_, gfb[i][d], ALU.mult, eng="pool")
            P.tt(y[i], gn[0], gn[1], ALU.add, eng="pool")
            P.dma(S["Yrw"][sf:sf + 128, :], y[i], eng="sp")
    P.barrier()
    tok2feat(P, S, S["Yrw"], 2)


def stage_rwkv(P, S, l):
    stage_rwkv_prep(P, S, l)
    stage_rwkv_scan(P, S, l)
    stage_rwkv_out(P, S, l)
SCRATCH = {
    "MODD": ([1, 2 * 6144], "f32"),
    "HYP": ([NTOK, 768], "f32"),
    "RWP": ([NTOK, 1216], "f32"),
    "VSW": ([NTOK, 128], "f32"),
    "VDF": ([NTOK, 256], "f32"),
    "QK": ([14, 64, NTOK], "bf16"),
    "GT": ([4096, NTOK], "f32"),
    "YT": ([4, 256, NTOK], "bf16"),
    "H1": ([NTOK, D], "f32"),
    "H2": ([NTOK, D], "f32"),
    "GFT": ([2816, UTW], "bf16"),
    "Gx": ([2, 256, 2 * SEQ], "bf16"),
    "Gc": ([2, 256, 2 * CTX], "bf16"),
    "Yhy": ([NTOK, 256], "f32"),
    "ANT": ([2, 64, NTOK, 4], "f32"), "RT": ([2, 64, NTOK, 4], "f32"), "WT": ([2, 64, NTOK, 4], "f32"),
    "BT": ([2, NTOK, 256], "f32"), "KT": ([2, NTOK, 256], "f32"), "VBD": ([2, 4, NTOK, 256], "f32"),
    "OD": ([2, 4, NTOK, 256], "f32"), "GFB": ([2, NTOK, 256], "f32"), "BON": ([NTOK, 256], "f32"),
    "Ysw": ([NTOK, 256], "f32"),
    "Yrw": ([NTOK, 256], "f32"),
    "Ydf": ([NTOK, 256], "f32"),
}

WEIGHTS = {
    "ada_w": [2, 1024, 6144], "ada_b": [2, 6144], "w_in": [2, 1024, 7360], "w_swap": [2, 1024, 896],
    "hy_conv_w": [2, 3, 768], "hy_conv_b": [2, 768], "hy_f_w1": [2, 33, 64], "hy_f_b1": [2, 64],
    "hy_f_w2": [2, 64, 64], "hy_f_b2": [2, 64], "hy_f_w3": [2, 64, 1024], "hy_f_freq": [2, 64],
    "hy_bias": [2, 2, 256], "swa_sink": [2, 4], "rwkv_mu": [2, 1216], "rwkv_w0": [2, 2, 256],
    "rwkv_w2": [2, 2, 64, 256], "rwkv_a0": [2, 256], "rwkv_a2": [2, 64, 256], "rwkv_g2": [2, 2, 128, 256],
    "rwkv_kk": [2, 256], "rwkv_ka": [2, 256], "rwkv_rk": [2, 4, 64], "rwkv_lnx_g": [2, 256],
    "rwkv_lnx_b": [2, 256], "diff_lq1": [2, 32], "diff_lk1": [2, 32], "diff_lq2": [2, 32], "diff_lk2": [2, 32],
    "diff_subln_g": [2, 64], "w_branch": [2, 4, 256, 1024], "w_out": [2, 1024, 1024], "ln1_g": [2, 1024],
    "ln1_b": [2, 1024], "ffn_w_up": [2, 1024, 5632], "ffn_conv_w": [2, 3, 5632], "ffn_conv_b": [2, 5632],
    "ffn_w_down": [2, 2816, 1024], "ln2_g": [2, 1024], "ln2_b": [2, 1024],
}
CONSTS = {"ident": [128, 128], "antiid": [128, 128], "rope": [4, 64, NTOK], "maskLR": [2, 128, 512], "mask8": [8, 256],
          "hyE_x": [33, 2 * SEQ], "hyE_c": [33, 2 * CTX], "hyT_x": [1, 2 * SEQ], "hyT_c": [1, 2 * CTX], "hyND": [256, 1]}


def build(nc, dbg=(), stages=None, nlayers=2, ext_in=()):
    P = Prog(nc)
    S = {}
    S["xin"] = P.dram("xin", [NTOK, D], F32, kind="ExternalInput")
    S["cvec"] = P.dram("cvec", [2, D], F32, kind="ExternalInput")
    for k, shp in {**WEIGHTS, **CONSTS}.items():
        S[k] = P.dram(k, shp, F32, kind="ExternalInput")
    for k, (shp, dt) in SCRATCH.items():
        S[k] = P.dram(k, shp, F32 if dt == "f32" else BF16, kind="ExternalOutput" if k in dbg else ("ExternalInput" if k in ext_in else "Internal"))
    S["out"] = P.dram("out", [SEQ, D], F32, kind="ExternalOutput")
    S["identb"] = P.sb([128, 128], BF16)
    with ExitStack() as st0:
        idf = P.sb([128, 128], stack=st0)
        P.dma(idf, S["ident"])
        P.copy(S["identb"], idf)
        P.barrier()
    for l in range(nlayers):
        src = S["xin"] if l == 0 else S["H2"]
        if stages is None or "mod" in stages:
            stage_mod(P, S, l)
        if stages is None or "A" in stages:
            with ExitStack() as st:
                uT = P.sb([128, 8, UTW], BF16, stack=st)
                stage_lnmod(P, S, src, 0, 1024, uT)
                stage_inproj(P, S, l, uT)
            P.barrier()
        lam_init = 0.8 - 0.6 * float(np.exp(-0.3 * l))
        if stages is None or "hyrw" in stages:
            stage_hyena_rwkv(P, S, l)
        if stages is not None and "hyena" in stages:
            stage_hyena(P, S, l)
        if stages is not None and "rwkv" in stages:
            stage_rwkv(P, S, l)
        if stages is None or "swa" in stages:
            stage_swa(P, S, l)
        if stages is None or "diff" in stages:
            stage_diff(P, S, l, lam_init)
        if stages is None or "merge" in stages:
            stage_merge(P, S, l, src)
        if stages is None or "ffn" in stages:
            stage_ffn(P, S, l, (l == nlayers - 1) and ("H2" not in dbg))
    P.finalize()
    return P


def rope_tables():
    pos = np.arange(SEQ)
    row = (pos // 64).astype(np.float32)
    col = (pos % 64).astype(np.float32)
    out = np.zeros((4, 64, NTOK), np.float32)
    out[0, :, :CTX] = 1.0
    out[2, :, :CTX] = 1.0
    for ti, d in ((0, 64), (2, 32)):
        nf = d // 4
        inv = (10000.0 ** (-np.arange(nf, dtype=np.float32) / nf)).astype(np.float32)
        for rep in range(64 // d):
            for half, p in enumerate((row, col)):
                ang = (p[None, :] * inv[:, None]).astype(np.float32)
                c, s = np.cos(ang), np.sin(ang)
                b = rep * d + half * 2 * nf
                out[ti, b:b + nf, CTX:] = c
                out[ti, b + nf:b + 2 * nf, CTX:] = c
                out[ti + 1, b:b + nf, CTX:] = -s
                out[ti + 1, b + nf:b + 2 * nf, CTX:] = s
    return out


def hy_consts(L):
    pos = np.concatenate([np.arange(L - 1, -1, -1), np.arange(L)]).astype(np.float64)
    t = np.linspace(0.0, 1.0, L, dtype=np.float32)[pos.astype(np.int64)]
    w = (2.0 * np.pi * pos / L)
    fbv = np.linspace(1e-4, 15, 16, dtype=np.float32).astype(np.float64)
    ang = w[None, :] * fbv[:, None]
    E = np.concatenate([t[None, :].astype(np.float64), np.cos(ang), -np.sin(ang)], 0).astype(np.float32)
    return np.ascontiguousarray(E), np.ascontiguousarray(t[None, :].astype(np.float32))


def swap_cols():
    idx = []
    for (c0, n, d) in ((SWQ0, 256, 64), (SWK0, 128, 64), (DFQ0, 256, 32), (DFK0, 256, 32)):
        nf = d // 4
        for j in range(n):
            i = j % (2 * nf)
            idx.append(c0 + (j + nf if i < nf else j - nf))
    return np.array(idx)


def make_in_maps(inputs, cores):
    ins = {k: np.ascontiguousarray(np.asarray(v, dtype=np.float32)) for k, v in inputs.items()}
    common = {k: ins[k] for k in WEIGHTS if k != "w_swap"}
    common["w_swap"] = np.ascontiguousarray(ins["w_in"][:, :, swap_cols()])
    common["ident"] = np.eye(128, dtype=np.float32)
    common["antiid"] = np.ascontiguousarray(np.eye(128, dtype=np.float32)[::-1])
    common["rope"] = rope_tables()
    common["hyE_x"], common["hyT_x"] = hy_consts(SEQ)
    common["hyE_c"], common["hyT_c"] = hy_consts(CTX)
    lo, hi = np.log(1e-2) / 1.5, np.log(1e-2) / 0.3
    common["hyND"] = np.ascontiguousarray(-np.abs(np.linspace(lo, hi, 256, dtype=np.float32))[:, None])
    common["mask8"] = np.ascontiguousarray((np.arange(8)[:, None] % 4 == np.arange(256)[None, :] // 64).astype(np.float32))
    rr = np.arange(128)
    mL = (rr[:, None] >= rr[None, :]).astype(np.float32)
    mR = (rr[:, None] <= rr[None, :]).astype(np.float32)
    common["maskLR"] = np.ascontiguousarray(np.stack([np.tile(mL, (1, 4)), np.tile(mR, (1, 4))], 0))
    maps = []
    for b in cores:
        m = dict(common)
        m["xin"] = np.ascontiguousarray(np.concatenate([ins["ctx"][b], ins["x"][b]], 0))
        m["cvec"] = np.ascontiguousarray(np.stack([ins["c"][b], ins["c_ctx"]], 0))
        maps.append(m)
    return maps


def kernel(**inputs):
    nc = bass.Bass("TRN2", target_bir_lowering=False)
    build(nc)
    maps = make_in_maps(inputs, list(range(8)))
    res = run_bass_kernel_spmd(nc, maps, core_ids=list(range(8)))
    return np.stack([r["out"] for r in res.results], 0).astype(np.float32)
```

```python
import numpy as np
import concourse.bass as bass
import concourse.mybir as mybir
from contextlib import ExitStack

F32 = mybir.dt.float32
BF16 = mybir.dt.bfloat16
AF = mybir.ActivationFunctionType
ALU = mybir.AluOpType
AX = mybir.AxisListType

EPOCH = 30000
NDMA = 8


class V:
    __slots__ = ("ap", "key")

    def __init__(self, ap, key):
        self.ap = ap
        self.key = key

    def __getitem__(self, idx):
        return V(self.ap[idx], self.key)

    def k(self, sub):
        base = self.key[0] if isinstance(self.key, tuple) else self.key
        return V(self.ap, (base, sub))

    def re(self, s, **kw):
        return V(self.ap.rearrange(s, **kw), self.key)

    def bc(self, shape):
        return V(self.ap.to_broadcast(list(shape)), self.key)

    def bitcast(self, dt):
        return V(self.ap.bitcast(dt), self.key)


def _ap(x):
    return x.ap if isinstance(x, V) else x


class Prog:
    ENG = ("pe", "act", "dve", "pool", "sp")

    def __init__(self, nc):
        self.nc = nc
        self.es = ExitStack()
        self.ops = {e: [] for e in self.ENG}
        self.cnt = {e: 0 for e in self.ENG}
        self.sems = {e: [] for e in self.ENG}
        self.waited = {}
        self.last_w = {}
        self.readers = {}
        self.dma_cnt = {e: 0 for e in self.ENG}
        self.dma_sems = {e: None for e in self.ENG}
        self.nbuf = 0
        self.out_tokens = []

    def sb(self, shape, dt=F32, name=None, stack=None):
        self.nbuf += 1
        name = name or f"sb{self.nbuf}"
        h = (stack or self.es).enter_context(self.nc.sbuf_tensor(f"{name}_{self.nbuf}", list(shape), dt))
        return V(h[:] if hasattr(h, "__getitem__") else h.ap(), f"{name}_{self.nbuf}")

    def ps(self, shape, dt=F32, name=None, stack=None):
        self.nbuf += 1
        name = name or f"ps{self.nbuf}"
        h = (stack or self.es).enter_context(self.nc.psum_tensor(f"{name}_{self.nbuf}", list(shape), dt))
        return V(h[:] if hasattr(h, "__getitem__") else h.ap(), f"{name}_{self.nbuf}")

    def dram(self, name, shape, dt=F32, kind="Internal"):
        h = self.nc.dram_tensor(name, list(shape), dt, kind=kind)
        return V(h.ap(), name)

    def _sem_for(self, eng, idx):
        ep = idx // EPOCH
        while len(self.sems[eng]) <= ep:
            s = self.es.enter_context(self.nc.semaphore(f"s_{eng}_{len(self.sems[eng])}"))
            self.sems[eng].append(s)
        return self.sems[eng][ep], (idx % EPOCH) + 1

    def _wait(self, eng, tok):
        if tok is None:
            return
        if tok[0] == "e":
            _, src, idx = tok
            if src == eng and eng == "pe":
                return
            if self.waited.get((eng, src), -1) >= idx:
                return
            self.waited[(eng, src)] = idx
            sem, val = self._sem_for(src, idx)
            self.ops[eng].append(("w", sem, val))
        else:
            _, sem, val, sid = tok
            if self.waited.get((eng, sid), -1) >= val:
                return
            self.waited[(eng, sid)] = val
            self.ops[eng].append(("w", sem, val))

    def _deps(self, eng, reads, writes):
        for k in reads:
            self._wait(eng, self.last_w.get(k))
        for k in writes:
            self._wait(eng, self.last_w.get(k))
            for t in self.readers.get(k, ()):
                self._wait(eng, t)

    def _commit(self, tok, reads, writes):
        for k in reads:
            self.readers.setdefault(k, []).append(tok)
        for k in writes:
            self.last_w[k] = tok
            self.readers[k] = []

    def op(self, eng, fn, outs, ins):
        reads = [x.key for x in ins if isinstance(x, V)]
        writes = [x.key for x in outs if isinstance(x, V)]
        self._deps(eng, reads, writes)
        idx = self.cnt[eng]
        self.cnt[eng] += 1
        sem, _ = self._sem_for(eng, idx)
        self.ops[eng].append(("i", fn, sem, 1))
        tok = ("e", eng, idx)
        self._commit(tok, reads, writes)
        return tok

    def dma(self, out, in_, eng="sp", **kw):
        reads = [in_.key]
        writes = [out.key]
        self._deps(eng, reads, writes)
        if self.dma_sems[eng] is None:
            self.dma_sems[eng] = [self.es.enter_context(self.nc.semaphore(f"d_{eng}_{i}")) for i in range(NDMA)]
        j = self.dma_cnt[eng]
        self.dma_cnt[eng] += 1
        s = j % NDMA
        sem = self.dma_sems[eng][s]
        sid = f"d_{eng}_{s}"
        if j >= NDMA:
            self._wait(eng, ("d", sem, 16 * (j // NDMA), sid))
        o, i = out.ap, in_.ap
        self.ops[eng].append(("i", lambda e: e.dma_start(out=o, in_=i, **kw), sem, 16))
        tok = ("d", sem, 16 * (j // NDMA + 1), sid)
        self._commit(tok, reads, writes)
        return tok

    def mm(self, out, lhsT, rhs, start=True, stop=True):
        o, l, r = out.ap, lhsT.ap, rhs.ap
        return self.op("pe", lambda e: e.matmul(o, l, r, start=start, stop=stop), [out], [lhsT, rhs])

    def transpose(self, out, in_, ident):
        o, i, d = out.ap, in_.ap, ident.ap
        return self.op("pe", lambda e: e.transpose(o, i, d), [out], [in_, ident])

    def act(self, out, in_, func, bias=0.0, scale=1.0, accum_out=None):
        o, i, b, s = out.ap, in_.ap, _ap(bias), _ap(scale)
        if accum_out is not None:
            a = accum_out.ap
            fn = lambda e: e.activation(o, i, func, bias=b, scale=s, accum_out=a)
            outs = [out, accum_out]
        else:
            fn = lambda e: e.activation(o, i, func, bias=b, scale=s)
            outs = [out]
        return self.op("act", fn, outs, [in_, bias, scale])

    def tt(self, out, a, b, op, eng="dve"):
        o, x, y = out.ap, a.ap, b.ap
        return self.op(eng, lambda e: e.tensor_tensor(o, x, y, op), [out], [a, b])

    def ts(self, out, a, s1, op0, s2=None, op1=None, eng="dve", accum_out=None):
        o, x, p, q = out.ap, a.ap, _ap(s1), _ap(s2)
        kw = {}
        outs = [out]
        if op1 is not None:
            kw["op1"] = op1
        if accum_out is not None:
            kw["accum_out"] = accum_out.ap
            outs.append(accum_out)
        return self.op(eng, lambda e: e.tensor_scalar(o, x, p, q, op0, **kw), outs, [a, s1, s2])

    def stt(self, out, a, scalar, b, op0, op1, eng="dve"):
        o, x, s, y = out.ap, a.ap, _ap(scalar), b.ap
        return self.op("dve", lambda e: e.scalar_tensor_tensor(o, x, s, y, op0, op1), [out], [a, scalar, b])

    def copy(self, out, in_, eng="dve"):
        o, i = out.ap, in_.ap
        if eng == "act":
            return self.op("act", lambda e: e.copy(o, i), [out], [in_])
        return self.op(eng, lambda e: e.tensor_copy(o, i), [out], [in_])

    def reduce(self, out, in_, op, axis=AX.X, eng="dve", abs_=None):
        o, i = out.ap, in_.ap
        kw = {}
        if abs_:
            kw["apply_absolute_value"] = True
        return self.op(eng, lambda e: e.tensor_reduce(o, i, axis, op, **kw), [out], [in_])

    def recip(self, out, in_):
        o, i = out.ap, in_.ap
        return self.op("dve", lambda e: e.reciprocal(o, i), [out], [in_])

    def memset(self, out, val, eng="dve"):
        o = out.ap
        return self.op(eng, lambda e: e.memset(o, val), [out], [])

    def iota(self, out, pattern, base=0, channel_multiplier=0):
        o = out.ap
        return self.op("pool", lambda e: e.iota(o, pattern, base=base, channel_multiplier=channel_multiplier,
                                                allow_small_or_imprecise_dtypes=True), [out], [])

    def barrier(self):
        toks = [("e", e, self.cnt[e] - 1) for e in self.ENG if self.cnt[e] > 0]
        for e in self.ENG:
            for j in range(min(self.dma_cnt[e], NDMA)):
                jj = self.dma_cnt[e] - 1 - j
                s = jj % NDMA
                toks.append(("d", self.dma_sems[e][s], 16 * (jj // NDMA + 1), f"d_{e}_{s}"))
        for e in self.ENG:
            for t in toks:
                self._wait(e, t)
        self.last_w = {}
        self.readers = {}

    def finalize(self):
        self.barrier()
        nc = self.nc
        ops = self.ops
        with nc.Block() as block:
            def mk(name):
                def body(e):
                    for it in ops[name]:
                        if it[0] == "w":
                            e.wait_ge(it[1], it[2])
                        else:
                            it[1](e).then_inc(it[2], it[3])
                return body
            block.tensor(mk("pe"))
            block.scalar(mk("act"))
            block.vector(mk("dve"))
            block.gpsimd(mk("pool"))
            block.sync(mk("sp"))
        self.es.close()
from concourse.bass_utils import run_bass_kernel_spmd
D = 1024
SEQ = 4096
CTX = 256
NTOK = SEQ + CTX
NT = NTOK // 128
INC = 7360
UTW = NTOK + 3
LN_EPS = 1e-6
HY0 = 0
SWQ0, SWK0, SWV0 = 768, 1024, 1152
RW0 = 1280
DFQ0, DFK0, DFV0 = 2496, 2752, 3008
GT0 = 3264
ALPHA = 4 ** 0.25


def ucol(tok):
    return tok + 1 if tok < CTX else tok + 2


def bc_rows(v, n):
    return V(v.ap.to_broadcast([128, n]), v.key)


def stage_mod(P, S, l):
    with ExitStack() as st:
        cs = P.sb([128, 2, 8], stack=st)
        P.dma(cs, S["cvec"].re("w (kc p) -> p w kc", p=128), allow_slow_non_contiguous=True)
        P.act(cs, cs, AF.Silu)
        brow = P.sb([1, 6144], stack=st)
        P.dma(brow, S["ada_b"][l:l + 1, :])
        res = P.sb([1, 2, 6144], stack=st)
        wts = [P.sb([128, 8, 512], stack=st) for _ in range(2)]
        pss = [P.ps([1, 512], stack=st) for _ in range(2)]
        for nb in range(12):
            wt = wts[nb % 2]
            P.dma(wt, S["ada_w"][l, :, nb * 512:(nb + 1) * 512].re("(kc p) n -> p kc n", p=128),
                  eng="sp" if nb % 2 == 0 else "pool")
            for w in range(2):
                ps = pss[w]
                for kc in range(8):
                    P.mm(ps, cs[:, w, kc:kc + 1], wt[:, kc, :], start=(kc == 0), stop=(kc == 7))
                P.tt(res[:, w, nb * 512:(nb + 1) * 512], ps, brow[:, nb * 512:(nb + 1) * 512], ALU.add)
        for w in range(2):
            for off in (1024, 4096):
                P.ts(res[:, w, off:off + 1024], res[:, w, off:off + 1024], 1.0, ALU.add)
        P.dma(S["MODD"], res.re("o w n -> o (w n)"))
    P.barrier()


def ln_stats(P, xt, junk, stt_):
    P.memset(stt_, 0.0)
    P.act(junk, xt, AF.Identity, accum_out=stt_[:, 0:1])
    P.act(junk, xt, AF.Square, accum_out=stt_[:, 1:2])
    P.ts(stt_[:, 0:1], stt_[:, 0:1], 1.0 / D, ALU.mult)
    P.tt(stt_[:, 2:3], stt_[:, 0:1], stt_[:, 0:1], ALU.mult)
    P.stt(stt_[:, 2:3], stt_[:, 1:2], 1.0 / D, stt_[:, 2:3], ALU.mult, ALU.subtract)
    P.ts(stt_[:, 2:3], stt_[:, 2:3], LN_EPS, ALU.add)
    P.act(stt_[:, 2:3], stt_[:, 2:3], AF.Sqrt)
    P.recip(stt_[:, 3:4], stt_[:, 2:3])
    P.stt(stt_[:, 4:5], stt_[:, 0:1], -1.0, stt_[:, 3:4], ALU.mult, ALU.mult)


def stage_lnmod(P, S, src, shoff, scoff, uT):
    with ExitStack() as st:
        _stage_lnmod(P, S, src, shoff, scoff, uT, st)
    P.barrier()


def _stage_lnmod(P, S, src, shoff, scoff, uT, st):
    modb = {}
    for w in range(2):
        sh = P.sb([128, D], stack=st)
        sc = P.sb([128, D], stack=st)
        P.dma(sh, bc_rows(S["MODD"][:, w * 6144 + shoff: w * 6144 + shoff + D], D))
        P.dma(sc, bc_rows(S["MODD"][:, w * 6144 + scoff: w * 6144 + scoff + D], D))
        modb[w] = (sh, sc)
    xts = [P.sb([128, D], stack=st) for _ in range(2)]
    junk = P.sb([128, D], stack=st)
    xn = P.sb([128, D], stack=st)
    ub = P.sb([128, D], BF16, stack=st)
    sts = [P.sb([128, 8], stack=st) for _ in range(2)]
    pts = [P.ps([128, 8, 128], BF16, stack=st) for _ in range(2)]
    P.memset(uT[:, :, 0:1], 0.0)
    P.memset(uT[:, :, CTX + 1:CTX + 2], 0.0)
    P.memset(uT[:, :, UTW - 1:UTW], 0.0)
    for t in range(NT):
        xt = xts[t % 2]
        s_ = sts[t % 2]
        P.dma(xt, src[t * 128:(t + 1) * 128, :], eng="sp" if t % 2 == 0 else "pool")
        ln_stats(P, xt, junk, s_)
        P.act(xn, xt, AF.Identity, bias=s_[:, 4:5], scale=s_[:, 3:4])
        sh, sc = modb[1 if t < 2 else 0]
        P.tt(xn, xn, sc, ALU.mult)
        P.tt(ub, xn, sh, ALU.add)
        pt = pts[t % 2]
        for kc in range(8):
            P.transpose(pt[:, kc, :], ub[:, kc * 128:(kc + 1) * 128], S["identb"])
        c0 = ucol(t * 128)
        P.copy(uT[:, :, c0:c0 + 128], pt, eng="act")


TOKCH = [(1, 0, 256)] + [(258 + i * 512, 256 + i * 512, 512) for i in range(8)]


def wview(S, name, l, c0, n):
    return S[name][l, :, c0:c0 + n].re("(kc p) n -> p kc n", p=128)


def stage_inproj(P, S, l, uT):
    W = "w_in"
    with ExitStack() as st:
        wst = [P.sb([128, 8, 384], stack=st) for _ in range(2)]
        wb = [[P.sb([128, 8, 384], BF16, stack=st) for _ in range(3)] for _ in range(2)]
        tap = [P.sb([128, 3, 384], stack=st) for _ in range(2)]
        bia = [P.sb([128, 384], stack=st) for _ in range(2)]
        pss = [P.ps([128, 384], stack=st) for _ in range(3)]
        osb = [P.sb([128, 384], stack=st) for _ in range(3)]
        blocks = []
        for c in range(0, 768, 384):
            blocks.append(("hy", HY0 + c, c, 384))
        for c, n in ((0, 384), (384, 384), (768, 384), (1152, 64)):
            blocks.append(("rw", RW0 + c, c, n))
        for c, n in ((0, 128),):
            blocks.append(("vsw", SWV0, 0, 128))
        blocks.append(("vdf", DFV0, 0, 256))
        it = 0
        for bi, (kind, wc0, oc0, n) in enumerate(blocks):
            ws = wst[bi % 2]
            P.dma(ws[:, :, 0:n], wview(S, W, l, wc0, n), eng="sp" if bi % 2 == 0 else "pool")
            wbb = wb[bi % 2]
            tp = tap[bi % 2]
            if kind == "hy":
                for j in range(3):
                    P.dma(tp[:, j, 0:n], bc_rows(S["hy_conv_w"][l, j:j + 1, oc0:oc0 + n], n))
                P.dma(bia[bi % 2][:, 0:n], bc_rows(S["hy_conv_b"][l:l + 1, oc0:oc0 + n], n))
                for j in range(3):
                    for kc in range(8):
                        P.tt(wbb[j][:, kc, 0:n], ws[:, kc, 0:n], tp[:, j, 0:n], ALU.mult, eng="pool" if kc % 2 else "dve")
                nj = 3
                dst = S["HYP"]
            elif kind == "rw":
                P.dma(tp[:, 0, 0:n], bc_rows(S["rwkv_mu"][l:l + 1, oc0:oc0 + n], n))
                P.ts(tp[:, 1, 0:n], tp[:, 0, 0:n], -1.0, ALU.mult, 1.0, ALU.add)
                P.ts(tp[:, 2, 0:n], tp[:, 0, 0:n], 0.5, ALU.mult)
                for kc in range(8):
                    P.tt(wbb[1][:, kc, 0:n], ws[:, kc, 0:n], tp[:, 1, 0:n], ALU.mult, eng="pool" if kc % 2 else "dve")
                    P.tt(wbb[0][:, kc, 0:n], ws[:, kc, 0:n], tp[:, 2, 0:n], ALU.mult, eng="pool" if kc % 2 else "dve")
                nj = 3
                dst = S["RWP"]
            else:
                for kc in range(8):
                    P.copy(wbb[1][:, kc, 0:n], ws[:, kc, 0:n], eng="pool" if kc % 2 else "dve")
                nj = 1
                dst = S["VSW"] if kind == "vsw" else S["VDF"]
            for t in range(NT):
                ps = pss[it % 3]
                ob = osb[it % 3]
                it += 1
                c0 = ucol(t * 128)
                if nj == 3:
                    seq = [(0, -1), (1, 0), (2 if kind == "hy" else 0, 1)]
                else:
                    seq = [(1, 0)]
                k = 0
                tot = len(seq) * 8
                for (wj, sh) in seq:
                    for kc in range(8):
                        P.mm(ps[:, 0:n], uT[:, kc, c0 + sh:c0 + sh + 128], wbb[wj][:, kc, 0:n],
                             start=(k == 0), stop=(k == tot - 1))
                        k += 1
                if kind == "hy":
                    P.tt(ob[:, 0:n], ps[:, 0:n], bia[bi % 2][:, 0:n], ALU.add)
                else:
                    P.copy(ob[:, 0:n], ps[:, 0:n], eng="act")
                P.dma(dst[t * 128:(t + 1) * 128, oc0:oc0 + n], ob[:, 0:n], eng="sp" if it % 2 else "pool")
    P.barrier()
    with ExitStack() as st:
        NU = 14
        ws = P.sb([128, 8, 896], stack=st)
        wq = P.sb([128, 8, 896], BF16, stack=st)
        wsw = P.sb([128, 8, 896], BF16, stack=st)
        srcs = [(SWQ0, 256, 0), (SWK0, 128, 256), (DFQ0, 256, 384), (DFK0, 256, 640)]
        for (c0, n, o) in srcs:
            P.dma(ws[:, :, o:o + n], wview(S, W, l, c0, n))
        for kc in range(8):
            P.copy(wq[:, kc, :], ws[:, kc, :], eng="pool" if kc % 2 else "dve")
        P.dma(ws, wview(S, "w_swap", l, 0, 896), eng="pool")
        for kc in range(8):
            P.copy(wsw[:, kc, :], ws[:, kc, :], eng="pool" if kc % 2 else "dve")
        rts = [P.sb([64, 4, 512], stack=st) for _ in range(2)]
        psa = [P.ps([64, 512], stack=st) for _ in range(2)]
        psb = [P.ps([64, 512], stack=st) for _ in range(2)]
        t1 = [P.sb([64, 512], stack=st) for _ in range(2)]
        t2 = [P.sb([64, 512], stack=st) for _ in range(2)]
        ob = [P.sb([64, 512], BF16, stack=st) for _ in range(2)]
        it = 0
        for ci, (uc0, tok0, n) in enumerate(TOKCH):
            rt = rts[ci % 2]
            P.dma(rt[:, :, 0:n], S["rope"][:, :, tok0:tok0 + n].re("f d t -> d f t"))
            for u in range(NU):
                pa, pb = psa[it % 2], psb[it % 2]
                a1, a2, o_ = t1[it % 2], t2[it % 2], ob[it % 2]
                it += 1
                for kc in range(8):
                    P.mm(pa[:, 0:n], wq[:, kc, u * 64:(u + 1) * 64], uT[:, kc, uc0:uc0 + n], start=(kc == 0), stop=(kc == 7))
                for kc in range(8):
                    P.mm(pb[:, 0:n], wsw[:, kc, u * 64:(u + 1) * 64], uT[:, kc, uc0:uc0 + n], start=(kc == 0), stop=(kc == 7))
                tb = 0 if u < 6 else 2
                P.tt(a1[:, 0:n], pa[:, 0:n], rt[:, tb, 0:n], ALU.mult)
                P.tt(a2[:, 0:n], pb[:, 0:n], rt[:, tb + 1, 0:n], ALU.mult, eng="dve")
                P.tt(o_[:, 0:n], a1[:, 0:n], a2[:, 0:n], ALU.add, eng="pool")
                P.dma(S["QK"][u, :, tok0:tok0 + n], o_[:, 0:n], eng="sp" if it % 2 else "pool")
    P.barrier()
    with ExitStack() as st:
        wst = [P.sb([128, 8, 512], stack=st) for _ in range(2)]
        wbs = [P.sb([128, 8, 512], BF16, stack=st) for _ in range(2)]
        pss = [P.ps([128, 512], stack=st) for _ in range(3)]
        osb = [P.sb([128, 512], stack=st) for _ in range(3)]
        it = 0
        for bi in range(8):
            ws = wst[bi % 2]
            wb_ = wbs[bi % 2]
            P.dma(ws, wview(S, W, l, GT0 + bi * 512, 512), eng="sp" if bi % 2 == 0 else "pool")
            for kc in range(8):
                P.copy(wb_[:, kc, :], ws[:, kc, :], eng="pool" if kc % 2 else "dve")
            for cc in range(4):
                for (uc0, tok0, n) in TOKCH:
                    ps = pss[it % 3]
                    o_ = osb[it % 3]
                    it += 1
                    for kc in range(8):
                        P.mm(ps[:, 0:n], wb_[:, kc, cc * 128:(cc + 1) * 128], uT[:, kc, uc0:uc0 + n], start=(kc == 0), stop=(kc == 7))
                    P.act(o_[:, 0:n], ps[:, 0:n], AF.Sigmoid)
                    r0 = bi * 512 + cc * 128
                    P.dma(S["GT"][r0:r0 + 128, tok0:tok0 + n], o_[:, 0:n], eng="sp" if it % 2 else "pool")
    P.barrier()


def resid_ln(P, pss, hsrc, gb, lg, lb, dst_rows, tmp, stt_, res, eng_i):
    for hf in range(2):
        P.tt(tmp[:, hf * 512:(hf + 1) * 512], pss[hf], gb[:, hf * 512:(hf + 1) * 512], ALU.mult)
    P.stt(tmp, hsrc, ALPHA, tmp, ALU.mult, ALU.add, eng="pool")
    ln_stats(P, tmp, res, stt_)
    P.act(res, tmp, AF.Identity, bias=stt_[:, 4:5], scale=stt_[:, 3:4])
    P.tt(res, res, lg, ALU.mult, eng="pool")
    P.tt(res, res, lb, ALU.add, eng="pool")
    for (d, r0, r1) in dst_rows:
        P.dma(d, res[r0:r1, :], eng="sp" if eng_i % 2 else "pool")


def load_bc(P, S, st, name, l, n=D):
    t = P.sb([128, n], stack=st)
    P.dma(t, bc_rows(S[name][l:l + 1, :], n))
    return t


def stage_merge(P, S, l, src):
    with ExitStack() as st:
        wbr = P.sb([128, 8, D], BF16, stack=st)
        wo = P.sb([128, 8, D], BF16, stack=st)
        wst = P.sb([128, 8, D], stack=st)
        P.dma(wst, S["w_branch"][l].re("j (kc p) n -> p (j kc) n", p=128))
        for i in range(8):
            P.copy(wbr[:, i, :], wst[:, i, :], eng="pool" if i % 2 else "dve")
        P.dma(wst, S["w_out"][l].re("(kc p) n -> p kc n", p=128))
        for i in range(8):
            P.copy(wo[:, i, :], wst[:, i, :], eng="pool" if i % 2 else "dve")
        gbc = []
        for w in range(2):
            g = P.sb([128, D], stack=st)
            P.dma(g, bc_rows(S["MODD"][:, w * 6144 + 2048: w * 6144 + 3072], D))
            gbc.append(g)
        lg = load_bc(P, S, st, "ln1_g", l)
        lb = load_bc(P, S, st, "ln1_b", l)
        yts = [P.sb([128, 8, 512], BF16, stack=st) for _ in range(2)]
        gts = [P.sb([128, 4, 512], stack=st) for _ in range(2)]
        accT = P.sb([128, 8, 512], BF16, stack=st)
        acc = P.sb([128, 512], stack=st)
        prod = [P.sb([128, 512], stack=st) for _ in range(2)]
        pss = [P.ps([128, 512], stack=st) for _ in range(4)]
        pmix = [P.ps([128, 512], stack=st) for _ in range(2)]
        hts = [P.sb([128, D], stack=st) for _ in range(2)]
        tmp = P.sb([128, D], stack=st)
        res = P.sb([128, D], stack=st)
        sts = [P.sb([128, 8], stack=st) for _ in range(2)]
        GTv = S["GT"].re("(j dc p) t -> dc p j t", j=4, dc=8, p=128)
        YTv = S["YT"].re("j (kc p) t -> p (j kc) t", p=128)
        it = 0
        ti = 0
        for ci, (uc0, tok0, n) in enumerate(TOKCH):
            yt = yts[ci % 2]
            P.dma(yt[:, :, 0:n], YTv[:, :, tok0:tok0 + n])
            for dc in range(8):
                gt = gts[dc % 2]
                P.dma(gt[:, :, 0:n], GTv[dc][:, :, tok0:tok0 + n], eng="pool")
                for j in range(4):
                    ps = pss[it % 4]
                    it += 1
                    for k2 in range(2):
                        P.mm(ps[:, 0:n], wbr[:, j * 2 + k2, dc * 128:(dc + 1) * 128], yt[:, j * 2 + k2, 0:n],
                             start=(k2 == 0), stop=(k2 == 1))
                    if j == 0:
                        P.tt(acc[:, 0:n], ps[:, 0:n], gt[:, j, 0:n], ALU.mult)
                    else:
                        pr = prod[j % 2]
                        P.tt(pr[:, 0:n], ps[:, 0:n], gt[:, j, 0:n], ALU.mult)
                        if j < 3:
                            P.tt(acc[:, 0:n], acc[:, 0:n], pr[:, 0:n], ALU.add, eng="pool")
                        else:
                            P.tt(accT[:, dc, 0:n], acc[:, 0:n], pr[:, 0:n], ALU.add, eng="pool")
            for tt_ in range(n // 128):
                tok = tok0 + tt_ * 128
                ht = hts[ti % 2]
                P.dma(ht, src[tok:tok + 128, :])
                for hf in range(2):
                    for dc in range(8):
                        P.mm(pmix[hf], accT[:, dc, tt_ * 128:(tt_ + 1) * 128], wo[:, dc, hf * 512:(hf + 1) * 512],
                             start=(dc == 0), stop=(dc == 7))
                resid_ln(P, pmix, ht, gbc[1 if tok < CTX else 0], lg, lb,
                         [(S["H1"][tok:tok + 128, :], 0, 128)], tmp, sts[ti % 2], res, ti)
                ti += 1
    P.barrier()


def stage_ffn(P, S, l, last):
    FC = 22
    with ExitStack() as st:
        uT = P.sb([128, 8, UTW], BF16, stack=st)
        stage_lnmod(P, S, S["H1"], 3072, 4096, uT)
        with ExitStack() as st2:
            wst = [P.sb([128, 8, 256], stack=st2) for _ in range(2)]
            wbs = [P.sb([128, 8, 256], BF16, stack=st2) for _ in range(2)]
            taps = [P.sb([128, 2, 4], stack=st2) for _ in range(2)]
            hT = [P.sb([128, UTW], stack=st2) for _ in range(2)]
            cT = [P.sb([128, UTW], stack=st2) for _ in range(2)]
            gT = [P.sb([128, UTW], BF16, stack=st2) for _ in range(2)]
            pss = [P.ps([128, 512], stack=st2) for _ in range(4)]
            for ab in range(2):
                P.memset(hT[ab], 0.0)
                P.memset(cT[ab], 0.0)
            it = 0
            for fi in range(FC):
                ws, wb_, tp = wst[fi % 2], wbs[fi % 2], taps[fi % 2]
                for ab in range(2):
                    c0 = ab * 2816 + fi * 128
                    P.dma(ws[:, :, ab * 128:(ab + 1) * 128], wview(S, "ffn_w_up", l, c0, 128), eng="sp" if ab else "pool")
                    P.dma(tp[:, ab, 0:3], S["ffn_conv_w"][l, :, c0:c0 + 128].re("j p -> p j"), allow_slow_non_contiguous=True)
                    P.dma(tp[:, ab, 3:4], S["ffn_conv_b"][l:l + 1, c0:c0 + 128].re("o p -> p o"), allow_slow_non_contiguous=True)
                for kc in range(8):
                    P.copy(wb_[:, kc, :], ws[:, kc, :], eng="pool" if kc % 2 else "dve")
                for ab in range(2):
                    h = hT[ab]
                    for (uc0, tok0, n) in TOKCH:
                        ps = pss[it % 4]
                        it += 1
                        for kc in range(8):
                            P.mm(ps[:, 0:n], wb_[:, kc, ab * 128:(ab + 1) * 128], uT[:, kc, uc0:uc0 + n], start=(kc == 0), stop=(kc == 7))
                        P.copy(h[:, uc0:uc0 + n], ps[:, 0:n], eng="act")
                    c = cT[ab]
                    e = "dve" if ab == 0 else "pool"
                    Wd = UTW - 2
                    P.ts(c[:, 1:1 + Wd], h[:, 0:Wd], tp[:, ab, 0:1], ALU.mult, tp[:, ab, 3:4], ALU.add, eng=e)
                    P.stt(c[:, 1:1 + Wd], h[:, 1:1 + Wd], tp[:, ab, 1:2], c[:, 1:1 + Wd], ALU.mult, ALU.add, eng=e)
                    P.stt(c[:, 1:1 + Wd], h[:, 2:2 + Wd], tp[:, ab, 2:3], c[:, 1:1 + Wd], ALU.mult, ALU.add, eng=e)
                P.act(cT[0], cT[0], AF.Silu)
                g = gT[fi % 2]
                P.tt(g, cT[0], cT[1], ALU.mult)
                P.dma(S["GFT"][fi * 128:(fi + 1) * 128, :], g, eng="sp")
        P.barrier()
    P.barrier()
    with ExitStack() as st:
        wd = P.sb([128, FC, D], BF16, stack=st)
        wst = [P.sb([128, 2, D], stack=st) for _ in range(2)]
        for i in range(FC // 2):
            P.dma(wst[i % 2], S["ffn_w_down"][l, i * 256:(i + 1) * 256, :].re("(k p) n -> p k n", p=128), eng="sp" if i % 2 else "pool")
            P.copy(wd[:, 2 * i, :], wst[i % 2][:, 0, :], eng="dve")
            P.copy(wd[:, 2 * i + 1, :], wst[i % 2][:, 1, :], eng="pool")
        gbc = []
        for w in range(2):
            g = P.sb([128, D], stack=st)
            P.dma(g, bc_rows(S["MODD"][:, w * 6144 + 5120: w * 6144 + 6144], D))
            gbc.append(g)
        lg = load_bc(P, S, st, "ln2_g", l)
        lb = load_bc(P, S, st, "ln2_b", l)
        gts = [P.sb([128, FC, 128], BF16, stack=st) for _ in range(2)]
        hts = [P.sb([128, D], stack=st) for _ in range(2)]
        pmix = [P.ps([128, 512], stack=st) for _ in range(4)]
        tmp = P.sb([128, D], stack=st)
        res = P.sb([128, D], stack=st)
        sts = [P.sb([128, 8], stack=st) for _ in range(2)]
        GFv = S["GFT"].re("(fc p) t -> p fc t", p=128)
        for t in range(NT):
            tok = t * 128
            c0 = ucol(tok)
            gt = gts[t % 2]
            P.dma(gt, GFv[:, :, c0:c0 + 128], eng="pool")
            ht = hts[t % 2]
            P.dma(ht, S["H1"][tok:tok + 128, :])
            pm = pmix[(t % 2) * 2:(t % 2) * 2 + 2]
            for hf in range(2):
                for fc in range(FC):
                    P.mm(pm[hf], gt[:, fc, :], wd[:, fc, hf * 512:(hf + 1) * 512], start=(fc == 0), stop=(fc == FC - 1))
            if last:
                dsts = [(S["out"][tok - CTX:tok - CTX + 128, :], 0, 128)] if tok >= CTX else []
            else:
                dsts = [(S["H2"][tok:tok + 128, :], 0, 128)]
            if dsts:
                resid_ln(P, pm, ht, gbc[1 if tok < CTX else 0], lg, lb, dsts, tmp, sts[t % 2], res, t)
    P.barrier()


def tok2feat(P, S, src, j):
    with ExitStack() as st:
        xs = [P.sb([128, 256], stack=st) for _ in range(2)]
        xb = [P.sb([128, 256], BF16, stack=st) for _ in range(2)]
        pt = [P.ps([128, 2, 128], BF16, stack=st) for _ in range(2)]
        ob = [P.sb([128, 2, 128], BF16, stack=st) for _ in range(2)]
        for t in range(NT):
            i = t % 2
            P.dma(xs[i], src[t * 128:(t + 1) * 128, :], eng="sp" if i else "pool")
            P.copy(xb[i], xs[i], eng="pool")
            for kc in range(2):
                P.transpose(pt[i][:, kc, :], xb[i][:, kc * 128:(kc + 1) * 128], S["identb"])
            P.copy(ob[i], pt[i], eng="act")
            P.dma(S["YT"][j].re("(kc p) t -> p kc t", p=128)[:, :, t * 128:(t + 1) * 128], ob[i], eng="sp" if i else "pool")
    P.barrier()


def load_vext(P, S, st, src, nh):
    ve = P.sb([128, NT, nh, 65], BF16, stack=st)
    P.memset(ve.re("p t h d -> p (t h d)"), 1.0)
    vs = [P.sb([128, nh * 64], stack=st) for _ in range(2)]
    for t in range(NT):
        i = t % 2
        P.dma(vs[i], src[t * 128:(t + 1) * 128, :], eng="sp" if i else "pool")
        P.copy(ve[:, t, :, 0:64], vs[i].re("p (h d) -> p h d", h=nh), eng="pool" if i else "dve")
    return ve


def stage_swa(P, S, l):
    with ExitStack() as st:
        qk = P.sb([64, 6, NTOK], BF16, stack=st)
        for u in range(6):
            P.dma(qk[:, u, :], S["QK"][u], eng="sp" if u % 2 else "pool")
        ve = load_vext(P, S, st, S["VSW"], 2)
        mk32 = P.sb([128, 2, 512], stack=st)
        P.dma(mk32, S["maskLR"].re("m p f -> p m f"))
        mk = P.sb([128, 2, 512], BF16, stack=st)
        P.copy(mk, mk32)
        es = P.sb([128, 4], stack=st)
        P.dma(es, bc_rows(S["swa_sink"][l:l + 1, :], 4))
        P.act(es, es, AF.Exp)
        pS = [P.ps([128, 4, 128], stack=st) for _ in range(2)]
        pO = [P.ps([128, 4, 65], stack=st) for _ in range(2)]
        E = [P.sb([128, 5, 4, 128], BF16, stack=st) for _ in range(2)]
        den = [P.sb([128, 4], stack=st) for _ in range(2)]
        yt = [P.sb([128, 4, 64], stack=st) for _ in range(2)]
        it = 0
        for qb in range(NT):
            if qb < 2:
                kts = [(0, None), (1, None)]
            else:
                kts = []
                if qb - 1 >= 2:
                    kts.append((qb - 1, 0))
                kts.append((qb, None))
                if qb + 1 < NT:
                    kts.append((qb + 1, 1))
                kts += [(0, None), (1, None)]
            e_ = E[qb % 2]
            for ki, (kt, m) in enumerate(kts):
                ps = pS[it % 2]
                it += 1
                for h in range(4):
                    P.mm(ps[:, h, :], qk[:, 4 + h // 2, kt * 128:(kt + 1) * 128], qk[:, h, qb * 128:(qb + 1) * 128])
                P.act(e_[:, ki].re("p h q -> p (h q)"), ps.re("p h q -> p (h q)"), AF.Exp, scale=0.125)
                if m is not None:
                    P.tt(e_[:, ki].re("p h q -> p (h q)"), e_[:, ki].re("p h q -> p (h q)"), mk[:, m, :], ALU.mult)
            po = pO[qb % 2]
            for h in range(4):
                for ki, (kt, m) in enumerate(kts):
                    P.mm(po[:, h, :], e_[:, ki, h, :], ve[:, kt, h // 2, :], start=(ki == 0), stop=(ki == len(kts) - 1))
            d_ = den[qb % 2]
            y_ = yt[qb % 2]
            P.tt(d_, po[:, :, 64], es, ALU.add)
            P.recip(d_, d_)
            for h in range(4):
                P.ts(y_[:, h, :], po[:, h, 0:64], d_[:, h:h + 1], ALU.mult)
            P.dma(S["Ysw"][qb * 128:(qb + 1) * 128, :], y_.re("p h d -> p (h d)"), eng="sp" if qb % 2 else "pool")
    P.barrier()
    tok2feat(P, S, S["Ysw"], 1)


def stage_diff(P, S, l, lam_init):
    with ExitStack() as st:
        qkc = [P.sb([32, 8, NTOK], BF16, stack=st) for _ in range(2)]
        for u in range(8):
            for c in range(2):
                P.dma(qkc[c][:, u, :], S["QK"][6 + u, c * 32:(c + 1) * 32, :], eng="sp" if u % 2 else "pool")
        ve = load_vext(P, S, st, S["VDF"], 4)
        idf = P.sb([128, 128], stack=st)
        P.dma(idf, S["ident"])
        lv = P.sb([128, 4, 32], stack=st)
        for i, nm in enumerate(("diff_lq1", "diff_lk1", "diff_lq2", "diff_lk2")):
            P.dma(lv[:, i, :], bc_rows(S[nm][l:l + 1, :], 32))
        lt = P.sb([128, 8], stack=st)
        pr = P.sb([128, 2, 32], stack=st)
        P.tt(pr[:, 0, :], lv[:, 0, :], lv[:, 1, :], ALU.mult)
        P.tt(pr[:, 1, :], lv[:, 2, :], lv[:, 3, :], ALU.mult)
        P.reduce(lt[:, 0:2], pr, ALU.add)
        P.act(lt[:, 0:2], lt[:, 0:2], AF.Exp)
        P.tt(lt[:, 2:3], lt[:, 1:2], lt[:, 0:1], ALU.subtract)
        P.ts(lt[:, 3:4], lt[:, 2:3], -lam_init, ALU.add)
        gsub = P.sb([128, 64], stack=st)
        P.dma(gsub, bc_rows(S["diff_subln_g"][l:l + 1, :], 64))
        P.ts(gsub, gsub, 1.0 - lam_init, ALU.mult)
        pS = [P.ps([128, 512], stack=st) for _ in range(2)]
        pO = [P.ps([65, 512], stack=st) for _ in range(2)]
        pT = [P.ps([128, 2, 65], stack=st) for _ in range(2)]
        E = [P.sb([128, 512], BF16, stack=st) for _ in range(3)]
        oT = [P.sb([65, 512], stack=st) for _ in range(2)]
        tm = [P.sb([128, 8], stack=st) for _ in range(2)]
        a_ = [P.sb([128, 64], stack=st) for _ in range(2)]
        w_ = [P.sb([128, 64], stack=st) for _ in range(2)]
        jk = P.sb([128, 64], stack=st)
        y_ = [P.sb([128, 64], stack=st) for _ in range(2)]
        it = 0
        ti = 0
        sc = 32 ** -0.5
        for h in range(4):
            for (uc0, tok0, n) in TOKCH:
                kts = [0, 1] if tok0 < CTX else list(range(NT))
                for c in range(2):
                    po = pO[c]
                    pend = None
                    for ki, kt in enumerate(kts):
                        ps = pS[it % 2]
                        e_ = E[it % 3]
                        it += 1
                        P.mm(ps[:, 0:n], qkc[c][:, 4 + h, kt * 128:(kt + 1) * 128],
                             qkc[c][:, h, tok0:tok0 + n])
                        if pend is not None:
                            pk_, pe_ = pend
                            P.mm(po[:, 0:n], ve[:, kts[pk_], h, :], pe_[:, 0:n], start=(pk_ == 0), stop=False)
                        P.act(e_[:, 0:n], ps[:, 0:n], AF.Exp, scale=sc)
                        pend = (ki, e_)
                    pk_, pe_ = pend
                    P.mm(po[:, 0:n], ve[:, kts[pk_], h, :], pe_[:, 0:n], start=(pk_ == 0), stop=True)
                    P.copy(oT[c][:, 0:n], po[:, 0:n], eng="act")
                for tt_ in range(n // 128):
                    i = ti % 2
                    ti += 1
                    pt = pT[i]
                    for c in range(2):
                        P.mm(pt[:, c, :], oT[c][:, tt_ * 128:(tt_ + 1) * 128], idf[0:65, 0:65])
                    t_ = tm[i]
                    P.recip(t_[:, 0:2], pt[:, :, 64])
                    P.tt(t_[:, 2:3], t_[:, 1:2], lt[:, 3:4], ALU.mult)
                    P.ts(a_[i], pt[:, 0, 0:64], t_[:, 0:1], ALU.mult)
                    P.stt(w_[i], pt[:, 1, 0:64], t_[:, 2:3], a_[i], ALU.mult, ALU.add)
                    P.memset(t_[:, 4:5], 0.0)
                    P.act(jk, w_[i], AF.Square, accum_out=t_[:, 4:5])
                    P.ts(t_[:, 5:6], t_[:, 4:5], 1.0 / 64, ALU.mult, 1e-5, ALU.add)
                    P.act(t_[:, 5:6], t_[:, 5:6], AF.Sqrt)
                    P.recip(t_[:, 6:7], t_[:, 5:6])
                    P.stt(y_[i], w_[i], t_[:, 6:7], gsub, ALU.mult, ALU.mult)
                    tok = tok0 + tt_ * 128
                    P.dma(S["Ydf"][tok:tok + 128, h * 64:(h + 1) * 64], y_[i], eng="sp" if i else "pool")
    P.barrier()
    tok2feat(P, S, S["Ydf"], 3)

TWO_PI = 6.283185307179586


def hy_filter(P, S, l, L, En, Tn, G):
    N2 = 2 * L
    cw = min(512, N2)
    with ExitStack() as st:
        w1 = P.sb([33, 64], stack=st)
        w2 = P.sb([64, 64], stack=st)
        w3 = P.sb([64, 1024], stack=st)
        P.dma(w1, S["hy_f_w1"][l])
        P.dma(w2, S["hy_f_w2"][l])
        P.dma(w3, S["hy_f_w3"][l])
        fr = P.sb([64, 8], stack=st)
        P.dma(fr[:, 0:1], S["hy_f_freq"][l:l + 1, :].re("o p -> p o"), allow_slow_non_contiguous=True)
        P.dma(fr[:, 1:2], S["hy_f_b1"][l:l + 1, :].re("o p -> p o"), allow_slow_non_contiguous=True)
        P.dma(fr[:, 2:3], S["hy_f_b2"][l:l + 1, :].re("o p -> p o"), allow_slow_non_contiguous=True)
        P.ts(fr[:, 3:4], fr[:, 0:1], 1.0 / TWO_PI, ALU.mult)
        P.tt(fr[:, 4:5], fr[:, 3:4], fr[:, 1:2], ALU.mult)
        P.ts(fr[:, 4:5], fr[:, 4:5], 8.0, ALU.add)
        P.tt(fr[:, 5:6], fr[:, 3:4], fr[:, 2:3], ALU.mult)
        P.ts(fr[:, 5:6], fr[:, 5:6], 8.0, ALU.add)
        negpi = P.sb([128, 1], stack=st)
        P.memset(negpi, 1.5707963267948966)
        h2 = P.sb([64, N2], stack=st)
        Et = [P.sb([33, cw], stack=st) for _ in range(2)]
        ps1 = [P.ps([64, cw], stack=st) for _ in range(2)]
        ps2 = [P.ps([64, cw], stack=st) for _ in range(2)]
        v1 = [P.sb([64, cw], stack=st) for _ in range(2)]
        h1 = [P.sb([64, cw], stack=st) for _ in range(2)]
        vi = [P.sb([64, cw], mybir.dt.int32, stack=st) for _ in range(2)]
        vf = [P.sb([64, cw], stack=st) for _ in range(2)]
        sa = [P.sb([64, cw], stack=st) for _ in range(2)]
        sb_ = [P.sb([64, cw], stack=st) for _ in range(2)]

        def sin_red(ps, s2, out, i):
            P.ts(v1[i], ps, fr[:, 3:4], ALU.mult, s2, ALU.add)
            P.copy(vi[i], v1[i])
            P.copy(vf[i], vi[i])
            P.tt(v1[i], v1[i], vf[i], ALU.subtract)
            P.act(sa[i], v1[i], AF.Sin, scale=3.141592653589793)
            P.act(sb_[i], v1[i], AF.Sin, bias=negpi[0:64, :], scale=-3.141592653589793)
            P.stt(out, sa[i], 2.0, sb_[i], ALU.mult, ALU.mult)

        for ch in range(N2 // cw):
            i = ch % 2
            sl = slice(ch * cw, (ch + 1) * cw)
            P.dma(Et[i], S[En][:, sl])
            P.mm(ps1[i], w1, Et[i])
            sin_red(ps1[i], fr[:, 4:5], h1[i], i)
            P.mm(ps2[i], w2, h1[i])
            sin_red(ps2[i], fr[:, 5:6], h2[:, sl], i)
        T128 = P.sb([128, N2], stack=st)
        P.dma(T128, bc_rows(S[Tn], N2))
        nd = P.sb([128, 2], stack=st)
        P.dma(nd, S["hyND"].re("(c p) o -> p (c o)", p=128), allow_slow_non_contiguous=True)
        fw = min(512, L)
        filt = [P.sb([128, L], stack=st) for _ in range(2)]
        fb = [P.sb([128, L], BF16, stack=st) for _ in range(2)]
        junk = P.sb([128, L], stack=st)
        asum = P.sb([128, 4], stack=st)
        psf = [P.ps([128, fw], stack=st) for _ in range(2)]
        dec = [P.sb([128, fw], stack=st) for _ in range(2)]
        it = 0
        for o in range(2):
            for chalf in range(2):
                P.memset(asum, 0.0)
                for dr in range(2):
                    q = o * 4 + dr * 2 + chalf
                    pos0 = L if dr == 0 else 0
                    for ch in range(L // fw):
                        i = it % 2
                        it += 1
                        sl = slice(pos0 + ch * fw, pos0 + (ch + 1) * fw)
                        P.mm(psf[i], w3[:, q * 128:(q + 1) * 128], h2[:, sl])
                        P.act(dec[i], T128[:, sl], AF.Exp, scale=nd[:, chalf:chalf + 1])
                        P.tt(filt[dr][:, ch * fw:(ch + 1) * fw], psf[i], dec[i], ALU.mult)
                    P.act(junk, filt[dr], AF.Abs, accum_out=asum[:, dr:dr + 1])
                P.tt(asum[:, 2:3], asum[:, 0:1], asum[:, 1:2], ALU.add)
                P.recip(asum[:, 3:4], asum[:, 2:3])
                for dr in range(2):
                    P.ts(fb[dr], filt[dr], asum[:, 3:4], ALU.mult, eng="dve" if dr else "pool")
                rows = slice(chalf * 128, (chalf + 1) * 128)
                P.dma(G[o, rows, 0:L - 1], fb[1][:, 0:L - 1], eng="sp")
                P.dma(G[o, rows, L - 1:2 * L - 1], fb[0][:, 0:L], eng="pool")
    P.barrier()


def hy_conv_seq(P, S, l, tok0, L, G, GL):
    with ExitStack() as st:
        for _ in hy_conv_gen(P, S, l, tok0, L, G, GL, st):
            pass
    P.barrier()


def hy_conv_gen(P, S, l, tok0, L, G, GL, st, nhk=3):
    NB = L // 128
    ND_ = 2 * NB - 1
    HKW = ND_ * 128
    Gt = G.ap.tensor
    if True:
        jf = P.sb([128, 128], stack=st)
        Jb = P.sb([128, 128], BF16, stack=st)
        P.dma(jf, S["antiid"])
        P.copy(Jb, jf)
        A = P.sb([128, NB, 256], stack=st)
        B = P.sb([128, NB, 256], stack=st)
        Vb = P.sb([128, NB, 256], BF16, stack=st)
        Zr = P.sb([128, NB, 256], BF16, stack=st)
        b01 = P.sb([128, 2, 256], stack=st)
        for o in range(2):
            P.dma(b01[:, o, :], bc_rows(S["hy_bias"][l, o:o + 1, :], 256))
        src = S["HYP"][tok0:tok0 + L, :].re("(a r) c -> r a c", r=128)
        P.dma(A, src[:, :, 0:256], eng="sp")
        P.dma(B, src[:, :, 256:512], eng="pool")
        hks = [P.sb([128, HKW], BF16, stack=st) for _ in range(nhk)]
        psJ = [P.ps([128, 512], stack=st) for _ in range(2)]
        psY = [P.ps([128, 16, NB], stack=st) for _ in range(2)]
        tmp = [P.sb([128, NB, 16], stack=st) for _ in range(1)]
        Vf = Vb.re("p a c -> p (a c)")
        Zf = Zr.re("p a c -> p (a c)")
        ds = [0] + [d for d in range(-(NB - 1), NB) if d != 0]
        yield
        for o in range(2):
            inp = A if o == 0 else B
            for a in range(NB):
                P.copy(Vb[:, a, :], inp[:, a, :], eng="pool" if a % 2 else "dve")
            for ch in range(NB * 256 // 512):
                pj = psJ[ch % 2]
                P.mm(pj, Jb, Vf[:, ch * 512:(ch + 1) * 512])
                P.copy(Zf[:, ch * 512:(ch + 1) * 512], pj, eng="act")
            for a in range(NB):
                P.tt(inp[:, a, :], inp[:, a, :], b01[:, o, :], ALU.mult, eng="pool")
            if o == 1:
                P.dma(A, src[:, :, 512:768], eng="sp")
            oth = B if o == 0 else A
            for cg in range(16):
                py = psY[cg % 2]
                for ci in range(16):
                    c = cg * 16 + ci
                    hk = hks[c % nhk]
                    hv = V(bass.AP(tensor=Gt, offset=(o * 256 + c) * GL, ap=[[1, 128], [1, HKW]]), G.key)
                    P.dma(hk, hv, eng="sp" if c % 2 else "pool")
                    for di, d in enumerate(ds):
                        a_lo, a_hi = max(0, d), min(NB - 1, NB - 1 + d)
                        P.mm(py[:, ci, a_lo:a_hi + 1], hk[:, (d + NB - 1) * 128:(d + NB) * 128],
                             Zr[:, a_lo - d:a_hi - d + 1, c], start=(di == 0), stop=(di == len(ds) - 1))
                        if di % 4 == 3:
                            yield
                cs = slice(cg * 16, (cg + 1) * 16)
                tp = tmp[0]
                P.tt(tp, py.re("p c a -> p a c"), inp[:, :, cs], ALU.add)
                P.tt(oth[:, :, cs], tp, oth[:, :, cs], ALU.mult, eng="pool")
        P.dma(S["Yhy"][tok0:tok0 + L, :].re("(a r) c -> r a c", r=128), A, eng="sp")
    yield


def stage_hyena(P, S, l):
    hy_filter(P, S, l, SEQ, "hyE_x", "hyT_x", S["Gx"])
    hy_filter(P, S, l, CTX, "hyE_c", "hyT_c", S["Gc"])
    hy_conv_seq(P, S, l, CTX, SEQ, S["Gx"], 2 * SEQ)
    hy_conv_seq(P, S, l, 0, CTX, S["Gc"], 2 * CTX)
    tok2feat(P, S, S["Yhy"], 0)


def stage_hyena_rwkv(P, S, l):
    hy_filter(P, S, l, SEQ, "hyE_x", "hyT_x", S["Gx"])
    hy_filter(P, S, l, CTX, "hyE_c", "hyT_c", S["Gc"])
    stage_rwkv_prep(P, S, l)
    with ExitStack() as st:
        gen = hy_conv_gen(P, S, l, CTX, SEQ, S["Gx"], 2 * SEQ, st, nhk=4)
        next(gen)
        stage_rwkv_scan(P, S, l, filler=gen, st_outer=st)
        for _ in gen:
            pass
    P.barrier()
    hy_conv_seq(P, S, l, 0, CTX, S["Gc"], 2 * CTX)
    tok2feat(P, S, S["Yhy"], 0)
    stage_rwkv_out(P, S, l)

RCH = 8
F32R = mybir.dt.float32r


def s0_bwd(t):
    return 128 - 128 * t if t < 2 else 4480 - 128 * t


def stage_rwkv_prep(P, S, l):
    with ExitStack() as st:
        idf = P.sb([128, 128], stack=st)
        jf = P.sb([128, 128], stack=st)
        P.dma(idf, S["ident"])
        P.dma(jf, S["antiid"])
        zt = P.sb([128, 8704], stack=st)
        P.memset(zt, 0.0)
        vz = S["VBD"].re("d h s c -> (d h s c)").re("(p f) -> p f", p=128)
        for i in range(8):
            P.dma(vz[:, i * 8704:(i + 1) * 8704], zt, eng="sp" if i % 2 else "pool")
        w2 = [P.sb([64, 256], stack=st) for _ in range(2)]
        g2 = [P.sb([128, 256], stack=st) for _ in range(2)]
        a2 = P.sb([64, 256], stack=st)
        for d in range(2):
            P.dma(w2[d], S["rwkv_w2"][l, d])
            P.dma(g2[d], S["rwkv_g2"][l, d])
        P.dma(a2, S["rwkv_a2"][l])
        w0 = [P.sb([128, 256], stack=st) for _ in range(2)]
        for d in range(2):
            P.dma(w0[d], bc_rows(S["rwkv_w0"][l, d:d + 1, :], 256))
        a0 = load_bc(P, S, st, "rwkv_a0", l, 256)
        kkb = load_bc(P, S, st, "rwkv_kk", l, 256)
        kab = load_bc(P, S, st, "rwkv_ka", l, 256)
        omka = P.sb([128, 256], stack=st)
        P.ts(omka, kab, -1.0, ALU.mult, 1.0, ALU.add)
        rkb = P.sb([128, 256], stack=st)
        P.dma(rkb, bc_rows(S["rwkv_rk"][l:l + 1].re("o h d -> o (h d)"), 256))
        X = [P.sb([128, 1216], stack=st) for _ in range(2)]
        th = P.sb([128, 128], stack=st)
        sg = P.sb([128, 256], stack=st)
        pT = [P.ps([128, 128], stack=st) for _ in range(2)]
        tT = [P.sb([128, 128], stack=st) for _ in range(5)]
        pP = [P.ps([128, 256], stack=st) for _ in range(2)]
        wd = [P.sb([128, 256], stack=st) for _ in range(2)]
        a_ = P.sb([128, 256], stack=st)
        gd = [P.sb([128, 256], stack=st) for _ in range(2)]
        kk = P.sb([128, 256], stack=st)
        sq = P.sb([128, 256], stack=st)
        s4 = P.sb([128, 16], stack=st)
        an = P.sb([128, 256], stack=st)
        b_ = P.sb([128, 256], stack=st)
        kp = P.sb([128, 256], stack=st)
        t1 = P.sb([128, 256], stack=st)
        bon = P.sb([128, 256], stack=st)
        pK = [P.ps([64, 4, 128], stack=st) for _ in range(2)]
        kst = [P.sb([64, 128, 4], stack=st) for _ in range(2)]
        rv = [P.sb([128, 256], stack=st) for _ in range(2)]
        it = 0
        for t in range(NT):
            x = X[t % 2]
            P.dma(x, S["RWP"][t * 128:(t + 1) * 128, :], eng="sp" if t % 2 else "pool")
            r, k, v = x[:, 0:256], x[:, 256:512], x[:, 512:768]
            P.act(th, x[:, 768:896], AF.Tanh)
            P.act(sg, x[:, 960:1216], AF.Sigmoid)
            srcs = [(th[:, 0:64], 64), (th[:, 64:128], 64), (x[:, 896:960], 64), (sg[:, 0:128], 128), (sg[:, 128:256], 128)]
            for i, (sv, m) in enumerate(srcs):
                pt = pT[i % 2]
                P.mm(pt[0:m, :], sv, idf)
                P.copy(tT[i][0:m, :], pt[0:m, :], eng="act")
            for d in range(2):
                pp = pP[d]
                P.mm(pp, tT[d][0:64, :], w2[d])
                P.tt(wd[d], pp, w0[d], ALU.add)
                P.act(wd[d], wd[d], AF.Sigmoid)
                P.act(wd[d], wd[d], AF.Exp, scale=-0.6065306597126334)
            pp = pP[0]
            P.mm(pp, tT[2][0:64, :], a2)
            P.tt(a_, pp, a0, ALU.add)
            P.act(a_, a_, AF.Sigmoid)
            for d in range(2):
                pp = pP[1 - d]
                P.mm(pp, tT[3 + d], g2[d])
                P.copy(gd[d], pp, eng="act")
                P.dma(S["GFB"][d, t * 128:(t + 1) * 128, :], gd[d], eng="sp")
            P.tt(kk, k, kkb, ALU.mult)
            P.tt(sq, kk, kk, ALU.mult, eng="pool")
            P.reduce(s4[:, 0:4], sq.re("p (h d) -> p h d", h=4), ALU.add)
            P.act(s4[:, 0:4], s4[:, 0:4], AF.Sqrt)
            P.ts(s4[:, 0:4], s4[:, 0:4], 1e-12, ALU.max)
            P.recip(s4[:, 4:8], s4[:, 0:4])
            P.ts(s4[:, 4:8], s4[:, 4:8], -1.0, ALU.mult)
            for h in range(4):
                P.ts(an[:, h * 64:(h + 1) * 64], kk[:, h * 64:(h + 1) * 64], s4[:, 4 + h:5 + h], ALU.mult)
            P.stt(b_, an, -1.0, a_, ALU.mult, ALU.mult)
            P.tt(t1, a_, kab, ALU.mult, eng="pool")
            P.tt(t1, t1, omka, ALU.add, eng="pool")
            P.tt(kp, k, t1, ALU.mult, eng="pool")
            P.tt(t1, r, kp, ALU.mult, eng="pool")
            P.tt(t1, t1, rkb, ALU.mult, eng="pool")
            P.reduce(s4[:, 8:12], t1.re("p (h d) -> p h d", h=4), ALU.add)
            for h in range(4):
                P.ts(bon[:, h * 64:(h + 1) * 64], v[:, h * 64:(h + 1) * 64], s4[:, 8 + h:9 + h], ALU.mult)
            P.dma(S["BON"][t * 128:(t + 1) * 128, :], bon, eng="sp")
            sf = t * 128
            sb_ = s0_bwd(t)
            for (src, name, dirs) in ((an, "ANT", (0, 1)), (r, "RT", (0, 1)), (wd[0], "WT", (0,)), (wd[1], "WT", (1,))):
                for d in dirs:
                    pk = pK[it % 2]
                    ks = kst[it % 2]
                    it += 1
                    for h in range(4):
                        P.mm(pk[:, h, :], src[:, h * 64:(h + 1) * 64], idf if d == 0 else jf)
                    P.copy(ks.re("k s h -> k h s"), pk, eng="act")
                    s0 = sf if d == 0 else sb_
                    P.dma(S[name][d, :, s0:s0 + 128, :], ks, eng="sp" if it % 2 else "pool")
            for (src, name) in ((b_, "BT"), (kp, "KT")):
                P.dma(S[name][0, sf:sf + 128, :], src, eng="pool")
            for h in range(4):
                P.dma(S["VBD"][0, h, sf:sf + 128, h * 64:(h + 1) * 64], v[:, h * 64:(h + 1) * 64], eng="sp")
            for i, (src, name) in enumerate(((b_, "BT"), (kp, "KT"), (v, "V"))):
                pp = pP[i % 2]
                P.mm(pp, jf, src)
                rr = rv[i % 2]
                P.copy(rr, pp, eng="act")
                if name == "V":
                    for h in range(4):
                        P.dma(S["VBD"][1, h, sb_:sb_ + 128, h * 64:(h + 1) * 64], rr[:, h * 64:(h + 1) * 64], eng="sp")
                else:
                    P.dma(S[name][1, sb_:sb_ + 128, :], rr, eng="pool")
    P.barrier()


def stage_rwkv_scan(P, S, l, filler=None, st_outer=None):
    CH = RCH
    NCH = NTOK // CH
    with ExitStack() as st:
        LA = [P.sb([128, CH, 40], F32R, stack=st) for _ in range(2)]
        L2 = [P.sb([16, CH, 128], F32R, stack=st) for _ in range(2)]
        R2 = [P.sb([16, CH, 256], F32R, stack=st) for _ in range(2)]
        Wt = [P.sb([128, CH, 4], stack=st) for _ in range(2)]
        OB = 2
        Os = [P.sb([40, OB, 256], stack=st) for _ in range(2)]
        for i in range(2):
            P.memset(LA[i].re("p s c -> p (s c)").bitcast(F32), 0.0)
            P.memset(L2[i].re("p s c -> p (s c)").bitcast(F32), 0.0)
            P.memset(R2[i].re("p s c -> p (s c)").bitcast(F32), 0.0)
        mask8 = P.sb([8, 256], stack=st)
        P.dma(mask8, S["mask8"])
        St = P.sb([128, 256], stack=st)
        P.memset(St, 0.0)
        Sr = P.sb([128, 256], F32R, stack=st)
        P.memset(Sr.bitcast(F32), 0.0)
        pA = [P.ps([40, 256], stack=st) for _ in range(2)]
        pU = [P.ps([128, 256], stack=st) for _ in range(2)]
        S3 = St.re("p (h v) -> p h v", h=4)

        def load_chunk(c):
            i = c % 2
            s0 = c * CH
            if c == NCH:
                for d in range(2):
                    P.dma(LA[i][d * 64:(d + 1) * 64, 0:1, 32 + 4 * d:36 + 4 * d].bitcast(F32), S["RT"][d, :, s0 - 1:s0, :], eng="sp")
                return
            for d in range(2):
                rows = slice(d * 64, (d + 1) * 64)
                P.dma(LA[i][rows, :, 4 * d:4 * d + 4].bitcast(F32), S["ANT"][d, :, s0:s0 + CH, :], eng="sp")
                if c == 0:
                    P.dma(LA[i][rows, 1:CH, 32 + 4 * d:36 + 4 * d].bitcast(F32), S["RT"][d, :, 0:CH - 1, :], eng="act")
                else:
                    P.dma(LA[i][rows, :, 32 + 4 * d:36 + 4 * d].bitcast(F32), S["RT"][d, :, s0 - 1:s0 + CH - 1, :], eng="act")
                P.dma(Wt[i][rows, :, :], S["WT"][d, :, s0:s0 + CH, :], eng="sp")
                P.dma(L2[i][4 * d:4 * d + 4, :, d * 64:(d + 1) * 64].bitcast(F32),
                      S["BT"][d, s0:s0 + CH, :].re("s (h k) -> h s k", h=4), eng="act")
                P.dma(L2[i][8 + 4 * d:12 + 4 * d, :, d * 64:(d + 1) * 64].bitcast(F32),
                      S["KT"][d, s0:s0 + CH, :].re("s (h k) -> h s k", h=4), eng="sp")
                P.dma(R2[i][8 + 4 * d:12 + 4 * d, :, :].bitcast(F32), S["VBD"][d, :, s0:s0 + CH, :], eng="act")

        load_chunk(0)
        for s in range(NTOK + 1):
            c, pos = divmod(s, CH)
            i = c % 2
            if pos == 0 and c + 1 <= NCH:
                load_chunk(c + 1)
            pa = pA[s % 2]
            P.mm(pa, LA[i][:, pos, :], Sr)
            if s >= 1:
                cp, pp_ = divmod(s - 1, OB)
                P.copy(Os[cp % 2][32:40, pp_, :], pa[32:40, :], eng="act")
                if pp_ == OB - 1:
                    P.dma(S["OD"].re("d h s c -> (d h) s c")[:, cp * OB:(cp + 1) * OB, :], Os[cp % 2][32:40, :, :], eng="sp")
            if s == NTOK:
                break
            if filler is not None:
                next(filler, None)
            P.tt(R2[i][0:8, pos, :], pa[0:8, :], mask8, ALU.mult)
            pu = pU[s % 2]
            P.mm(pu, L2[i][:, pos, :], R2[i][:, pos, :])
            P.tt(S3, S3, V(Wt[i].ap[:, pos, :].unsqueeze(2).to_broadcast([128, 4, 64]), Wt[i].key), ALU.mult, eng="pool")
            P.tt(Sr, St, pu, ALU.add)
            P.tt(St, St, pu, ALU.add)
            if filler is not None:
                next(filler, None)
    if filler is None:
        P.barrier()


def stage_rwkv_out(P, S, l):
    with ExitStack() as st:
        jf = P.sb([128, 128], stack=st)
        P.dma(jf, S["antiid"])
        gam = load_bc(P, S, st, "rwkv_lnx_g", l, 256)
        bet = load_bc(P, S, st, "rwkv_lnx_b", l, 256)
        o_ = [[P.sb([128, 256], stack=st) for _ in range(2)] for _ in range(2)]
        gfb = [[P.sb([128, 256], stack=st) for _ in range(2)] for _ in range(2)]
        bon = [P.sb([128, 256], stack=st) for _ in range(2)]
        orev = [P.sb([128, 256], stack=st) for _ in range(2)]
        pj = [P.ps([128, 256], stack=st) for _ in range(2)]
        sq = P.sb([128, 256], stack=st)
        s4 = [P.sb([128, 16], stack=st) for _ in range(2)]
        gn = [P.sb([128, 256], stack=st) for _ in range(2)]
        y = [P.sb([128, 256], stack=st) for _ in range(2)]
        for t in range(NT):
            i = t % 2
            sf, sb_ = t * 128, s0_bwd(t)
            for d in range(2):
                s0 = sf if d == 0 else sb_
                for h in range(4):
                    P.dma(o_[i][d][:, h * 64:(h + 1) * 64], S["OD"][d, h, s0:s0 + 128, h * 64:(h + 1) * 64],
                          eng="sp" if h % 2 else "pool")
                P.dma(gfb[i][d], S["GFB"][d, sf:sf + 128, :], eng="sp")
            P.dma(bon[i], S["BON"][sf:sf + 128, :], eng="pool")
            for d in range(2):
                if d == 1:
                    P.mm(pj[i], jf, o_[i][1])
                    P.copy(orev[i], pj[i], eng="act")
                    od = orev[i]
                else:
                    od = o_[i][0]
                s_ = s4[d]
                o3 = od.re("p (h d) -> p h d", h=4)
                P.reduce(s_[:, 0:4], o3, ALU.add)
                P.tt(sq, od, od, ALU.mult)
                P.reduce(s_[:, 4:8], sq.re("p (h d) -> p h d", h=4), ALU.add)
                P.ts(s_[:, 0:4], s_[:, 0:4], 1.0 / 64, ALU.mult)
                P.tt(s_[:, 8:12], s_[:, 0:4], s_[:, 0:4], ALU.mult)
                P.stt(s_[:, 4:8], s_[:, 4:8], 1.0 / 64, s_[:, 8:12], ALU.mult, ALU.subtract)
                P.ts(s_[:, 4:8], s_[:, 4:8], 64e-5, ALU.add)
                P.act(s_[:, 4:8], s_[:, 4:8], AF.Sqrt)
                P.recip(s_[:, 8:12], s_[:, 4:8])
                g_ = gn[d]
                for h in range(4):
                    P.ts(g_[:, h * 64:(h + 1) * 64], od[:, h * 64:(h + 1) * 64], s_[:, h:h + 1], ALU.subtract,
                         s_[:, 8 + h:9 + h], ALU.mult)
                P.tt(g_, g_, gam, ALU.mult, eng="pool")
                P.tt(g_, g_, bet, ALU.add, eng="pool")
                P.tt(g_, g_, bon[i], ALU.add, eng="pool")
                P.tt(g_, g_, gfb[i][d], ALU.mult, eng="pool")
            P.tt(y[i], gn[0], gn[1], ALU.add, eng="pool")
            P.dma(S["Yrw"][sf:sf + 128, :], y[i], eng="sp")
    P.barrier()
    tok2feat(P, S, S["Yrw"], 2)


def stage_rwkv(P, S, l):
    stage_rwkv_prep(P, S, l)
    stage_rwkv_scan(P, S, l)
    stage_rwkv_out(P, S, l)
SCRATCH = {
    "MODD": ([1, 2 * 6144], "f32"),
    "HYP": ([NTOK, 768], "f32"),
    "RWP": ([NTOK, 1216], "f32"),
    "VSW": ([NTOK, 128], "f32"),
    "VDF": ([NTOK, 256], "f32"),
    "QK": ([14, 64, NTOK], "bf16"),
    "GT": ([4096, NTOK], "f32"),
    "YT": ([4, 256, NTOK], "bf16"),
    "H1": ([NTOK, D], "f32"),
    "H2": ([NTOK, D], "f32"),
    "GFT": ([2816, UTW], "bf16"),
    "Gx": ([2, 256, 2 * SEQ], "bf16"),
    "Gc": ([2, 256, 2 * CTX], "bf16"),
    "Yhy": ([NTOK, 256], "f32"),
    "ANT": ([2, 64, NTOK, 4], "f32"), "RT": ([2, 64, NTOK, 4], "f32"), "WT": ([2, 64, NTOK, 4], "f32"),
    "BT": ([2, NTOK, 256], "f32"), "KT": ([2, NTOK, 256], "f32"), "VBD": ([2, 4, NTOK, 256], "f32"),
    "OD": ([2, 4, NTOK, 256], "f32"), "GFB": ([2, NTOK, 256], "f32"), "BON": ([NTOK, 256], "f32"),
    "Ysw": ([NTOK, 256], "f32"),
    "Yrw": ([NTOK, 256], "f32"),
    "Ydf": ([NTOK, 256], "f32"),
}

WEIGHTS = {
    "ada_w": [2, 1024, 6144], "ada_b": [2, 6144], "w_in": [2, 1024, 7360], "w_swap": [2, 1024, 896],
    "hy_conv_w": [2, 3, 768], "hy_conv_b": [2, 768], "hy_f_w1": [2, 33, 64], "hy_f_b1": [2, 64],
    "hy_f_w2": [2, 64, 64], "hy_f_b2": [2, 64], "hy_f_w3": [2, 64, 1024], "hy_f_freq": [2, 64],
    "hy_bias": [2, 2, 256], "swa_sink": [2, 4], "rwkv_mu": [2, 1216], "rwkv_w0": [2, 2, 256],
    "rwkv_w2": [2, 2, 64, 256], "rwkv_a0": [2, 256], "rwkv_a2": [2, 64, 256], "rwkv_g2": [2, 2, 128, 256],
    "rwkv_kk": [2, 256], "rwkv_ka": [2, 256], "rwkv_rk": [2, 4, 64], "rwkv_lnx_g": [2, 256],
    "rwkv_lnx_b": [2, 256], "diff_lq1": [2, 32], "diff_lk1": [2, 32], "diff_lq2": [2, 32], "diff_lk2": [2, 32],
    "diff_subln_g": [2, 64], "w_branch": [2, 4, 256, 1024], "w_out": [2, 1024, 1024], "ln1_g": [2, 1024],
    "ln1_b": [2, 1024], "ffn_w_up": [2, 1024, 5632], "ffn_conv_w": [2, 3, 5632], "ffn_conv_b": [2, 5632],
    "ffn_w_down": [2, 2816, 1024], "ln2_g": [2, 1024], "ln2_b": [2, 1024],
}
CONSTS = {"ident": [128, 128], "antiid": [128, 128], "rope": [4, 64, NTOK], "maskLR": [2, 128, 512], "mask8": [8, 256],
          "hyE_x": [33, 2 * SEQ], "hyE_c": [33, 2 * CTX], "hyT_x": [1, 2 * SEQ], "hyT_c": [1, 2 * CTX], "hyND": [256, 1]}


def build(nc, dbg=(), stages=None, nlayers=2, ext_in=()):
    P = Prog(nc)
    S = {}
    S["xin"] = P.dram("xin", [NTOK, D], F32, kind="ExternalInput")
    S["cvec"] = P.dram("cvec", [2, D], F32, kind="ExternalInput")
    for k, shp in {**WEIGHTS, **CONSTS}.items():
        S[k] = P.dram(k, shp, F32, kind="ExternalInput")
    for k, (shp, dt) in SCRATCH.items():
        S[k] = P.dram(k, shp, F32 if dt == "f32" else BF16, kind="ExternalOutput" if k in dbg else ("ExternalInput" if k in ext_in else "Internal"))
    S["out"] = P.dram("out", [SEQ, D], F32, kind="ExternalOutput")
    S["identb"] = P.sb([128, 128], BF16)
    with ExitStack() as st0:
        idf = P.sb([128, 128], stack=st0)
        P.dma(idf, S["ident"])
        P.copy(S["identb"], idf)
        P.barrier()
    for l in range(nlayers):
        src = S["xin"] if l == 0 else S["H2"]
        if stages is None or "mod" in stages:
            stage_mod(P, S, l)
        if stages is None or "A" in stages:
            with ExitStack() as st:
                uT = P.sb([128, 8, UTW], BF16, stack=st)
                stage_lnmod(P, S, src, 0, 1024, uT)
                stage_inproj(P, S, l, uT)
            P.barrier()
        lam_init = 0.8 - 0.6 * float(np.exp(-0.3 * l))
        if stages is None or "hyrw" in stages:
            stage_hyena_rwkv(P, S, l)
        if stages is not None and "hyena" in stages:
            stage_hyena(P, S, l)
        if stages is not None and "rwkv" in stages:
            stage_rwkv(P, S, l)
        if stages is None or "swa" in stages:
            stage_swa(P, S, l)
        if stages is None or "diff" in stages:
            stage_diff(P, S, l, lam_init)
        if stages is None or "merge" in stages:
            stage_merge(P, S, l, src)
        if stages is None or "ffn" in stages:
            stage_ffn(P, S, l, (l == nlayers - 1) and ("H2" not in dbg))
    P.finalize()
    return P


def rope_tables():
    pos = np.arange(SEQ)
    row = (pos // 64).astype(np.float32)
    col = (pos % 64).astype(np.float32)
    out = np.zeros((4, 64, NTOK), np.float32)
    out[0, :, :CTX] = 1.0
    out[2, :, :CTX] = 1.0
    for ti, d in ((0, 64), (2, 32)):
        nf = d // 4
        inv = (10000.0 ** (-np.arange(nf, dtype=np.float32) / nf)).astype(np.float32)
        for rep in range(64 // d):
            for half, p in enumerate((row, col)):
                ang = (p[None, :] * inv[:, None]).astype(np.float32)
                c, s = np.cos(ang), np.sin(ang)
                b = rep * d + half * 2 * nf
                out[ti, b:b + nf, CTX:] = c
                out[ti, b + nf:b + 2 * nf, CTX:] = c
                out[ti + 1, b:b + nf, CTX:] = -s
                out[ti + 1, b + nf:b + 2 * nf, CTX:] = s
    return out


def hy_consts(L):
    pos = np.concatenate([np.arange(L - 1, -1, -1), np.arange(L)]).astype(np.float64)
    t = np.linspace(0.0, 1.0, L, dtype=np.float32)[pos.astype(np.int64)]
    w = (2.0 * np.pi * pos / L)
    fbv = np.linspace(1e-4, 15, 16, dtype=np.float32).astype(np.float64)
    ang = w[None, :] * fbv[:, None]
    E = np.concatenate([t[None, :].astype(np.float64), np.cos(ang), -np.sin(ang)], 0).astype(np.float32)
    return np.ascontiguousarray(E), np.ascontiguousarray(t[None, :].astype(np.float32))


def swap_cols():
    idx = []
    for (c0, n, d) in ((SWQ0, 256, 64), (SWK0, 128, 64), (DFQ0, 256, 32), (DFK0, 256, 32)):
        nf = d // 4
        for j in range(n):
            i = j % (2 * nf)
            idx.append(c0 + (j + nf if i < nf else j - nf))
    return np.array(idx)


def make_in_maps(inputs, cores):
    ins = {k: np.ascontiguousarray(np.asarray(v, dtype=np.float32)) for k, v in inputs.items()}
    common = {k: ins[k] for k in WEIGHTS if k != "w_swap"}
    common["w_swap"] = np.ascontiguousarray(ins["w_in"][:, :, swap_cols()])
    common["ident"] = np.eye(128, dtype=np.float32)
    common["antiid"] = np.ascontiguousarray(np.eye(128, dtype=np.float32)[::-1])
    common["rope"] = rope_tables()
    common["hyE_x"], common["hyT_x"] = hy_consts(SEQ)
    common["hyE_c"], common["hyT_c"] = hy_consts(CTX)
    lo, hi = np.log(1e-2) / 1.5, np.log(1e-2) / 0.3
    common["hyND"] = np.ascontiguousarray(-np.abs(np.linspace(lo, hi, 256, dtype=np.float32))[:, None])
    common["mask8"] = np.ascontiguousarray((np.arange(8)[:, None] % 4 == np.arange(256)[None, :] // 64).astype(np.float32))
    rr = np.arange(128)
    mL = (rr[:, None] >= rr[None, :]).astype(np.float32)
    mR = (rr[:, None] <= rr[None, :]).astype(np.float32)
    common["maskLR"] = np.ascontiguousarray(np.stack([np.tile(mL, (1, 4)), np.tile(mR, (1, 4))], 0))
    maps = []
    for b in cores:
        m = dict(common)
        m["xin"] = np.ascontiguousarray(np.concatenate([ins["ctx"][b], ins["x"][b]], 0))
        m["cvec"] = np.ascontiguousarray(np.stack([ins["c"][b], ins["c_ctx"]], 0))
        maps.append(m)
    return maps


def kernel(**inputs):
    nc = bass.Bass("TRN2", target_bir_lowering=False)
    build(nc)
    maps = make_in_maps(inputs, list(range(8)))
    res = run_bass_kernel_spmd(nc, maps, core_ids=list(range(8)))
    return np.stack([r["out"] for r in res.results], 0).astype(np.float32)
```
